# Optimizing a Trainium2 kernel written in Bass

```python
import jax, jax.numpy as jnp
from jax import lax
import numpy as np

D_MODEL = 1024
BATCH = 4
SEQ = 8192
DEPTH = 1
DEC_BATCH = 128
DEC_SEQ = 1
PAST_LEN = 16384
PAGE_SIZE = 128

HEAD_DIM = 64
A_WINDOWS = (128, 512, 2048)
A_DILATIONS = (1, 4, 16)
A_GROUPS = 3
A_HEADS = 8
A_WIDTH = A_HEADS * HEAD_DIM
B_WINDOW = 128
B_HEADS = 8
B_KV_HEADS = 2
B_WIDTH = B_HEADS * HEAD_DIM
B_KV_WIDTH = B_KV_HEADS * HEAD_DIM
BLOCK = 128
NORM_EPS = 1e-6
ALIBI_MAX_EXP = 8.0
IN_SPLITS = (A_WIDTH,) * (3 * A_GROUPS) + (B_WIDTH, B_KV_WIDTH, B_KV_WIDTH, A_WIDTH, B_WIDTH, D_MODEL, D_MODEL)
IN_WIDTH = sum(IN_SPLITS)
SPLIT_POINTS = tuple(int(c) for c in np.cumsum(IN_SPLITS)[:-1])

kernel_name = "dilated_swa_gated_hybrid_step"


def rms_norm(x, gain):
    xf = x.astype(jnp.float32)
    y = xf * lax.rsqrt(jnp.mean(xf * xf, axis=-1, keepdims=True) + NORM_EPS)
    return (y * gain.astype(jnp.float32)).astype(x.dtype)


def alibi_slopes(n_heads, dilation):
    expo = -ALIBI_MAX_EXP * np.arange(1, n_heads + 1, dtype=np.float32) / n_heads
    return (np.exp2(expo) / dilation).astype(np.float32)


def project_inputs(x, norm_gain, w_in, qk_norm_a, qk_norm_b):
    n, t = x.shape[:2]
    h = rms_norm(x, norm_gain)
    parts = jnp.split(jnp.einsum("ntd,de->nte", h, w_in), SPLIT_POINTS, axis=-1)
    heads = lambda a: a.reshape(n, t, -1, HEAD_DIM)
    a_qkv = [(rms_norm(heads(parts[3 * g]), qk_norm_a[g, 0]),
              rms_norm(heads(parts[3 * g + 1]), qk_norm_a[g, 1]),
              heads(parts[3 * g + 2])) for g in range(A_GROUPS)]
    i = 3 * A_GROUPS
    b_qkv = (rms_norm(heads(parts[i]), qk_norm_b[0]),
             rms_norm(heads(parts[i + 1]), qk_norm_b[1]),
             heads(parts[i + 2]))
    gates = tuple(parts[i + 3:])
    return a_qkv, b_qkv, gates


def masked_softmax(s, valid, sink):
    s = jnp.where(valid, s, -jnp.inf)
    lse = jax.nn.logsumexp(s, axis=-1)
    if sink is not None:
        lse = jnp.logaddexp(lse, sink)
    return jnp.exp(s - lse[..., None]), lse


def banded_attention(q, k, v, slopes, sinks, max_delta, dist):
    n, length, n_heads, hd = q.shape
    kh = k.shape[2]
    rep = n_heads // kh
    nb = -(-length // BLOCK)
    pad = nb * BLOCK - length
    blocks = lambda a: jnp.pad(a, ((0, 0), (0, pad), (0, 0), (0, 0))).reshape(n, nb, BLOCK, a.shape[2], hd)
    shift = lambda a: jnp.pad(a, ((0, 0), (1, 0), (0, 0), (0, 0), (0, 0)))[:, :-1]
    qb = blocks(q).reshape(n, nb, BLOCK, kh, rep, hd)
    kb, vb = blocks(k), blocks(v)
    kk = jnp.concatenate([shift(kb), kb], axis=2)
    vv = jnp.concatenate([shift(vb), vb], axis=2)
    s = jnp.einsum("nbqgrd,nbkgd->nbgrqk", qb, kk).astype(jnp.float32) * (hd ** -0.5)
    delta = np.arange(BLOCK)[:, None] + BLOCK - np.arange(2 * BLOCK)[None, :]
    k_abs = np.arange(nb)[:, None] * BLOCK - BLOCK + np.arange(2 * BLOCK)[None, :]
    valid = (delta >= 0)[None] & (delta <= max_delta)[None] & (k_abs[:, None, :] >= 0)
    bias = -jnp.asarray(slopes).reshape(kh, rep)[:, :, None, None] * (delta * dist).astype(np.float32)
    sink = None if sinks is None else sinks.astype(jnp.float32).reshape(kh, rep)[:, :, None]
    p, lse = masked_softmax(s + bias, valid[:, None, None], sink)
    o = jnp.einsum("nbgrqk,nbkgd->nbqgrd", p.astype(v.dtype), vv)
    o = o.reshape(n, nb * BLOCK, n_heads, hd)[:, :length]
    lse = lse.transpose(0, 1, 4, 2, 3).reshape(n, nb * BLOCK, n_heads)[:, :length]
    return o, lse


def dilated_prompt(q, k, v, window, dil):
    n, s_len = q.shape[:2]
    stride = lambda a: a.reshape(n, s_len // dil, dil, *a.shape[2:]).swapaxes(1, 2).reshape(n * dil, s_len // dil, *a.shape[2:])
    unstride = lambda a: a.reshape(n, dil, s_len // dil, *a.shape[2:]).swapaxes(1, 2).reshape(n, s_len, *a.shape[2:])
    o, lse = banded_attention(stride(q), stride(k), stride(v), alibi_slopes(A_HEADS, dil), None, window // dil, dil)
    return unstride(o), unstride(lse)


def decode_attention(q, cache_kv, k_new, v_new, n_keys, dil, slopes, sinks, window):
    n, t_new, n_heads, hd = q.shape
    kh = k_new.shape[2]
    rep = n_heads // kh
    buf_len = cache_kv.shape[1]
    kk = jnp.concatenate([cache_kv[:, :, 0].astype(k_new.dtype), k_new], axis=1)
    vv = jnp.concatenate([cache_kv[:, :, 1].astype(v_new.dtype), v_new], axis=1)
    steps = np.arange(n_keys)
    idx = buf_len + np.arange(t_new)[:, None] - steps[None, :] * dil
    valid = idx >= 0
    idx = np.maximum(idx, 0)
    kg, vg = kk[:, idx], vv[:, idx]
    s = jnp.einsum("ntgrd,ntkgd->ntgrk", q.reshape(n, t_new, kh, rep, hd), kg).astype(jnp.float32) * (hd ** -0.5)
    bias = -jnp.asarray(slopes).reshape(kh, rep)[:, :, None] * (steps * dil).astype(np.float32)
    sink = None if sinks is None else sinks.astype(jnp.float32).reshape(kh, rep)
    p, lse = masked_softmax(s + bias, valid[:, None, None, :], sink)
    o = jnp.einsum("ntgrk,ntkgd->ntgrd", p.astype(vv.dtype), vg).reshape(n, t_new, n_heads, hd)
    keep = min(window, buf_len + t_new)
    new_kv = jnp.stack([kk[:, -keep:], vv[:, -keep:]], axis=2)
    return o, lse.reshape(n, t_new, n_heads), new_kv


def merge_branches(x, o_a_groups, lse_a, o_b, gates, w_branch_a, w_branch_b, w_out):
    n, t = x.shape[:2]
    z_a, z_b, g_a, g_b = gates
    wts = jax.nn.softmax(lse_a, axis=0)
    o_a = jnp.einsum("gnth,gnthd->nthd", wts.astype(o_a_groups.dtype), o_a_groups)
    a = o_a.reshape(n, t, A_WIDTH) * jax.nn.silu(z_a)
    b = o_b.reshape(n, t, B_WIDTH) * jax.nn.silu(z_b)
    mixed = jax.nn.sigmoid(g_a) * jnp.einsum("nte,ed->ntd", a, w_branch_a) \
        + jax.nn.sigmoid(g_b) * jnp.einsum("nte,ed->ntd", b, w_branch_b)
    return x + jnp.einsum("ntd,de->nte", mixed, w_out)


def layer_prompt(x, norm_gain, w_in, qk_norm_a, qk_norm_b, b_sinks, w_branch_a, w_branch_b, w_out):
    a_qkv, (bq, bk, bv), gates = project_inputs(x, norm_gain, w_in, qk_norm_a, qk_norm_b)
    s_len = x.shape[1]
    o_list, lse_list, new_a = [], [], []
    for g in range(A_GROUPS):
        q, k, v = a_qkv[g]
        o, lse = dilated_prompt(q, k, v, A_WINDOWS[g], A_DILATIONS[g])
        o_list.append(o)
        lse_list.append(lse)
        keep = min(A_WINDOWS[g], s_len)
        new_a.append(jnp.stack([k[:, s_len - keep:], v[:, s_len - keep:]], axis=2))
    o_b, _ = banded_attention(bq, bk, bv, alibi_slopes(B_HEADS, 1), b_sinks, B_WINDOW, 1)
    keep = min(B_WINDOW, s_len)
    new_b = jnp.stack([bk[:, s_len - keep:], bv[:, s_len - keep:]], axis=2)
    y = merge_branches(x, jnp.stack(o_list), jnp.stack(lse_list), o_b, gates, w_branch_a, w_branch_b, w_out)
    return y, new_a, new_b


def layer_sample(x, caches_a, cache_b, norm_gain, w_in, qk_norm_a, qk_norm_b, b_sinks, w_branch_a, w_branch_b, w_out):
    a_qkv, (bq, bk, bv), gates = project_inputs(x, norm_gain, w_in, qk_norm_a, qk_norm_b)
    o_list, lse_list, new_a = [], [], []
    for g in range(A_GROUPS):
        q, k, v = a_qkv[g]
        dil = A_DILATIONS[g]
        o, lse, kv = decode_attention(q, caches_a[g], k, v, A_WINDOWS[g] // dil + 1, dil,
                                      alibi_slopes(A_HEADS, dil), None, A_WINDOWS[g])
        o_list.append(o)
        lse_list.append(lse)
        new_a.append(kv)
    o_b, _, new_b = decode_attention(bq, cache_b, bk, bv, B_WINDOW + 1, 1, alibi_slopes(B_HEADS, 1), b_sinks, B_WINDOW)
    y = merge_branches(x, jnp.stack(o_list), jnp.stack(lse_list), o_b, gates, w_branch_a, w_branch_b, w_out)
    return y, new_a, new_b


def setup_inputs(seed: int = 0) -> dict:
    key = jax.random.key(seed)
    ks = jax.random.split(key, 16)
    nrm = lambda k, shape: jax.random.normal(k, shape, dtype=jnp.float32)
    return {
        "x_prompt": nrm(ks[0], (BATCH, SEQ, D_MODEL)),
        "x_sample": nrm(ks[1], (DEC_BATCH, DEC_SEQ, D_MODEL)),
        "cache_a1_kv": nrm(ks[2], (DEPTH, DEC_BATCH, min(A_WINDOWS[0], PAST_LEN), 2, A_HEADS, HEAD_DIM)),
        "cache_a2_kv": nrm(ks[3], (DEPTH, DEC_BATCH, min(A_WINDOWS[1], PAST_LEN), 2, A_HEADS, HEAD_DIM)),
        "cache_a3_kv": nrm(ks[4], (DEPTH, DEC_BATCH, min(A_WINDOWS[2], PAST_LEN), 2, A_HEADS, HEAD_DIM)),
        "cache_b_kv": nrm(ks[5], (DEPTH, DEC_BATCH, min(B_WINDOW, PAST_LEN), 2, B_KV_HEADS, HEAD_DIM)),
        "norm_gain": 1.0 + 0.02 * nrm(ks[6], (DEPTH, D_MODEL)),
        "w_in": nrm(ks[7], (DEPTH, D_MODEL, IN_WIDTH)) * D_MODEL ** -0.5,
        "qk_norm_a": 1.0 + 0.02 * nrm(ks[8], (DEPTH, A_GROUPS, 2, HEAD_DIM)),
        "qk_norm_b": 1.0 + 0.02 * nrm(ks[9], (DEPTH, 2, HEAD_DIM)),
        "b_sinks": nrm(ks[10], (DEPTH, B_HEADS)),
        "w_branch_a": nrm(ks[11], (DEPTH, A_WIDTH, D_MODEL)) * A_WIDTH ** -0.5,
        "w_branch_b": nrm(ks[12], (DEPTH, B_WIDTH, D_MODEL)) * B_WIDTH ** -0.5,
        "w_out": nrm(ks[13], (DEPTH, D_MODEL, D_MODEL)) * D_MODEL ** -0.5,
    }


def reference(x_prompt, x_sample, cache_a1_kv, cache_a2_kv, cache_a3_kv, cache_b_kv, norm_gain, w_in,
              qk_norm_a, qk_norm_b, b_sinks, w_branch_a, w_branch_b, w_out):
    caches_a = (cache_a1_kv, cache_a2_kv, cache_a3_kv)
    y_prompt, y_sample = x_prompt, x_sample
    pa = [[] for _ in range(A_GROUPS)]
    sa = [[] for _ in range(A_GROUPS)]
    pb, sb = [], []
    for layer in range(DEPTH):
        params = (norm_gain[layer], w_in[layer], qk_norm_a[layer], qk_norm_b[layer], b_sinks[layer],
                  w_branch_a[layer], w_branch_b[layer], w_out[layer])
        y_prompt, new_pa, new_pb = layer_prompt(y_prompt, *params)
        y_sample, new_sa, new_sb = layer_sample(y_sample, [c[layer] for c in caches_a], cache_b_kv[layer], *params)
        for g in range(A_GROUPS):
            pa[g].append(new_pa[g])
            sa[g].append(new_sa[g])
        pb.append(new_pb)
        sb.append(new_sb)
    new_a1_prompt, new_a2_prompt, new_a3_prompt = [jnp.stack(r) for r in pa]
    new_a1_sample, new_a2_sample, new_a3_sample = [jnp.stack(r) for r in sa]
    new_b_prompt = jnp.stack(pb)
    new_b_sample = jnp.stack(sb)
    return (y_prompt, y_sample, new_a1_prompt, new_a2_prompt, new_a3_prompt, new_b_prompt,
            new_a1_sample, new_a2_sample, new_a3_sample, new_b_sample)
```

```python
import numpy as np
from contextlib import ExitStack
import concourse.bass as bass
import concourse.mybir as mybir
from concourse.bass_utils import run_bass_kernel_spmd

F32 = mybir.dt.float32
BF16 = mybir.dt.bfloat16
ALU = mybir.AluOpType
AF = mybir.ActivationFunctionType
AX = mybir.AxisListType

PE, ACT, DVE, POOL, SP = "tensor", "scalar", "vector", "gpsimd", "sync"
ENGS = (PE, ACT, DVE, POOL, SP)


class Buf:
    __slots__ = ("name", "w", "r", "dsem", "dcnt", "excl")

    def __init__(self, name, excl=False):
        self.name = name
        self.excl = excl
        self.w = None
        self.r = {}
        self.dsem = None
        self.dcnt = 0


class Ev:
    __slots__ = ("kind", "a", "b")

    def __init__(self, kind, a, b):
        self.kind, self.a, self.b = kind, a, b


class _Rec:
    def __init__(self):
        self.call = None

    def __getattr__(self, name):
        def f(*args, **kwargs):
            self.call = (name, args, kwargs)
            return self
        return f


def _bind(fn):
    r = _Rec()
    fn(r)
    assert r.call is not None
    name, args, kwargs = r.call
    return lambda eng: getattr(eng, name)(*args, **kwargs)


class Sched:
    def __init__(self, nc, stack):
        self.nc = nc
        self.stack = stack
        self.streams = {e: [] for e in ENGS}
        self.esem = {e: stack.enter_context(nc.semaphore("es_" + e)) for e in ENGS}
        self.nsem = len(ENGS)
        self.bulk_sem = stack.enter_context(nc.semaphore("bulk"))
        self.bulk_cnt = 0
        self.final_waits = []
        self.recent_dma = []

    def buf(self, name, excl=False):
        return Buf(name, excl)

    def _deps(self, eng, reads, writes):
        deps = []
        for b in reads:
            if b.w is not None:
                deps.append(b.w)
            if b.excl:
                deps.extend(v for k, v in b.r.items() if k != eng)
        for b in writes:
            if b.w is not None:
                deps.append(b.w)
            deps.extend(b.r.values())
        out = []
        seen = {}
        for d in deps:
            if d.kind == "e":
                if d.a == PE and eng == PE:
                    continue
                k = ("e", d.a)
                if k not in seen or seen[k].b < d.b:
                    seen[k] = d
            else:
                k = ("d", id(d.a))
                if k not in seen or seen[k].b < d.b:
                    seen[k] = d
        return list(seen.values())

    def op(self, eng, fn, reads=(), writes=()):
        deps = self._deps(eng, reads, writes)
        idx = len(self.streams[eng])
        fn = _bind(fn)
        rec = {"fn": fn, "deps": deps, "needed": False, "dma": None}
        self.streams[eng].append(rec)
        for d in deps:
            if d.kind == "e":
                self.streams[d.a][d.b]["needed"] = True
        ev = Ev("e", eng, idx)
        for b in reads:
            b.r[eng] = ev
        for b in writes:
            b.w = ev
            b.r = {}
        return ev

    def dma(self, q, fn, reads=(), writes=(), semowner=None):
        deps = self._deps(q, reads, writes)
        for d in deps:
            if d.kind == "e":
                self.streams[d.a][d.b]["needed"] = True
        if semowner is None:
            sem = self.bulk_sem
            self.bulk_cnt += 16
            val = self.bulk_cnt
        else:
            if semowner.dsem is None:
                semowner.dsem = self.stack.enter_context(self.nc.semaphore("ds_" + semowner.name))
                self.nsem += 1
            semowner.dcnt += 16
            sem, val = semowner.dsem, semowner.dcnt
        fn = _bind(fn)
        rec = {"fn": fn, "deps": deps, "needed": False, "dma": (sem, 16)}
        self.streams[q].append(rec)
        ev = Ev("d", sem, val)
        if semowner is not None:
            self.recent_dma.append(ev)
        for b in reads:
            b.r[("d", id(sem))] = ev
        for b in writes:
            b.w = ev
            b.r = {}
        return ev

    def wait_all(self, eng, evs):
        best = {}
        for d in evs:
            k = ("e", d.a) if d.kind == "e" else ("d", id(d.a))
            if k not in best or best[k].b < d.b:
                best[k] = d
        evs = list(best.values())
        rec = {"fn": None, "deps": list(evs), "needed": False, "dma": None}
        for d in evs:
            if d.kind == "e":
                self.streams[d.a][d.b]["needed"] = True
        self.streams[eng].append(rec)

    def emit(self):
        nc = self.nc
        val = {}
        for e in ENGS:
            c = 0
            for i, rec in enumerate(self.streams[e]):
                if rec["dma"] is None and rec["fn"] is not None and rec["needed"]:
                    c += 1
                    val[(e, i)] = c
        self.maxval = {e: max([v for (ee, _), v in val.items() if ee == e], default=0) for e in ENGS}
        with nc.Block() as block:
            def run(eng_name):
                def body(eng):
                    waited = {}
                    for i, rec in enumerate(self.streams[eng_name]):
                        for d in rec["deps"]:
                            if d.kind == "e":
                                sem, v = self.esem[d.a], val[(d.a, d.b)]
                            else:
                                sem, v = d.a, d.b
                            k = id(sem)
                            if waited.get(k, 0) >= v:
                                continue
                            waited[k] = v
                            eng.wait_ge(sem, v)
                        if rec["fn"] is None:
                            continue
                        ins = rec["fn"](eng)
                        if rec["dma"] is not None:
                            ins.then_inc(rec["dma"][0], 16)
                        elif rec["needed"]:
                            ins.then_inc(self.esem[eng_name], 1)
                return body
            block.tensor(run(PE))
            block.scalar(run(ACT))
            block.vector(run(DVE))
            block.gpsimd(run(POOL))
            block.sync(run(SP))

D = 1024
SBT = 2048
NCH = 66
EPS = 1e-6


def CH_Q(g, c): return 12 * (g - 1) + c
def CH_K(g, c): return 12 * (g - 1) + 4 + c
def CH_V(g, c): return 12 * (g - 1) + 8 + c
def CH_BQ(c): return 36 + c
CH_BK, CH_BV = 40, 41
def CH_ZA(c): return 42 + c
def CH_ZB(c): return 46 + c
def CH_GA(j): return 50 + j
def CH_GB(j): return 58 + j


class Rot:
    def __init__(self, items):
        self.items, self.i = items, 0

    def next(self):
        it = self.items[self.i % len(self.items)]
        self.i += 1
        return it


def build_program(NSB, NS, with_sample=True, with_copy=True):
    nc = bass.Bass("TRN2", target_bir_lowering=False)
    T = NSB * SBT

    def din(name, shape):
        return nc.dram_tensor(name, list(shape), F32, kind="ExternalInput").ap()

    def dout(name, shape):
        return nc.dram_tensor(name, list(shape), F32, kind="ExternalOutput").ap()

    x_d = din("x", [T, D]); xh_d = din("xh", [SBT, D]); flag_d = din("flag", [128, 1])
    xs_d = din("xs", [NS, D])
    win_d = din("win", [NCH, 128, 8 * 128])
    gain_d = din("gain", [128, 8])
    wba_d = din("wba", [128, 4 * 1024]); wbb_d = din("wbb", [128, 4 * 1024]); wout_d = din("wout", [128, 8 * 1024])
    qkg_d = din("qkg", [128, 8]); sk_d = din("sk", [128, 4])
    ident_d = din("ident", [128, 128]); bones_d = din("bones", [128, 128]); swap_d = din("swapm", [128, 128])
    masks_d = din("masks", [128, 8 * 256])
    dbias_d = din("dbias", [128, 8])
    ca_d = [din("ca1", [NS, 128, 1024]), din("ca2", [NS, 512, 1024]), din("ca3", [NS, 2048, 1024])]
    cb_d = din("cb", [NS, 128, 256])
    y_d = dout("y", [T, D]); ys_d = dout("ys", [NS, D])
    pa_d = [dout("pa1", [128, 1024]), dout("pa2", [512, 1024]), dout("pa3", [2048, 1024])]
    pb_d = dout("pb", [128, 256])
    sa_d = [dout("sa1", [NS, 128, 1024]), dout("sa2", [NS, 512, 1024]), dout("sa3", [NS, 2048, 1024])]
    sb_d = dout("sb", [NS, 128, 256])

    st = ExitStack()
    S = Sched(nc, st)
    out_evs = []

    def sb(name, shape, dt=F32):
        return nc.alloc_sbuf_tensor(name, list(shape), dt)

    xT = sb("xT", [128, 8, SBT], BF16); b_xT = S.buf("xT")
    WST = [sb("wst%d" % i, [128, 8, 128], F32) for i in range(2)]; b_WST = [S.buf("wst%d" % i) for i in range(2)]
    WBF = [sb("wbf%d" % i, [128, 8, 128], BF16) for i in range(4)]; b_WBF = [S.buf("wbf%d" % i) for i in range(4)]
    K3 = [sb("k3_%d" % i, [128, SBT], BF16) for i in range(5)]; b_K3 = [S.buf("k3_%d" % i) for i in range(5)]
    V3 = [sb("v3_%d" % i, [128, 16, 192], BF16) for i in range(5)]; b_V3 = [S.buf("v3_%d" % i) for i in range(5)]
    K2t = [sb("k2t%d" % i, [128, 512], BF16) for i in range(4)]; b_K2t = [S.buf("k2t%d" % i) for i in range(4)]
    V2t = [sb("v2t%d" % i, [128, 4, 192], BF16) for i in range(4)]; b_V2t = [S.buf("v2t%d" % i) for i in range(4)]
    K1t = [sb("k1t%d" % i, [128, 128], BF16) for i in range(4)]; b_K1t = [S.buf("k1t%d" % i) for i in range(4)]
    V1t = [sb("v1t%d" % i, [128, 1, 192], BF16) for i in range(4)]; b_V1t = [S.buf("v1t%d" % i) for i in range(4)]
    KBt = [sb("kbt%d" % i, [128, 128], BF16) for i in range(2)]; b_KBt = [S.buf("kbt%d" % i) for i in range(2)]
    VBt = [sb("vbt%d" % i, [128, 1, 192], BF16) for i in range(2)]; b_VBt = [S.buf("vbt%d" % i) for i in range(2)]
    aT = sb("aT", [128, 4, SBT], BF16); b_aT = [S.buf("aT%d" % i) for i in range(4)]
    bT = sb("bT", [128, 4, SBT], BF16); b_bT = [S.buf("bT%d" % i) for i in range(4)]
    MASK = sb("mask", [128, 8, 256], BF16); b_const = S.buf("const")
    identF = sb("identF", [128, 128], F32); identB = sb("identB", [128, 128], BF16)
    bonesB = sb("bonesB", [128, 128], BF16); swapF = sb("swapF", [128, 128], F32)
    gainS = sb("gainS", [128, 8], F32); qkgS = sb("qkgS", [128, 8], F32); skS = sb("skS", [128, 4], F32)
    flagS = sb("flagS", [128, 1], F32); epsS = sb("epsS", [128, 1], F32); oneS = sb("oneS", [128, 1], F32)
    onesB = sb("onesB", [128, 64], BF16)
    dbiasS = sb("dbiasS", [128, 8], F32)
    stat = sb("stat", [128, 8], F32); b_stat = S.buf("stat")

    ARENA_BYTES = 52 * 1024
    arena = sb("arena", [128, ARENA_BYTES // 2], BF16)
    a_off = [0]

    def carve(shape, dt):
        n = int(np.prod(shape[1:]))
        nb = n * (4 if dt == F32 else 2)
        nb_al = (nb + 63) // 64 * 64
        o = a_off[0]
        assert o + nb_al <= ARENA_BYTES, (o, nb_al)
        a_off[0] += nb_al
        ap = arena[0:shape[0], o // 2:(o + nb) // 2]
        if dt == F32:
            ap = ap.bitcast(F32)
        if len(shape) == 3:
            ap = ap.rearrange("p (a b) -> p a b", b=shape[2])
        return ap

    a_off[0] = 0
    XST = [carve([128, 1024], F32) for _ in range(2)]; b_XST = [S.buf("xst%d" % i) for i in range(2)]
    XB = [carve([128, 1024], BF16) for _ in range(2)]; b_XB = [S.buf("xb%d" % i) for i in range(2)]
    cst = carve([128, 8 * 256], F32)
    a_off[0] = 0
    K2c = carve([128, 512 + SBT], BF16); b_K2c = S.buf("k2c")
    V2c = carve([128, 20, 192], BF16); b_V2c = S.buf("v2c")
    K1c = carve([128, 128 + SBT], BF16); b_K1c = S.buf("k1c")
    V1c = carve([128, 17, 192], BF16); b_V1c = S.buf("v1c")
    QT = [carve([128, 512], BF16) for _ in range(3)]; b_QT = [S.buf("qt%d" % i) for i in range(3)]
    EXPT = [carve([128, 512], F32) for _ in range(2)]; b_EXPT = [S.buf("expt%d" % i) for i in range(2)]
    PT = [carve([128, 512], BF16) for _ in range(2)]; b_PT = [S.buf("pt%d" % i) for i in range(2)]
    SQ = [carve([128, 512], BF16) for _ in range(2)]; b_SQ = [S.buf("sq%d" % i) for i in range(2)]
    NL = [carve([128, 512], F32) for _ in range(2)]; b_NL = [S.buf("nl%d" % i) for i in range(2)]
    SZ = carve([128, SBT], BF16); b_SZ = S.buf("sz")
    RW = carve([128, 512], F32); b_RW = S.buf("rw")
    AUN = carve([128, 512], F32); b_AUN = S.buf("aun")
    KF = carve([128, 512], F32); b_KF = S.buf("kf")
    KO = carve([128, 512], F32); b_KO = S.buf("ko")
    VO = KO; b_VO = b_KO
    att_end = a_off[0]
    a_off[0] = 0
    WBA = carve([128, 4, 1024], BF16); WBB = carve([128, 4, 1024], BF16); WOUT = carve([128, 8, 512], BF16)
    b_WBA = S.buf("wba"); b_WBB = S.buf("wbb"); b_WOUT = S.buf("wout")
    MIX = carve([128, 8, 512], BF16); b_MIX = [S.buf("mix%d" % i) for i in range(8)]
    SG = [carve([128, 512], F32) for _ in range(2)]; b_SG = [S.buf("sg%d" % i) for i in range(2)]
    T1 = [carve([128, 512], F32) for _ in range(2)]; b_T1 = [S.buf("t1%d" % i) for i in range(2)]
    XR = [carve([128, 512], F32) for _ in range(2)]; b_XR = [S.buf("xr%d" % i) for i in range(2)]
    WF = [carve([128, 1024], F32) for _ in range(2)]; b_WF = [S.buf("wf%d" % i) for i in range(2)]

    print("sbuf remaining", nc.sbuf_bytes_remaining, "att_end", att_end, "fin_end", a_off[0])
    PS = [nc.alloc_psum_tensor("ps%d" % i, [128, 512], F32) for i in range(8)]
    b_PS = [S.buf("ps%d" % i, excl=True) for i in range(8)]
    rPJ = Rot([0, 1]); rNB = Rot([2, 3]); rST = Rot([4, 5])
    UA, UB = 6, 7
    rWST = Rot([0, 1]); rWBF = Rot([0, 1, 2, 3])
    rEXPT = Rot([0, 1]); rPT = Rot([0, 1]); rSQ = Rot([0, 1]); rNL = Rot([0, 1])

    def fence():
        evs = []
        for e in (PE, ACT, DVE, POOL):
            n = len(S.streams[e])
            for i in range(n - 1, -1, -1):
                r = S.streams[e][i]
                if r["fn"] is not None and r["dma"] is None:
                    evs.append(Ev("e", e, i))
                    break
        evs = evs + list(S.recent_dma)
        S.recent_dma = []
        for e in (PE, ACT, DVE, POOL, SP):
            S.wait_all(e, evs)
    fence.dma_evs = []

    def load_const(dst, src, cols, conv=None):
        ev = S.dma(SP, lambda e: e.dma_start(out=cst[:, 0:cols], in_=src), writes=[b_const], semowner=b_const)
        if conv == "bf":
            S.op(DVE, lambda e: e.tensor_copy(out=dst, in_=cst[:, 0:cols]), reads=[b_const], writes=[b_const])
        else:
            S.op(DVE, lambda e: e.tensor_copy(out=dst, in_=cst[:, 0:cols]), reads=[b_const], writes=[b_const])

    load_const(identF[:, :], ident_d, 128); load_const(identB[:, :], ident_d, 128)
    load_const(bonesB[:, :], bones_d, 128); load_const(swapF[:, :], swap_d, 128)
    load_const(MASK[:, :, :].rearrange("p a b -> p (a b)"), masks_d, 2048)
    load_const(gainS[:, :], gain_d, 8); load_const(qkgS[:, :], qkg_d, 8); load_const(skS[:, :], sk_d, 4)
    load_const(flagS[:, :], flag_d, 1); load_const(dbiasS[:, :], dbias_d, 8)
    S.op(DVE, lambda e: e.memset(epsS[:, :], EPS), writes=[b_const])
    S.op(DVE, lambda e: e.memset(oneS[:, :], 1.0), writes=[b_const])
    S.op(DVE, lambda e: e.memset(onesB[:, :], 1.0), writes=[b_const])
    S.op(ACT, lambda e: e.activation(out=skS[:, :], in_=skS[:, :], func=AF.Exp), reads=[b_const], writes=[b_const])
    C = [b_const]

    bulk = []
    if with_copy:
        def add_copy(src, dst):
            bulk.append((src, dst))
        for n in range(NS):
            for g, L in ((2, 2048), (1, 512), (0, 128)):
                r = 1
                while r < L:
                    nr = min(256, L - r)
                    for (a, bnd) in ((16, None),):
                        pass
                    o = 16 if nr % 16 == 0 else (15 if nr % 15 == 0 else 1)
                    sv = ca_d[g][n, r:r + nr, :].rearrange("(o i) c -> o (i c)", o=o)
                    dv = sa_d[g][n, r - 1:r - 1 + nr, :].rearrange("(o i) c -> o (i c)", o=o)
                    add_copy(sv, dv)
                    r += nr
        sv = cb_d[:, 1:128, :].rearrange("n r c -> n (r c)")
        dv = sb_d[:, 0:127, :].rearrange("n r c -> n (r c)")
        add_copy(sv, dv)

    def issue_bulk(k=1):
        for _ in range(k):
            if bulk:
                sv, dv = bulk.pop(0)
                S.dma(ACT, lambda e, sv=sv, dv=dv: e.dma_start(out=dv, in_=sv))

    def load_w(ch, dup=None, src=None, scale=1.0):
        issue_bulk(1)
        si = rWST.next(); bi = rWBF.next()
        S.dma(SP, lambda e: e.dma_start(out=WST[si][:, :, :].rearrange("p a b -> p (a b)"), in_=win_d[ch]),
              writes=[b_WST[si]], semowner=b_WST[si])
        g_bc = gainS[:, :].unsqueeze(2).to_broadcast([128, 8, 128])
        if dup is None:
            S.op(POOL, lambda e: e.tensor_tensor(out=WBF[bi][:, :, :], in0=WST[si][:, :, :], in1=g_bc, op=ALU.mult),
                 reads=[b_WST[si]] + C, writes=[b_WBF[bi]])
        else:
            g64 = gainS[:, :].unsqueeze(2).to_broadcast([128, 8, 64])
            for half in (0, 1):
                S.op(POOL, lambda e, half=half: e.tensor_tensor(out=WBF[bi][:, :, 64 * half:64 * half + 64],
                                                               in0=WST[si][:, :, 64 * dup:64 * dup + 64], in1=g64, op=ALU.mult),
                     reads=[b_WST[si]] + C, writes=[b_WBF[bi]])
        return WBF[bi], b_WBF[bi]

    def proj_fm(w, bw, bank, ncols, rhs_fn):
        for k in range(8):
            S.op(PE, lambda e, k=k: e.matmul(PS[bank][:, 0:ncols], lhsT=w[:, k, :], rhs=rhs_fn(k), start=(k == 0), stop=(k == 7)),
                 reads=[bw, b_xT], writes=[b_PS[bank]])

    def norm_evac(bank, ncols, gcol, out_ap, out_bufs, f32_out=None):
        si = rSQ.next(); ni = rNL.next(); nb = rNB.next()
        S.op(ACT, lambda e: e.activation(out=SQ[si][:, 0:ncols], in_=PS[bank][:, 0:ncols], func=AF.Square),
             reads=[b_PS[bank]], writes=[b_SQ[si]])
        S.op(PE, lambda e: e.matmul(PS[nb][:, 0:ncols], lhsT=bonesB[:, :], rhs=SQ[si][:, 0:ncols], start=True, stop=True),
             reads=[b_SQ[si]] + C, writes=[b_PS[nb]])
        S.op(ACT, lambda e: e.activation(out=NL[ni][:, 0:ncols], in_=PS[nb][:, 0:ncols], func=AF.Ln, bias=epsS[:, 0:1], scale=1.0),
             reads=[b_PS[nb]] + C, writes=[b_NL[ni]])
        S.op(ACT, lambda e: e.activation(out=NL[ni][:, 0:ncols], in_=NL[ni][:, 0:ncols], func=AF.Exp, scale=-0.5),
             reads=[b_NL[ni]], writes=[b_NL[ni]])
        S.op(DVE, lambda e: e.scalar_tensor_tensor(out=out_ap, in0=PS[bank][:, 0:ncols], scalar=qkgS[:, gcol:gcol + 1],
                                                   in1=NL[ni][:, 0:ncols], op0=ALU.mult, op1=ALU.mult),
             reads=[b_PS[bank], b_NL[ni]] + C, writes=out_bufs)
        if f32_out is not None:
            fo, fb = f32_out
            S.op(DVE, lambda e: e.scalar_tensor_tensor(out=fo, in0=PS[bank][:, 0:ncols], scalar=qkgS[:, gcol:gcol + 1],
                                                       in1=NL[ni][:, 0:ncols], op0=ALU.mult, op1=ALU.mult),
                 reads=[b_PS[bank], b_NL[ni]] + C, writes=fb)

    def xT_win(w):
        return lambda k: xT[:, k, 512 * w:512 * w + 512]

    def tok_ap(k, off, step, n=128):
        return xT[:, k, off:off + step * (n - 1) + 1:step]

    def proj_v_blocks(w, bw, blocks, dst, dbuf, scale_flag=False, vout=None, vcols=128):
        for i0 in range(0, len(blocks), 4):
            grp = blocks[i0:i0 + 4]
            bank = rPJ.next()
            for bi, (off, step, di) in enumerate(grp):
                for k in range(8):
                    S.op(PE, lambda e, k=k, bi=bi, off=off, step=step: e.matmul(
                        PS[bank][:, 128 * bi:128 * bi + 128], lhsT=tok_ap(k, off, step), rhs=w[:, k, :],
                        start=(k == 0), stop=(k == 7)), reads=[bw, b_xT], writes=[b_PS[bank]])
            for bi, (off, step, di) in enumerate(grp):
                src = PS[bank][:, 128 * bi:128 * bi + 128].rearrange("p (h d) -> p h d", d=64)
                dv = dst[:, di, :].rearrange("p (h d) -> p h d", d=64)
                dsel = dst[:, di, :].rearrange("p (h d) -> p h d", d=64)
                for h in (0, 1):
                    if scale_flag:
                        S.op(DVE, lambda e, h=h, src=src, dsel=dsel: e.tensor_scalar(
                            out=dsel[:, 2 * h, :], in0=src[:, h, :], scalar1=flagS[:, 0:1], scalar2=None, op0=ALU.mult),
                            reads=[b_PS[bank]] + C, writes=[dbuf])
                    else:
                        S.op(DVE, lambda e, h=h, src=src, dsel=dsel: e.tensor_copy(out=dsel[:, 2 * h, :], in_=src[:, h, :]),
                             reads=[b_PS[bank]], writes=[dbuf])
                if scale_flag:
                    S.op(DVE, lambda e, dsel=dsel: e.tensor_copy(out=dsel[:, 1, :], in_=flagS[:, 0:1].to_broadcast([128, 64])),
                         reads=C, writes=[dbuf])
                else:
                    S.op(POOL, lambda e, dsel=dsel: e.tensor_copy(out=dsel[:, 1, :], in_=onesB[:, :]), reads=C, writes=[dbuf])
            if vout is not None:
                S.op(ACT, lambda e: e.activation(out=VO[:, 0:128 * len(grp)], in_=PS[bank][:, 0:128 * len(grp)], func=AF.Copy),
                     reads=[b_PS[bank]], writes=[b_VO])
                for bi in range(len(grp)):
                    dap = vout(i0 + bi)
                    if dap is None:
                        continue
                    ev = S.dma(SP, lambda e, bi=bi, dap=dap: e.dma_start(out=dap, in_=VO[:, 128 * bi:128 * bi + vcols]),
                               reads=[b_VO], semowner=b_VO)
                    out_evs.append(ev)

    def prologue(src_d, t0):
        rX = Rot([0, 1])
        for blk in range(16):
            i = rX.next()
            S.dma(SP, lambda e, blk=blk: e.dma_start(out=XST[i], in_=src_d[t0 + 128 * blk:t0 + 128 * blk + 128, :]),
                  writes=[b_XST[i]], semowner=b_XST[i])
            S.op(ACT, lambda e: e.activation(out=XB[i], in_=XST[i], func=AF.Square, accum_out=stat[:, 0:1]),
                 reads=[b_XST[i]], writes=[b_XB[i], b_stat])
            S.op(ACT, lambda e: e.activation(out=stat[:, 1:2], in_=stat[:, 0:1], func=AF.Ln, bias=epsS[:, 0:1], scale=1.0 / D),
                 reads=[b_stat] + C, writes=[b_stat])
            S.op(ACT, lambda e: e.activation(out=stat[:, 2:3], in_=stat[:, 1:2], func=AF.Exp, scale=-0.5),
                 reads=[b_stat], writes=[b_stat])
            S.op(DVE, lambda e: e.tensor_scalar(out=XB[i], in0=XST[i], scalar1=stat[:, 2:3], scalar2=None, op0=ALU.mult),
                 reads=[b_XST[i], b_stat], writes=[b_XB[i]])
            bank = rNB.next()
            pbf = PS[bank][:, :].bitcast(BF16)
            for k in range(8):
                S.op(PE, lambda e, k=k: e.transpose(pbf[:, 128 * k:128 * k + 128], XB[i][:, 128 * k:128 * k + 128], identB[:, :]),
                     reads=[b_XB[i]] + C, writes=[b_PS[bank]])
            S.op(DVE, lambda e, blk=blk: e.tensor_copy(out=xT[:, :, 128 * blk:128 * blk + 128],
                                                       in_=pbf.rearrange("p (k t) -> p k t", t=128)),
                 reads=[b_PS[bank]], writes=[b_xT])

    def attn_window(w, hg, groups, started):
        for tile in groups:
            for head in (0, 1):
                rows = slice(64 * head, 64 * head + 64)
                stb = rST.next()
                nseg = len(tile["segs"]); n = tile["n"]
                for si, sg in enumerate(tile["segs"]):
                    S.op(PE, lambda e, si=si, sg=sg, rows=rows: e.matmul(
                        PS[stb][:, n * si:n * si + n], lhsT=sg["k"][rows, :], rhs=sg["q"][rows, :], start=True, stop=True),
                        reads=[sg["kb"], sg["qb"]], writes=[b_PS[stb]])
                ei = rEXPT.next(); pi = rPT.next()
                tot = n * nseg
                S.op(ACT, lambda e, tot=tot: e.activation(out=EXPT[ei][:, 0:tot], in_=PS[stb][:, 0:tot], func=AF.Exp, scale=0.125),
                     reads=[b_PS[stb]], writes=[b_EXPT[ei]])
                m = tile["mask"](hg[head])
                nq = nseg // 2
                S.op(DVE, lambda e, tot=tot, m=m, nq=nq, n=n: e.tensor_tensor(
                    out=PT[pi][:, 0:tot].rearrange("p (a b c) -> p a b c", b=2, c=n),
                    in0=EXPT[ei][:, 0:tot].rearrange("p (a b c) -> p a b c", b=2, c=n),
                    in1=m.unsqueeze(1).to_broadcast([128, nq, 2, n]), op=ALU.mult),
                    reads=[b_EXPT[ei]] + C, writes=[b_PT[pi]])
                ub = UA if head == 0 else UB
                for si, sg in enumerate(tile["segs"]):
                    vb = sg["v"]
                    lhs = vb[:, 0:128] if head == 0 else vb[:, 64:192]
                    first = not started[head]
                    started[head] = True
                    S.op(PE, lambda e, si=si, sg=sg, lhs=lhs, first=first, ub=ub: e.matmul(
                        sg["o"](PS[ub]), lhsT=lhs, rhs=PT[pi][:, n * si:n * si + n], start=first, stop=False,
                        skip_group_check=True), reads=[sg["vb"], b_PT[pi]], writes=[b_PS[ub]])

    def mask_full(h):
        return MASK[:, h, :].rearrange("p (b c) -> p b c", c=128)

    def mask_win(w):
        return lambda h: MASK[:, h, :].rearrange("p (b c) -> p b c", c=128)[:, :, 32 * w:32 * w + 32]

    def segs_g1(w, Kc, bK, Vc, bV, q, bq, qoff=0):
        tiles = []
        for half in (0, 1):
            segs = []
            for i in (0, 1):
                qb = 4 * w + 2 * half + i
                qap = q[:, 128 * qb - qoff:128 * qb - qoff + 128]
                oc = 128 * (2 * half + i)
                o = (lambda oc: (lambda ps: ps[:, oc:oc + 128]))(oc)
                segs.append(dict(k=Kc[:, 128 + 128 * qb:256 + 128 * qb], kb=bK, q=qap, qb=bq, v=Vc[:, qb + 1, :], vb=bV, o=o))
                segs.append(dict(k=Kc[:, 128 * qb:128 * qb + 128], kb=bK, q=qap, qb=bq, v=Vc[:, qb, :], vb=bV, o=o))
            tiles.append(dict(segs=segs, n=128, mask=mask_full))
        return tiles

    def strided(ap2d, off, step, n):
        return ap2d[:, off:off + step * (n - 1) + 1:step]

    def segs_g2(w, Kc, bK, Vc, bV, q, bq, qoff=0):
        tiles = []
        for half in (0, 1):
            segs = []
            for i in (0, 1):
                r = 2 * half + i
                qap = strided(q, 512 * w + r - qoff, 4, 128)
                o = (lambda r: (lambda ps: strided(ps, r, 4, 128)))(r)
                segs.append(dict(k=strided(Kc, 512 + 512 * w + r, 4, 128), kb=bK, q=qap, qb=bq, v=Vc[:, 4 + 4 * w + r, :], vb=bV, o=o))
                segs.append(dict(k=strided(Kc, 512 * w + r, 4, 128), kb=bK, q=qap, qb=bq, v=Vc[:, 4 * w + r, :], vb=bV, o=o))
            tiles.append(dict(segs=segs, n=128, mask=mask_full))
        return tiles

    def segs_g3(w, Kcur, bKc, Kprev, bKp, Vcur, bVc, Vprev, bVp, q, bq, qoff=0):
        tiles = []
        for half in (0, 1):
            segs = []
            for i in range(8):
                r = 8 * half + i
                qap = strided(q, 512 * w + r - qoff, 16, 32)
                o = (lambda r: (lambda ps: strided(ps, r, 16, 32)))(r)
                segs.append(dict(k=strided(Kcur, r, 16, 128), kb=bKc, q=qap, qb=bq, v=Vcur[:, r, :], vb=bVc, o=o))
                segs.append(dict(k=strided(Kprev, r, 16, 128), kb=bKp, q=qap, qb=bq, v=Vprev[:, r, :], vb=bVp, o=o))
            tiles.append(dict(segs=segs, n=32, mask=mask_win(w)))
        return tiles

    def finish_window(w, c, dstT, b_dst, sink_col=None):
        if sink_col is None:
            S.op(DVE, lambda e: e.reciprocal(out=RW[0:64, :], in_=PS[UB][0:64, :]), reads=[b_PS[UB]], writes=[b_RW])
            S.op(DVE, lambda e: e.reciprocal(out=RW[64:128, :], in_=PS[UA][64:128, :]), reads=[b_PS[UA]], writes=[b_RW])
        else:
            S.op(DVE, lambda e: e.tensor_copy(out=RW[0:64, :], in_=PS[UB][0:64, :]), reads=[b_PS[UB]], writes=[b_RW])
            S.op(DVE, lambda e: e.tensor_copy(out=RW[64:128, :], in_=PS[UA][64:128, :]), reads=[b_PS[UA]], writes=[b_RW])
        nb = rNB.next()
        S.op(PE, lambda e: e.matmul(PS[nb][:, :], lhsT=swapF[:, :], rhs=RW[:, :], start=True, stop=True),
             reads=[b_RW] + C, writes=[b_PS[nb]])
        S.op(DVE, lambda e: e.tensor_tensor(out=AUN[0:64, :], in0=PS[UA][0:64, :], in1=SZ[0:64, 512 * w:512 * w + 512], op=ALU.mult),
             reads=[b_PS[UA], b_SZ], writes=[b_AUN])
        S.op(DVE, lambda e: e.tensor_tensor(out=AUN[64:128, :], in0=PS[UB][64:128, :], in1=SZ[64:128, 512 * w:512 * w + 512], op=ALU.mult),
             reads=[b_PS[UB], b_SZ], writes=[b_AUN])
        if sink_col is None:
            S.op(DVE, lambda e: e.tensor_tensor(out=dstT[:, c, 512 * w:512 * w + 512], in0=AUN[:, :], in1=PS[nb][:, :], op=ALU.mult),
                 reads=[b_AUN, b_PS[nb]], writes=[b_dst[c]])
        else:
            S.op(DVE, lambda e: e.tensor_scalar(out=RW[:, :], in0=PS[nb][:, :], scalar1=skS[:, sink_col:sink_col + 1], scalar2=None, op0=ALU.add),
                 reads=[b_PS[nb]] + C, writes=[b_RW])
            S.op(DVE, lambda e: e.reciprocal(out=RW[:, :], in_=RW[:, :]), reads=[b_RW], writes=[b_RW])
            S.op(DVE, lambda e: e.tensor_tensor(out=dstT[:, c, 512 * w:512 * w + 512], in0=AUN[:, :], in1=RW[:, :], op=ALU.mult),
                 reads=[b_AUN, b_RW], writes=[b_dst[c]])

    def silu_chunk(ch):
        wz, bwz = load_w(ch)
        for w in range(4):
            bank = rPJ.next()
            proj_fm(wz, bwz, bank, 512, xT_win(w))
            ni = rNL.next()
            S.op(ACT, lambda e: e.activation(out=NL[ni][:, :], in_=PS[bank][:, :], func=AF.Exp, scale=-1.0),
                 reads=[b_PS[bank]], writes=[b_NL[ni]])
            S.op(ACT, lambda e: e.activation(out=NL[ni][:, :], in_=NL[ni][:, :], func=AF.Ln, bias=oneS[:, 0:1], scale=1.0),
                 reads=[b_NL[ni]] + C, writes=[b_NL[ni]])
            S.op(ACT, lambda e: e.activation(out=NL[ni][:, :], in_=NL[ni][:, :], func=AF.Exp, scale=-1.0),
                 reads=[b_NL[ni]], writes=[b_NL[ni]])
            S.op(DVE, lambda e, w=w: e.tensor_tensor(out=SZ[:, 512 * w:512 * w + 512], in0=PS[bank][:, :], in1=NL[ni][:, :], op=ALU.mult),
                 reads=[b_PS[bank], b_NL[ni]], writes=[b_SZ])

    def k_out_rows(dst_d, tok_base_in_dst, c128, nblk, col0):
        nb = rNB.next()
        for b in range(nblk):
            S.op(PE, lambda e, b=b: e.transpose(PS[nb][:, 128 * b:128 * b + 128], KF[:, 128 * b:128 * b + 128], identF[:, :]),
                 reads=[b_KF] + C, writes=[b_PS[nb]])
        S.op(ACT, lambda e: e.activation(out=KO[:, 0:128 * nblk], in_=PS[nb][:, 0:128 * nblk], func=AF.Copy),
             reads=[b_PS[nb]], writes=[b_KO])
        dv = dst_d[tok_base_in_dst:tok_base_in_dst + 128 * nblk, col0:col0 + c128].rearrange("(b p) c -> p b c", p=128)
        ev = S.dma(SP, lambda e: e.dma_start(out=dv, in_=KO[:, 0:128 * nblk].rearrange("p (b c) -> p b c", c=128)[:, :, 0:c128]),
                   reads=[b_KO], semowner=b_KO)
        out_evs.append(ev)


    def load_fin_w(dst, bdst, src_d, nk, ncol, col0, width, scale):
        rW = Rot([0, 1])
        sv = src_d.rearrange("p (k c) -> p k c", c=ncol)
        for k in range(nk):
            i = rW.next()
            S.dma(SP, lambda e, k=k, i=i: e.dma_start(out=WF[i][:, 0:width], in_=sv[:, k, col0:col0 + width]),
                  writes=[b_WF[i]], semowner=b_WF[i])
            S.op(POOL, lambda e, k=k, i=i: e.tensor_scalar(out=dst[:, k, 0:width], in0=WF[i][:, 0:width], scalar1=scale, scalar2=None, op0=ALU.mult),
                 reads=[b_WF[i]], writes=[bdst])

    def sigmoid_from(bank, dst, bdst):
        S.op(ACT, lambda e: e.activation(out=dst, in_=PS[bank][:, :], func=AF.Exp, scale=-1.0), reads=[b_PS[bank]], writes=[bdst])
        S.op(ACT, lambda e: e.activation(out=dst, in_=dst, func=AF.Ln, bias=oneS[:, 0:1], scale=1.0), reads=[bdst] + C, writes=[bdst])
        S.op(ACT, lambda e: e.activation(out=dst, in_=dst, func=AF.Exp, scale=-1.0), reads=[bdst], writes=[bdst])

    def final_stage(tok0, ntok_total, x_src, y_dst, xTsrc, nwin, wcols):
        load_fin_w(WBA, b_WBA, wba_d, 4, 1024, 0, 1024, 1.0)
        load_fin_w(WBB, b_WBB, wbb_d, 4, 1024, 0, 1024, 1.0)
        for w in range(nwin):
            n = wcols
            cs = slice(wcols * w, wcols * w + n)
            for j in range(8):
                wga, bwga = load_w(CH_GA(j))
                wgb, bwgb = load_w(CH_GB(j))
                bka = rPJ.next()
                proj_fm(wga, bwga, bka, n, lambda k: xTsrc[:, k, cs])
                sigmoid_from_n(bka, SG[0][:, 0:n], b_SG[0], n)
                bkb = rPJ.next()
                proj_fm(wgb, bwgb, bkb, n, lambda k: xTsrc[:, k, cs])
                sigmoid_from_n(bkb, SG[1][:, 0:n], b_SG[1], n)
                ba = rST.next()
                for cc in range(4):
                    S.op(PE, lambda e, cc=cc, j=j: e.matmul(PS[ba][:, 0:n], lhsT=WBA[:, cc, 128 * j:128 * j + 128], rhs=aT[:, cc, cs],
                                                          start=(cc == 0), stop=(cc == 3)), reads=[b_WBA] + b_aT, writes=[b_PS[ba]])
                bb = rST.next()
                for cc in range(4):
                    S.op(PE, lambda e, cc=cc, j=j: e.matmul(PS[bb][:, 0:n], lhsT=WBB[:, cc, 128 * j:128 * j + 128], rhs=bT[:, cc, cs],
                                                          start=(cc == 0), stop=(cc == 3)), reads=[b_WBB] + b_bT, writes=[b_PS[bb]])
                S.op(DVE, lambda e: e.tensor_tensor(out=T1[0][:, 0:n], in0=PS[ba][:, 0:n], in1=SG[0][:, 0:n], op=ALU.mult),
                     reads=[b_PS[ba], b_SG[0]], writes=[b_T1[0]])
                S.op(DVE, lambda e: e.tensor_tensor(out=T1[1][:, 0:n], in0=PS[bb][:, 0:n], in1=SG[1][:, 0:n], op=ALU.mult),
                     reads=[b_PS[bb], b_SG[1]], writes=[b_T1[1]])
                S.op(POOL, lambda e, j=j: e.tensor_tensor(out=MIX[:, j, 0:n], in0=T1[0][:, 0:n], in1=T1[1][:, 0:n], op=ALU.add),
                     reads=[b_T1[0], b_T1[1]], writes=[b_MIX[j]])
            for half in range(2):
                load_fin_w(WOUT, b_WOUT, wout_d, 8, 1024, 512 * half, 512, 1.0)
                nblk = (n + 127) // 128
                for b in range(nblk):
                    nt = min(128, n - 128 * b)
                    yb = UA if (b % 2 == 0) else UB
                    for k in range(8):
                        S.op(PE, lambda e, k=k, b=b, nt=nt, yb=yb: e.matmul(PS[yb][0:nt, :], lhsT=MIX[:, k, 128 * b:128 * b + nt], rhs=WOUT[:, k, :],
                                                                 start=(k == 0), stop=(k == 7)), reads=b_MIX + [b_WOUT], writes=[b_PS[yb]])
                    xi = (b % 2)
                    r0 = tok0 + wcols * w + 128 * b
                    S.dma(SP, lambda e, r0=r0, nt=nt, xi=xi, half=half: e.dma_start(out=XR[xi][0:nt, :], in_=x_src[r0:r0 + nt, 512 * half:512 * half + 512]),
                          writes=[b_XR[xi]], semowner=b_XR[xi])
                    S.op(DVE, lambda e, nt=nt, xi=xi, yb=yb: e.tensor_tensor(out=XR[xi][0:nt, :], in0=XR[xi][0:nt, :], in1=PS[yb][0:nt, :], op=ALU.add),
                         reads=[b_XR[xi], b_PS[yb]], writes=[b_XR[xi]])
                    ev = S.dma(SP, lambda e, r0=r0, nt=nt, xi=xi, half=half: e.dma_start(out=y_dst[r0:r0 + nt, 512 * half:512 * half + 512], in_=XR[xi][0:nt, :]),
                               reads=[b_XR[xi]], semowner=b_XR[xi])
                    out_evs.append(ev)

    def sigmoid_from_n(bank, dst, bdst, n):
        S.op(ACT, lambda e: e.activation(out=dst, in_=PS[bank][:, 0:n], func=AF.Exp, scale=-1.0), reads=[b_PS[bank]], writes=[bdst])
        S.op(ACT, lambda e: e.activation(out=dst, in_=dst, func=AF.Ln, bias=oneS[:, 0:1], scale=1.0), reads=[bdst] + C, writes=[bdst])
        S.op(ACT, lambda e: e.activation(out=dst, in_=dst, func=AF.Exp, scale=-1.0), reads=[bdst], writes=[bdst])

    import os as _os
    STOP = int(_os.environ.get("MK_STOP", "0"))

    def finish():
        while bulk:
            issue_bulk(1)
        evs = list(out_evs) + list(S.recent_dma)
        if S.bulk_cnt:
            evs.append(Ev("d", S.bulk_sem, S.bulk_cnt))
        for e in (PE, ACT, DVE, POOL):
            for i in range(len(S.streams[e]) - 1, -1, -1):
                r = S.streams[e][i]
                if r["fn"] is not None and r["dma"] is None:
                    evs.append(Ev("e", e, i))
                    break
        S.wait_all(SP, evs)
        S.emit()
        st.close()
        return nc

    slot_prev = [0, 1, 2, 3]
    spare = [4]

    prologue(xh_d, 0)
    fence()
    if STOP == 1:
        return finish()
    for c in range(4):
        wk, bwk = load_w(CH_K(3, c))
        sl = slot_prev[c]
        for w in range(4):
            bank = rPJ.next()
            proj_fm(wk, bwk, bank, 512, xT_win(w))
            norm_evac(bank, 512, 5, K3[sl][:, 512 * w:512 * w + 512], [b_K3[sl]])
        wv, bwv = load_w(CH_V(3, c))
        proj_v_blocks(wv, bwv, [(r, 16, r) for r in range(16)], V3[sl], b_V3[sl], scale_flag=True)
        wk, bwk = load_w(CH_K(2, c))
        bank = rPJ.next()
        proj_fm(wk, bwk, bank, 512, xT_win(3))
        norm_evac(bank, 512, 3, K2t[c][:, :], [b_K2t[c]])
        wv, bwv = load_w(CH_V(2, c))
        proj_v_blocks(wv, bwv, [(1536 + r, 4, r) for r in range(4)], V2t[c], b_V2t[c], scale_flag=True)
        wk, bwk = load_w(CH_K(1, c))
        bank = rPJ.next()
        proj_fm(wk, bwk, bank, 128, lambda k: xT[:, k, SBT - 128:SBT])
        norm_evac(bank, 128, 1, K1t[c][:, :], [b_K1t[c]])
        wv, bwv = load_w(CH_V(1, c))
        proj_v_blocks(wv, bwv, [(SBT - 128, 1, 0)], V1t[c], b_V1t[c], scale_flag=True)
    for kvh in range(2):
        wk, bwk = load_w(CH_BK, dup=kvh)
        bank = rPJ.next()
        proj_fm(wk, bwk, bank, 128, lambda k: xT[:, k, SBT - 128:SBT])
        norm_evac(bank, 128, 7, KBt[kvh][:, :], [b_KBt[kvh]])
        wv, bwv = load_w(CH_BV, dup=kvh)
        proj_v_blocks(wv, bwv, [(SBT - 128, 1, 0)], VBt[kvh], b_VBt[kvh], scale_flag=True)
    fence()
    if STOP == 2:
        return finish()

    def k_tail_out(dst_ap64or128, ncols):
        nb = rNB.next()
        S.op(PE, lambda e: e.transpose(PS[nb][:, 0:128], KF[:, 384:512], identF[:, :]), reads=[b_KF] + C, writes=[b_PS[nb]])
        S.op(ACT, lambda e: e.activation(out=KO[:, 0:128], in_=PS[nb][:, 0:128], func=AF.Copy), reads=[b_PS[nb]], writes=[b_KO])
        ev = S.dma(SP, lambda e: e.dma_start(out=dst_ap64or128, in_=KO[:, 0:ncols]), reads=[b_KO], semowner=b_KO)
        out_evs.append(ev)

    for s in range(NSB):
        last = (s == NSB - 1)
        prologue(x_d, s * SBT)
        fence()
        for kvh in range(2):
            S.op(POOL, lambda e: e.tensor_copy(out=K1c[:, 0:128], in_=KBt[kvh][:, :]), reads=[b_KBt[kvh]], writes=[b_K1c])
            S.op(POOL, lambda e: e.tensor_copy(out=V1c[:, 0:1, :], in_=VBt[kvh][:, :, :]), reads=[b_VBt[kvh]], writes=[b_V1c])
            wk, bwk = load_w(CH_BK, dup=kvh)
            for w in range(4):
                bank = rPJ.next()
                proj_fm(wk, bwk, bank, 512, xT_win(w))
                f32o = (KF[:, :], [b_KF]) if (last and w == 3) else None
                norm_evac(bank, 512, 7, K1c[:, 128 + 512 * w:128 + 512 * w + 512], [b_K1c], f32_out=f32o)
                if last and w == 3:
                    k_tail_out(pb_d[:, 64 * kvh:64 * kvh + 64], 64)
            wv, bwv = load_w(CH_BV, dup=kvh)
            vob = (lambda bi, kvh=kvh: (pb_d[:, 128 + 64 * kvh:128 + 64 * kvh + 64] if bi == 15 else None)) if last else None
            proj_v_blocks(wv, bwv, [(128 * b, 1, b + 1) for b in range(16)], V1c, b_V1c, vout=vob, vcols=64)
            S.op(POOL, lambda e: e.tensor_copy(out=KBt[kvh][:, :], in_=K1c[:, SBT:SBT + 128]), reads=[b_K1c], writes=[b_KBt[kvh]])
            S.op(POOL, lambda e: e.tensor_copy(out=VBt[kvh][:, :, :], in_=V1c[:, 16:17, :]), reads=[b_V1c], writes=[b_VBt[kvh]])
            for c in (2 * kvh, 2 * kvh + 1):
                silu_chunk(CH_ZB(c))
                wq, bwq = load_w(CH_BQ(c))
                for w in range(4):
                    bank = rPJ.next()
                    proj_fm(wq, bwq, bank, 512, xT_win(w))
                    norm_evac(bank, 512, 6, QT[0][:, :], [b_QT[0]])
                    started = [False, False]
                    attn_window(w, (2 * c, 2 * c + 1), segs_g1(w, K1c, b_K1c, V1c, b_V1c, QT[0], b_QT[0], qoff=512 * w), started)
                    finish_window(w, c, bT, b_bT, sink_col=c)
        if STOP == 3:
            return finish()
        for c in range(4):
            silu_chunk(CH_ZA(c))
            S.op(POOL, lambda e: e.tensor_copy(out=K1c[:, 0:128], in_=K1t[c][:, :]), reads=[b_K1t[c]], writes=[b_K1c])
            S.op(POOL, lambda e: e.tensor_copy(out=V1c[:, 0:1, :], in_=V1t[c][:, :, :]), reads=[b_V1t[c]], writes=[b_V1c])
            S.op(POOL, lambda e: e.tensor_copy(out=K2c[:, 0:512], in_=K2t[c][:, :]), reads=[b_K2t[c]], writes=[b_K2c])
            S.op(POOL, lambda e: e.tensor_copy(out=V2c[:, 0:4, :], in_=V2t[c][:, :, :]), reads=[b_V2t[c]], writes=[b_V2c])
            slc = spare[0]; slp = slot_prev[c]
            plan = [(1, K1c, b_K1c, 128, 1), (2, K2c, b_K2c, 512, 3), (3, K3[slc], b_K3[slc], 0, 5)]
            for (g, Kd, bKd, koff, gcol) in plan:
                wk, bwk = load_w(CH_K(g, c))
                for w in range(4):
                    bank = rPJ.next()
                    proj_fm(wk, bwk, bank, 512, xT_win(w))
                    need_out = last and (g == 3 or w == 3)
                    f32o = (KF[:, :], [b_KF]) if need_out else None
                    norm_evac(bank, 512, gcol, Kd[:, koff + 512 * w:koff + 512 * w + 512], [bKd], f32_out=f32o)
                    if need_out:
                        if g == 3:
                            k_out_rows(pa_d[2], 512 * w, 128, 4, 128 * c)
                        elif g == 2:
                            k_out_rows(pa_d[1], 0, 128, 4, 128 * c)
                        else:
                            k_tail_out(pa_d[0][:, 128 * c:128 * c + 128], 128)
            wv, bwv = load_w(CH_V(1, c))
            vo1 = (lambda bi, c=c: (pa_d[0][:, 512 + 128 * c:512 + 128 * c + 128] if bi == 15 else None)) if last else None
            proj_v_blocks(wv, bwv, [(128 * b, 1, b + 1) for b in range(16)], V1c, b_V1c, vout=vo1)
            wv, bwv = load_w(CH_V(2, c))

            def vo2f(bi, c=c):
                j, r = bi // 4, bi % 4
                if j != 3:
                    return None
                return pa_d[1][:, 512 + 128 * c:512 + 128 * c + 128].rearrange("(i r) c -> r i c", r=4)[r]
            proj_v_blocks(wv, bwv, [(512 * j + r, 4, 4 + 4 * j + r) for j in range(4) for r in range(4)], V2c, b_V2c,
                          vout=(vo2f if last else None))
            wv, bwv = load_w(CH_V(3, c))

            def vo3f(bi, c=c):
                return pa_d[2][:, 512 + 128 * c:512 + 128 * c + 128].rearrange("(i r) c -> r i c", r=16)[bi]
            proj_v_blocks(wv, bwv, [(r, 16, r) for r in range(16)], V3[slc], b_V3[slc], vout=(vo3f if last else None))
            wq = [load_w(CH_Q(g, c)) for g in (1, 2, 3)]
            for w in range(4):
                started = [False, False]
                hg = (2 * c, 2 * c + 1)
                for gi in range(3):
                    bank = rPJ.next()
                    proj_fm(wq[gi][0], wq[gi][1], bank, 512, xT_win(w))
                    norm_evac(bank, 512, 2 * gi, QT[gi][:, :], [b_QT[gi]])
                attn_window(w, hg, segs_g1(w, K1c, b_K1c, V1c, b_V1c, QT[0], b_QT[0], qoff=512 * w), started)
                attn_window(w, hg, segs_g2(w, K2c, b_K2c, V2c, b_V2c, QT[1], b_QT[1], qoff=512 * w), started)
                attn_window(w, hg, segs_g3(w, K3[slc], b_K3[slc], K3[slp], b_K3[slp], V3[slc], b_V3[slc], V3[slp], b_V3[slp],
                                           QT[2], b_QT[2], qoff=512 * w), started)
                finish_window(w, c, aT, b_aT)
            S.op(POOL, lambda e: e.tensor_copy(out=K1t[c][:, :], in_=K1c[:, SBT:SBT + 128]), reads=[b_K1c], writes=[b_K1t[c]])
            S.op(POOL, lambda e: e.tensor_copy(out=V1t[c][:, :, :], in_=V1c[:, 16:17, :]), reads=[b_V1c], writes=[b_V1t[c]])
            S.op(POOL, lambda e: e.tensor_copy(out=K2t[c][:, :], in_=K2c[:, SBT:SBT + 512]), reads=[b_K2c], writes=[b_K2t[c]])
            S.op(POOL, lambda e: e.tensor_copy(out=V2t[c][:, :, :], in_=V2c[:, 16:20, :]), reads=[b_V2c], writes=[b_V2t[c]])
            spare[0] = slp
            slot_prev[c] = slc
        fence()
        if STOP == 4:
            return finish()
        final_stage(s * SBT, SBT, x_d, y_d, xT, 4, 512)
        fence()
        if STOP == 5:
            return finish()


    if with_sample:
        a_off[0] = 0
        CT = [carve([128, 1024], F32) for _ in range(2)]; b_CT = [S.buf("ct%d" % i) for i in range(2)]
        TMP = [carve([128, 512], F32) for _ in range(2)]; b_TMP = [S.buf("tmp%d" % i) for i in range(2)]
        PBC = carve([128, 512], F32); b_PBC = S.buf("pbc")
        SC = carve([128, 16], F32); b_SC = S.buf("sc")
        QS = carve([128, 16, NS], F32); b_QS = S.buf("qs")
        KS = carve([128, 14, NS], F32); b_KS = S.buf("ks")
        VS = carve([128, 14, NS], F32); b_VS = S.buf("vs")
        ZS = carve([128, 8, NS], F32); b_ZS = S.buf("zs")
        P0 = carve([128, 16, NS], F32); b_P0 = S.buf("p0")
        UT = carve([128, 8, NS], F32); b_UT = S.buf("ut")
        ZT = carve([128, 8, NS], F32); b_ZT = S.buf("zt")
        NR = carve([NS, 1024], F32); b_NR = S.buf("nr")
        NRB = carve([NS, 256], F32); b_NRB = S.buf("nrb")
        XSS = carve([NS, 1024], F32); b_XSS = S.buf("xss")
        XSB = carve([NS, 1024], BF16); b_XSB = S.buf("xsb")
        onesF = carve([128, 1], F32)
        bonesF = carve([128, 128], F32)
        assert a_off[0] <= ARENA_BYTES
        S.op(DVE, lambda e: e.memset(onesF, 1.0), writes=[b_SC])
        S.dma(SP, lambda e: e.dma_start(out=bonesF, in_=bones_d), writes=[b_PBC], semowner=b_PBC)
        S.dma(SP, lambda e: e.dma_start(out=XSS, in_=xs_d), writes=[b_XSS], semowner=b_XSS)
        S.op(ACT, lambda e: e.activation(out=XSB, in_=XSS, func=AF.Square, accum_out=stat[0:NS, 0:1]), reads=[b_XSS], writes=[b_XSB, b_stat])
        S.op(ACT, lambda e: e.activation(out=stat[0:NS, 1:2], in_=stat[0:NS, 0:1], func=AF.Ln, bias=epsS[0:NS, 0:1], scale=1.0 / D), reads=[b_stat] + C, writes=[b_stat])
        S.op(ACT, lambda e: e.activation(out=stat[0:NS, 2:3], in_=stat[0:NS, 1:2], func=AF.Exp, scale=-0.5), reads=[b_stat], writes=[b_stat])
        S.op(DVE, lambda e: e.tensor_scalar(out=XSB, in0=XSS, scalar1=stat[0:NS, 2:3], scalar2=None, op0=ALU.mult), reads=[b_XSS, b_stat], writes=[b_XSB])
        bank = rNB.next()
        pbf = PS[bank][:, :].bitcast(BF16)
        for k in range(8):
            S.op(PE, lambda e, k=k: e.transpose(pbf[:, NS * k:NS * k + NS], XSB[:, 128 * k:128 * k + 128], identB[0:NS, 0:NS]),
                 reads=[b_XSB] + C, writes=[b_PS[bank]])
        S.op(DVE, lambda e: e.tensor_copy(out=xT[:, :, 0:NS], in_=pbf[:, 0:8 * NS].rearrange("p (k t) -> p k t", t=NS)),
             reads=[b_PS[bank]], writes=[b_xT])
        xs_rhs = lambda k: xT[:, k, 0:NS]

        def proj_s(ch, dup=None):
            wq_, bw_ = load_w(ch, dup=dup)
            bank = rPJ.next()
            proj_fm(wq_, bw_, bank, NS, xs_rhs)
            return bank

        def norm_s(bank, gcol, dst, bdst):
            ti = 0
            S.op(ACT, lambda e: e.activation(out=TMP[ti][:, 0:NS], in_=PS[bank][:, 0:NS], func=AF.Square), reads=[b_PS[bank]], writes=[b_TMP[ti]])
            nb = rNB.next()
            S.op(PE, lambda e: e.matmul(PS[nb][:, 0:NS], lhsT=bonesF, rhs=TMP[ti][:, 0:NS], start=True, stop=True), reads=[b_TMP[ti], b_PBC], writes=[b_PS[nb]])
            S.op(ACT, lambda e: e.activation(out=TMP[ti][:, 0:NS], in_=PS[nb][:, 0:NS], func=AF.Ln, bias=epsS[:, 0:1], scale=1.0), reads=[b_PS[nb]] + C, writes=[b_TMP[ti]])
            S.op(ACT, lambda e: e.activation(out=TMP[ti][:, 0:NS], in_=TMP[ti][:, 0:NS], func=AF.Exp, scale=-0.5), reads=[b_TMP[ti]], writes=[b_TMP[ti]])
            S.op(DVE, lambda e: e.scalar_tensor_tensor(out=dst, in0=PS[bank][:, 0:NS], scalar=qkgS[:, gcol:gcol + 1], in1=TMP[ti][:, 0:NS], op0=ALU.mult, op1=ALU.mult),
                 reads=[b_PS[bank], b_TMP[ti]] + C, writes=[bdst])

        for g in (1, 2, 3):
            for c in range(4):
                norm_s(proj_s(CH_Q(g, c)), 2 * (g - 1), QS[:, 4 * (g - 1) + c, :], b_QS)
                norm_s(proj_s(CH_K(g, c)), 2 * (g - 1) + 1, KS[:, 4 * (g - 1) + c, :], b_KS)
                bank = proj_s(CH_V(g, c))
                S.op(DVE, lambda e, g=g, c=c, bank=bank: e.tensor_copy(out=VS[:, 4 * (g - 1) + c, :], in_=PS[bank][:, 0:NS]), reads=[b_PS[bank]], writes=[b_VS])
        for c in range(4):
            norm_s(proj_s(CH_BQ(c)), 6, QS[:, 12 + c, :], b_QS)
        for kvh in range(2):
            norm_s(proj_s(CH_BK, dup=kvh), 7, KS[:, 12 + kvh, :], b_KS)
            bank = proj_s(CH_BV, dup=kvh)
            S.op(DVE, lambda e, kvh=kvh, bank=bank: e.tensor_copy(out=VS[:, 12 + kvh, :], in_=PS[bank][:, 0:NS]), reads=[b_PS[bank]], writes=[b_VS])
        for i, ch in enumerate([CH_ZA(c) for c in range(4)] + [CH_ZB(c) for c in range(4)]):
            bank = proj_s(ch)
            ti = 1
            S.op(ACT, lambda e, bank=bank: e.activation(out=TMP[ti][:, 0:NS], in_=PS[bank][:, 0:NS], func=AF.Exp, scale=-1.0), reads=[b_PS[bank]], writes=[b_TMP[ti]])
            S.op(ACT, lambda e: e.activation(out=TMP[ti][:, 0:NS], in_=TMP[ti][:, 0:NS], func=AF.Ln, bias=oneS[:, 0:1], scale=1.0), reads=[b_TMP[ti]] + C, writes=[b_TMP[ti]])
            S.op(ACT, lambda e: e.activation(out=TMP[ti][:, 0:NS], in_=TMP[ti][:, 0:NS], func=AF.Exp, scale=-1.0), reads=[b_TMP[ti]], writes=[b_TMP[ti]])
            S.op(DVE, lambda e, i=i, bank=bank: e.tensor_tensor(out=ZS[:, i, :], in0=PS[bank][:, 0:NS], in1=TMP[ti][:, 0:NS], op=ALU.mult),
                 reads=[b_PS[bank], b_TMP[ti]], writes=[b_ZS])

        for g in range(3):
            nb = rNB.next()
            for c in range(4):
                S.op(PE, lambda e, g=g, c=c: e.transpose(PS[nb][0:NS, 128 * c:128 * c + 128], KS[:, 4 * g + c, :], identF[:, :]), reads=[b_KS] + C, writes=[b_PS[nb]])
            S.op(DVE, lambda e: e.tensor_copy(out=NR[:, 0:512], in_=PS[nb][0:NS, :]), reads=[b_PS[nb]], writes=[b_NR])
            nb2 = rNB.next()
            for c in range(4):
                S.op(PE, lambda e, g=g, c=c: e.transpose(PS[nb2][0:NS, 128 * c:128 * c + 128], VS[:, 4 * g + c, :], identF[:, :]), reads=[b_VS] + C, writes=[b_PS[nb2]])
            S.op(DVE, lambda e: e.tensor_copy(out=NR[:, 512:1024], in_=PS[nb2][0:NS, :]), reads=[b_PS[nb2]], writes=[b_NR])
            L = (128, 512, 2048)[g]
            ev = S.dma(SP, lambda e, g=g, L=L: e.dma_start(out=sa_d[g][:, L - 1, :], in_=NR[:, :]), reads=[b_NR], semowner=b_NR)
            out_evs.append(ev)
        nb = rNB.next()
        for kvh in range(2):
            S.op(PE, lambda e, kvh=kvh: e.transpose(PS[nb][0:NS, 128 * kvh:128 * kvh + 128], KS[:, 12 + kvh, :], identF[:, :]), reads=[b_KS] + C, writes=[b_PS[nb]])
            S.op(PE, lambda e, kvh=kvh: e.transpose(PS[nb][0:NS, 256 + 128 * kvh:256 + 128 * kvh + 128], VS[:, 12 + kvh, :], identF[:, :]), reads=[b_VS] + C, writes=[b_PS[nb]])
        S.op(DVE, lambda e: e.tensor_copy(out=NRB[:, :].rearrange("p (a b) -> p a b", b=64),
                                          in_=PS[nb][0:NS, :].rearrange("p (a b) -> p a b", b=128)[:, :, 0:64]), reads=[b_PS[nb]], writes=[b_NRB])
        ev = S.dma(SP, lambda e: e.dma_start(out=sb_d[:, 127, :], in_=NRB[:, :]), reads=[b_NRB], semowner=b_NRB)
        out_evs.append(ev)

        for qc in range(16):
            kc = qc if qc < 12 else 12 + (qc - 12) // 2
            S.op(DVE, lambda e, qc=qc, kc=kc: e.tensor_tensor(out=TMP[0][:, 0:NS], in0=QS[:, qc, :], in1=KS[:, kc, :], op=ALU.mult), reads=[b_QS, b_KS], writes=[b_TMP[0]])
            nb = rNB.next()
            S.op(PE, lambda e: e.matmul(PS[nb][:, 0:NS], lhsT=bonesF, rhs=TMP[0][:, 0:NS], start=True, stop=True), reads=[b_TMP[0], b_PBC], writes=[b_PS[nb]])
            S.op(ACT, lambda e, qc=qc: e.activation(out=P0[:, qc, :], in_=PS[nb][:, 0:NS], func=AF.Exp, scale=8.0), reads=[b_PS[nb]], writes=[b_P0])

        PU, PZ = UA, UB
        first_u = [True]
        rCT = Rot([0, 1]); rTMP = Rot([0, 1])
        for n in range(NS):
            for g in range(4):
                ci = rCT.next()
                if g < 3:
                    dil = (1, 4, 16)[g]; L = (128, 512, 2048)[g]
                    S.dma(SP, lambda e, g=g, n=n, L=L, dil=dil, ci=ci: e.dma_start(out=CT[ci][:, :], in_=ca_d[g][n, 0:L:dil, :]), writes=[b_CT[ci]], semowner=b_CT[ci])
                    kview = CT[ci][:, 0:512]
                    vview = CT[ci][:, 512:1024]
                else:
                    S.dma(SP, lambda e, n=n, ci=ci: e.dma_start(out=CT[ci][:, 0:256], in_=cb_d[n, :, :]), writes=[b_CT[ci]], semowner=b_CT[ci])
                    kview = CT[ci][:, 0:128].rearrange("p (a d) -> p a d", d=64).unsqueeze(2).to_broadcast([128, 2, 4, 64])
                    vview = CT[ci][:, 128:256].rearrange("p (a d) -> p a d", d=64).unsqueeze(2).to_broadcast([128, 2, 4, 64])
                qb_bank = rST.next()
                for c in range(4):
                    S.op(PE, lambda e, g=g, c=c, n=n: e.matmul(PS[qb_bank][:, 128 * c:128 * c + 128], lhsT=QS[:, 4 * g + c, n:n + 1].to_broadcast([128, 128]),
                                                             rhs=identF[:, :], start=True, stop=True), reads=[b_QS] + C, writes=[b_PS[qb_bank]])
                ti = rTMP.next()
                if g < 3:
                    S.op(DVE, lambda e, kview=kview, ti=ti: e.tensor_tensor(out=TMP[ti][:, :], in0=kview, in1=PS[qb_bank][:, :], op=ALU.mult),
                         reads=[b_CT[ci], b_PS[qb_bank]], writes=[b_TMP[ti]])
                else:
                    S.op(DVE, lambda e, kview=kview, ti=ti: e.tensor_tensor(out=TMP[ti][:, :].rearrange("p (a r d) -> p a r d", r=4, d=64), in0=kview,
                                                                           in1=PS[qb_bank][:, :].rearrange("p (a r d) -> p a r d", r=4, d=64), op=ALU.mult),
                         reads=[b_CT[ci], b_PS[qb_bank]], writes=[b_TMP[ti]])
                S.op(DVE, lambda e, ti=ti: e.tensor_reduce(out=SC[:, 0:8], in_=TMP[ti][:, :].rearrange("p (h d) -> p h d", d=64), axis=AX.X, op=ALU.add),
                     reads=[b_TMP[ti]], writes=[b_SC])
                S.op(DVE, lambda e: e.scalar_tensor_tensor(out=SC[:, 0:8], in0=SC[:, 0:8], scalar=0.125, in1=dbiasS[:, :], op0=ALU.mult, op1=ALU.add),
                     reads=[b_SC] + C, writes=[b_SC])
                S.op(ACT, lambda e: e.activation(out=SC[:, 8:16], in_=SC[:, 0:8], func=AF.Exp), reads=[b_SC], writes=[b_SC])
                S.op(DVE, lambda e: e.tensor_copy(out=PBC[:, :].rearrange("p (h d) -> p h d", d=64), in_=SC[:, 8:16].unsqueeze(2).to_broadcast([128, 8, 64])),
                     reads=[b_SC], writes=[b_PBC])
                ti2 = rTMP.next()
                if g < 3:
                    S.op(DVE, lambda e, vview=vview, ti2=ti2: e.tensor_tensor(out=TMP[ti2][:, :], in0=vview, in1=PBC[:, :], op=ALU.mult),
                         reads=[b_CT[ci], b_PBC], writes=[b_TMP[ti2]])
                else:
                    S.op(DVE, lambda e, vview=vview, ti2=ti2: e.tensor_tensor(out=TMP[ti2][:, :].rearrange("p (a r d) -> p a r d", r=4, d=64), in0=vview,
                                                                             in1=PBC[:, :].rearrange("p (a r d) -> p a r d", r=4, d=64), op=ALU.mult),
                         reads=[b_CT[ci], b_PBC], writes=[b_TMP[ti2]])
                colbase = 0 if g < 3 else 4
                for c in range(4):
                    col = (colbase + c) * NS + n
                    fu = first_u[0]
                    S.op(PE, lambda e, c=c, col=col, fu=fu, ti2=ti2: e.matmul(PS[PU][:, col:col + 1], lhsT=TMP[ti2][:, 128 * c:128 * c + 128], rhs=onesF,
                                                                         start=fu, stop=False, skip_group_check=True), reads=[b_TMP[ti2], b_SC], writes=[b_PS[PU]])
                    S.op(PE, lambda e, c=c, col=col, fu=fu: e.matmul(PS[PZ][:, col:col + 1], lhsT=PBC[:, 128 * c:128 * c + 128], rhs=onesF,
                                                               start=fu, stop=False, skip_group_check=True), reads=[b_PBC, b_SC], writes=[b_PS[PZ]])
                    first_u[0] = False
        S.op(DVE, lambda e: e.tensor_copy(out=UT[:, :, :].rearrange("p a b -> p (a b)"), in_=PS[PU][:, 0:8 * NS]), reads=[b_PS[PU]], writes=[b_UT])
        S.op(DVE, lambda e: e.tensor_copy(out=ZT[:, :, :].rearrange("p a b -> p (a b)"), in_=PS[PZ][:, 0:8 * NS]), reads=[b_PS[PZ]], writes=[b_ZT])
        for qc in range(16):
            uc = (qc % 4) if qc < 12 else 4 + (qc - 12)
            vc = qc if qc < 12 else 12 + (qc - 12) // 2
            S.op(DVE, lambda e, qc=qc, vc=vc: e.tensor_tensor(out=TMP[0][:, 0:NS], in0=P0[:, qc, :], in1=VS[:, vc, :], op=ALU.mult), reads=[b_P0, b_VS], writes=[b_TMP[0]])
            S.op(DVE, lambda e, uc=uc: e.tensor_tensor(out=UT[:, uc, :], in0=UT[:, uc, :], in1=TMP[0][:, 0:NS], op=ALU.add), reads=[b_UT, b_TMP[0]], writes=[b_UT])
            S.op(DVE, lambda e, uc=uc, qc=qc: e.tensor_tensor(out=ZT[:, uc, :], in0=ZT[:, uc, :], in1=P0[:, qc, :], op=ALU.add), reads=[b_ZT, b_P0], writes=[b_ZT])
        for c in range(4):
            S.op(DVE, lambda e, c=c: e.tensor_scalar(out=ZT[:, 4 + c, :], in0=ZT[:, 4 + c, :], scalar1=skS[:, c:c + 1], scalar2=None, op0=ALU.add), reads=[b_ZT] + C, writes=[b_ZT])
        S.op(DVE, lambda e: e.reciprocal(out=ZT[:, :, :], in_=ZT[:, :, :]), reads=[b_ZT], writes=[b_ZT])
        S.op(DVE, lambda e: e.tensor_tensor(out=UT[:, :, :], in0=UT[:, :, :], in1=ZT[:, :, :], op=ALU.mult), reads=[b_UT, b_ZT], writes=[b_UT])
        S.op(DVE, lambda e: e.tensor_tensor(out=aT[:, :, 0:NS], in0=UT[:, 0:4, :], in1=ZS[:, 0:4, :], op=ALU.mult), reads=[b_UT, b_ZS], writes=b_aT)
        S.op(DVE, lambda e: e.tensor_tensor(out=bT[:, :, 0:NS], in0=UT[:, 4:8, :], in1=ZS[:, 4:8, :], op=ALU.mult), reads=[b_UT, b_ZS], writes=b_bT)
        fence()
        final_stage(0, NS, xs_d, ys_d, xT, 1, NS)
        fence()

    return finish()

N_CORES = 8
NSB_FULL = 2
NS_FULL = 16
_prog_cache = {}


def _consts():
    ident = np.eye(128, dtype=np.float32)
    bones = np.zeros((128, 128), np.float32)
    bones[:64, :64] = 1.0 / 64
    bones[64:, 64:] = 1.0 / 64
    swapm = np.zeros((128, 128), np.float32)
    for m in range(128):
        swapm[(m + 64) % 128, m] = 1.0
    kk = np.arange(128)[:, None].astype(np.float64)
    qq = np.arange(128)[None, :].astype(np.float64)
    masks = np.zeros((128, 8, 256), np.float32)
    for h in range(8):
        m = 2.0 ** -(h + 1)
        md = np.where(qq >= kk, np.exp(-m * (qq - kk)), 0.0)
        mp = np.where(kk >= qq, np.exp(-m * (qq - kk + 128)), 0.0)
        masks[:, h, :128] = md
        masks[:, h, 128:] = mp
    dbias = np.zeros((128, 8), np.float32)
    for h in range(8):
        dbias[:, h] = -(2.0 ** -(h + 1)) * (128 - np.arange(128))
    return ident, bones, swapm, masks.reshape(128, 2048), dbias


def make_in_maps(inp, n_cores, NSB, NS, seq_len):
    f = lambda a: np.ascontiguousarray(a, dtype=np.float32)
    w_in = inp["w_in"][0]
    win = f(w_in.reshape(8, 128, NCH, 128).transpose(2, 1, 0, 3).reshape(NCH, 128, 1024))
    gain = f(inp["norm_gain"][0].reshape(8, 128).T)
    wba = f(inp["w_branch_a"][0].reshape(4, 128, 1024).transpose(1, 0, 2).reshape(128, 4096))
    wbb = f(inp["w_branch_b"][0].reshape(4, 128, 1024).transpose(1, 0, 2).reshape(128, 4096))
    wout = f(inp["w_out"][0].reshape(8, 128, 1024).transpose(1, 0, 2).reshape(128, 8192))
    qa = inp["qk_norm_a"][0]
    qb = inp["qk_norm_b"][0]
    cols = [qa[0, 0], qa[0, 1], qa[1, 0], qa[1, 1], qa[2, 0], qa[2, 1], qb[0], qb[1]]
    qkg = f(np.stack([np.tile(c, 2) for c in cols], axis=1))
    sinks = inp["b_sinks"][0]
    sk = f(np.stack([np.repeat(sinks[2 * c:2 * c + 2], 64) for c in range(4)], axis=1))
    ident, bones, swapm, masks, dbias = _consts()
    T = NSB * SBT
    halves = seq_len // T
    maps = []
    for core in range(n_cores):
        b, hf = core // halves, core % halves
        x = inp["x_prompt"][b, hf * T:(hf + 1) * T]
        if hf == 0:
            xh = np.zeros((SBT, D), np.float32)
            flag = np.zeros((128, 1), np.float32)
        else:
            xh = inp["x_prompt"][b, hf * T - SBT:hf * T]
            flag = np.ones((128, 1), np.float32)
        sl = slice(core * NS, (core + 1) * NS)
        m = dict(x=f(x), xh=f(xh), flag=flag, xs=f(inp["x_sample"][sl, 0]), win=win, gain=gain, wba=wba, wbb=wbb, wout=wout,
                 qkg=qkg, sk=sk, ident=ident, bones=bones, swapm=swapm, masks=masks, dbias=dbias,
                 ca1=f(inp["cache_a1_kv"][0, sl].reshape(NS, 128, 1024)),
                 ca2=f(inp["cache_a2_kv"][0, sl].reshape(NS, 512, 1024)),
                 ca3=f(inp["cache_a3_kv"][0, sl].reshape(NS, 2048, 1024)),
                 cb=f(inp["cache_b_kv"][0, sl].reshape(NS, 128, 256)))
        maps.append(m)
    return maps


def kernel(**inputs):
    key = (NSB_FULL, NS_FULL)
    if key not in _prog_cache:
        _prog_cache[key] = build_program(NSB_FULL, NS_FULL)
    nc = _prog_cache[key]
    maps = make_in_maps(inputs, N_CORES, NSB_FULL, NS_FULL, 8192)
    res = run_bass_kernel_spmd(nc, maps, core_ids=list(range(N_CORES)))
    R = res.results
    T = NSB_FULL * SBT
    y = np.stack([np.concatenate([R[2 * b]["y"], R[2 * b + 1]["y"]], axis=0) for b in range(4)], axis=0)
    ys = np.concatenate([R[c]["ys"] for c in range(N_CORES)], axis=0)[:, None, :]
    pa = [np.stack([R[2 * b + 1][n].reshape(-1, 2, 8, 64) for b in range(4)], axis=0)[None] for n in ("pa1", "pa2", "pa3")]
    pb = np.stack([R[2 * b + 1]["pb"].reshape(-1, 2, 2, 64) for b in range(4)], axis=0)[None]
    sa = [np.concatenate([R[c][n] for c in range(N_CORES)], axis=0).reshape(128, -1, 2, 8, 64)[None] for n in ("sa1", "sa2", "sa3")]
    sbo = np.concatenate([R[c]["sb"] for c in range(N_CORES)], axis=0).reshape(128, -1, 2, 2, 64)[None]
    return (y.astype(np.float32), ys.astype(np.float32), pa[0], pa[1], pa[2], pb, sa[0], sa[1], sa[2], sbo)
```

```python
import numpy as np
from contextlib import ExitStack
import concourse.bass as bass
import concourse.mybir as mybir
from concourse.bass_utils import run_bass_kernel_spmd

F32 = mybir.dt.float32
BF16 = mybir.dt.bfloat16
ALU = mybir.AluOpType
AF = mybir.ActivationFunctionType
AX = mybir.AxisListType

PE, ACT, DVE, POOL, SP = "tensor", "scalar", "vector", "gpsimd", "sync"
ENGS = (PE, ACT, DVE, POOL, SP)


class Buf:
    __slots__ = ("name", "w", "r", "dsem", "dcnt", "excl")

    def __init__(self, name, excl=False):
        self.name = name
        self.excl = excl
        self.w = None
        self.r = {}
        self.dsem = None
        self.dcnt = 0


class Ev:
    __slots__ = ("kind", "a", "b")

    def __init__(self, kind, a, b):
        self.kind, self.a, self.b = kind, a, b


class _Rec:
    def __init__(self):
        self.call = None

    def __getattr__(self, name):
        def f(*args, **kwargs):
            self.call = (name, args, kwargs)
            return self
        return f


def _bind(fn):
    r = _Rec()
    fn(r)
    assert r.call is not None
    name, args, kwargs = r.call
    return lambda eng: getattr(eng, name)(*args, **kwargs)


class Sched:
    def __init__(self, nc, stack):
        self.nc = nc
        self.stack = stack
        self.streams = {e: [] for e in ENGS}
        self.esem = {e: stack.enter_context(nc.semaphore("es_" + e)) for e in ENGS}
        self.nsem = len(ENGS)
        self.bulk_sem = stack.enter_context(nc.semaphore("bulk"))
        self.bulk_cnt = 0
        self.final_waits = []
        self.recent_dma = []

    def buf(self, name, excl=False):
        return Buf(name, excl)

    def _deps(self, eng, reads, writes):
        deps = []
        for b in reads:
            if b.w is not None:
                deps.append(b.w)
            if b.excl:
                deps.extend(v for k, v in b.r.items() if k != eng)
        for b in writes:
            if b.w is not None:
                deps.append(b.w)
            deps.extend(b.r.values())
        out = []
        seen = {}
        for d in deps:
            if d.kind == "e":
                if d.a == PE and eng == PE:
                    continue
                k = ("e", d.a)
                if k not in seen or seen[k].b < d.b:
                    seen[k] = d
            else:
                k = ("d", id(d.a))
                if k not in seen or seen[k].b < d.b:
                    seen[k] = d
        return list(seen.values())

    def op(self, eng, fn, reads=(), writes=()):
        deps = self._deps(eng, reads, writes)
        idx = len(self.streams[eng])
        fn = _bind(fn)
        rec = {"fn": fn, "deps": deps, "needed": False, "dma": None}
        self.streams[eng].append(rec)
        for d in deps:
            if d.kind == "e":
                self.streams[d.a][d.b]["needed"] = True
        ev = Ev("e", eng, idx)
        for b in reads:
            b.r[eng] = ev
        for b in writes:
            b.w = ev
            b.r = {}
        return ev

    def dma(self, q, fn, reads=(), writes=(), semowner=None):
        deps = self._deps(q, reads, writes)
        for d in deps:
            if d.kind == "e":
                self.streams[d.a][d.b]["needed"] = True
        if semowner is None:
            sem = self.bulk_sem
            self.bulk_cnt += 16
            val = self.bulk_cnt
        else:
            if semowner.dsem is None:
                semowner.dsem = self.stack.enter_context(self.nc.semaphore("ds_" + semowner.name))
                self.nsem += 1
            semowner.dcnt += 16
            sem, val = semowner.dsem, semowner.dcnt
        fn = _bind(fn)
        rec = {"fn": fn, "deps": deps, "needed": False, "dma": (sem, 16)}
        self.streams[q].append(rec)
        ev = Ev("d", sem, val)
        if semowner is not None:
            self.recent_dma.append(ev)
        for b in reads:
            b.r[("d", id(sem))] = ev
        for b in writes:
            b.w = ev
            b.r = {}
        return ev

    def wait_all(self, eng, evs):
        best = {}
        for d in evs:
            k = ("e", d.a) if d.kind == "e" else ("d", id(d.a))
            if k not in best or best[k].b < d.b:
                best[k] = d
        evs = list(best.values())
        rec = {"fn": None, "deps": list(evs), "needed": False, "dma": None}
        for d in evs:
            if d.kind == "e":
                self.streams[d.a][d.b]["needed"] = True
        self.streams[eng].append(rec)

    def emit(self):
        nc = self.nc
        val = {}
        for e in ENGS:
            c = 0
            for i, rec in enumerate(self.streams[e]):
                if rec["dma"] is None and rec["fn"] is not None and rec["needed"]:
                    c += 1
                    val[(e, i)] = c
        self.maxval = {e: max([v for (ee, _), v in val.items() if ee == e], default=0) for e in ENGS}
        with nc.Block() as block:
            def run(eng_name):
                def body(eng):
                    waited = {}
                    for i, rec in enumerate(self.streams[eng_name]):
                        for d in rec["deps"]:
                            if d.kind == "e":
                                sem, v = self.esem[d.a], val[(d.a, d.b)]
                            else:
                                sem, v = d.a, d.b
                            k = id(sem)
                            if waited.get(k, 0) >= v:
                                continue
                            waited[k] = v
                            eng.wait_ge(sem, v)
                        if rec["fn"] is None:
                            continue
                        ins = rec["fn"](eng)
                        if rec["dma"] is not None:
                            ins.then_inc(rec["dma"][0], 16)
                        elif rec["needed"]:
                            ins.then_inc(self.esem[eng_name], 1)
                return body
            block.tensor(run(PE))
            block.scalar(run(ACT))
            block.vector(run(DVE))
            block.gpsimd(run(POOL))
            block.sync(run(SP))

D = 1024
SBT = 2048
NCH = 66
EPS = 1e-6


def CH_Q(g, c): return 12 * (g - 1) + c
def CH_K(g, c): return 12 * (g - 1) + 4 + c
def CH_V(g, c): return 12 * (g - 1) + 8 + c
def CH_BQ(c): return 36 + c
CH_BK, CH_BV = 40, 41
def CH_ZA(c): return 42 + c
def CH_ZB(c): return 46 + c
def CH_GA(j): return 50 + j
def CH_GB(j): return 58 + j


class Rot:
    def __init__(self, items):
        self.items, self.i = items, 0

    def next(self):
        it = self.items[self.i % len(self.items)]
        self.i += 1
        return it


def build_program(NSB, NS, with_sample=True, with_copy=True):
    nc = bass.Bass("TRN2", target_bir_lowering=False)
    T = NSB * SBT

    def din(name, shape):
        return nc.dram_tensor(name, list(shape), F32, kind="ExternalInput").ap()

    def dout(name, shape):
        return nc.dram_tensor(name, list(shape), F32, kind="ExternalOutput").ap()

    x_d = din("x", [T, D]); xh_d = din("xh", [SBT, D]); flag_d = din("flag", [128, 1])
    xs_d = din("xs", [NS, D])
    win_d = din("win", [NCH, 128, 8 * 128])
    gain_d = din("gain", [128, 8])
    wba_d = din("wba", [128, 4 * 1024]); wbb_d = din("wbb", [128, 4 * 1024]); wout_d = din("wout", [128, 8 * 1024])
    qkg_d = din("qkg", [128, 8]); sk_d = din("sk", [128, 4])
    ident_d = din("ident", [128, 128]); bones_d = din("bones", [128, 128]); swap_d = din("swapm", [128, 128])
    masks_d = din("masks", [128, 8 * 256])
    dbias_d = din("dbias", [128, 8])
    ca_d = [din("ca1", [NS, 128, 1024]), din("ca2", [NS, 512, 1024]), din("ca3", [NS, 2048, 1024])]
    cb_d = din("cb", [NS, 128, 256])
    y_d = dout("y", [T, D]); ys_d = dout("ys", [NS, D])
    pa_d = [dout("pa1", [128, 1024]), dout("pa2", [512, 1024]), dout("pa3", [2048, 1024])]
    pb_d = dout("pb", [128, 256])
    sa_d = [dout("sa1", [NS, 128, 1024]), dout("sa2", [NS, 512, 1024]), dout("sa3", [NS, 2048, 1024])]
    sb_d = dout("sb", [NS, 128, 256])

    st = ExitStack()
    S = Sched(nc, st)
    out_evs = []

    def sb(name, shape, dt=F32):
        return nc.alloc_sbuf_tensor(name, list(shape), dt)

    xT = sb("xT", [128, 8, SBT], BF16); b_xT = S.buf("xT")
    WST = [sb("wst%d" % i, [128, 8, 128], F32) for i in range(2)]; b_WST = [S.buf("wst%d" % i) for i in range(2)]
    WBF = [sb("wbf%d" % i, [128, 8, 128], BF16) for i in range(4)]; b_WBF = [S.buf("wbf%d" % i) for i in range(4)]
    K3 = [sb("k3_%d" % i, [128, SBT], BF16) for i in range(5)]; b_K3 = [S.buf("k3_%d" % i) for i in range(5)]
    V3 = [sb("v3_%d" % i, [128, 16, 192], BF16) for i in range(5)]; b_V3 = [S.buf("v3_%d" % i) for i in range(5)]
    K2t = [sb("k2t%d" % i, [128, 512], BF16) for i in range(4)]; b_K2t = [S.buf("k2t%d" % i) for i in range(4)]
    V2t = [sb("v2t%d" % i, [128, 4, 192], BF16) for i in range(4)]; b_V2t = [S.buf("v2t%d" % i) for i in range(4)]
    K1t = [sb("k1t%d" % i, [128, 128], BF16) for i in range(4)]; b_K1t = [S.buf("k1t%d" % i) for i in range(4)]
    V1t = [sb("v1t%d" % i, [128, 1, 192], BF16) for i in range(4)]; b_V1t = [S.buf("v1t%d" % i) for i in range(4)]
    KBt = [sb("kbt%d" % i, [128, 128], BF16) for i in range(2)]; b_KBt = [S.buf("kbt%d" % i) for i in range(2)]
    VBt = [sb("vbt%d" % i, [128, 1, 192], BF16) for i in range(2)]; b_VBt = [S.buf("vbt%d" % i) for i in range(2)]
    aT = sb("aT", [128, 4, SBT], BF16); b_aT = [S.buf("aT%d" % i) for i in range(4)]
    bT = sb("bT", [128, 4, SBT], BF16); b_bT = [S.buf("bT%d" % i) for i in range(4)]
    MASK = sb("mask", [128, 8, 256], BF16); b_const = S.buf("const")
    identF = sb("identF", [128, 128], F32); identB = sb("identB", [128, 128], BF16)
    bonesB = sb("bonesB", [128, 128], BF16); swapF = sb("swapF", [128, 128], F32)
    gainS = sb("gainS", [128, 8], F32); qkgS = sb("qkgS", [128, 8], F32); skS = sb("skS", [128, 4], F32)
    flagS = sb("flagS", [128, 1], F32); epsS = sb("epsS", [128, 1], F32); oneS = sb("oneS", [128, 1], F32)
    onesB = sb("onesB", [128, 64], BF16)
    dbiasS = sb("dbiasS", [128, 8], F32)
    stat = sb("stat", [128, 8], F32); b_stat = S.buf("stat")

    ARENA_BYTES = 52 * 1024
    arena = sb("arena", [128, ARENA_BYTES // 2], BF16)
    a_off = [0]

    def carve(shape, dt):
        n = int(np.prod(shape[1:]))
        nb = n * (4 if dt == F32 else 2)
        nb_al = (nb + 63) // 64 * 64
        o = a_off[0]
        assert o + nb_al <= ARENA_BYTES, (o, nb_al)
        a_off[0] += nb_al
        ap = arena[0:shape[0], o // 2:(o + nb) // 2]
        if dt == F32:
            ap = ap.bitcast(F32)
        if len(shape) == 3:
            ap = ap.rearrange("p (a b) -> p a b", b=shape[2])
        return ap

    a_off[0] = 0
    XST = [carve([128, 1024], F32) for _ in range(2)]; b_XST = [S.buf("xst%d" % i) for i in range(2)]
    XB = [carve([128, 1024], BF16) for _ in range(2)]; b_XB = [S.buf("xb%d" % i) for i in range(2)]
    cst = carve([128, 8 * 256], F32)
    a_off[0] = 0
    K2c = carve([128, 512 + SBT], BF16); b_K2c = S.buf("k2c")
    V2c = carve([128, 20, 192], BF16); b_V2c = S.buf("v2c")
    K1c = carve([128, 128 + SBT], BF16); b_K1c = S.buf("k1c")
    V1c = carve([128, 17, 192], BF16); b_V1c = S.buf("v1c")
    QT = [carve([128, 512], BF16) for _ in range(3)]; b_QT = [S.buf("qt%d" % i) for i in range(3)]
    EXPT = [carve([128, 512], F32) for _ in range(2)]; b_EXPT = [S.buf("expt%d" % i) for i in range(2)]
    PT = [carve([128, 512], BF16) for _ in range(2)]; b_PT = [S.buf("pt%d" % i) for i in range(2)]
    SQ = [carve([128, 512], BF16) for _ in range(2)]; b_SQ = [S.buf("sq%d" % i) for i in range(2)]
    NL = [carve([128, 512], F32) for _ in range(2)]; b_NL = [S.buf("nl%d" % i) for i in range(2)]
    SZ = carve([128, SBT], BF16); b_SZ = S.buf("sz")
    RW = carve([128, 512], F32); b_RW = S.buf("rw")
    AUN = carve([128, 512], F32); b_AUN = S.buf("aun")
    KF = carve([128, 512], F32); b_KF = S.buf("kf")
    KO = carve([128, 512], F32); b_KO = S.buf("ko")
    VO = KO; b_VO = b_KO
    att_end = a_off[0]
    a_off[0] = 0
    WBA = carve([128, 4, 1024], BF16); WBB = carve([128, 4, 1024], BF16); WOUT = carve([128, 8, 512], BF16)
    b_WBA = S.buf("wba"); b_WBB = S.buf("wbb"); b_WOUT = S.buf("wout")
    MIX = carve([128, 8, 512], BF16); b_MIX = [S.buf("mix%d" % i) for i in range(8)]
    SG = [carve([128, 512], F32) for _ in range(2)]; b_SG = [S.buf("sg%d" % i) for i in range(2)]
    T1 = [carve([128, 512], F32) for _ in range(2)]; b_T1 = [S.buf("t1%d" % i) for i in range(2)]
    XR = [carve([128, 512], F32) for _ in range(2)]; b_XR = [S.buf("xr%d" % i) for i in range(2)]
    WF = [carve([128, 1024], F32) for _ in range(2)]; b_WF = [S.buf("wf%d" % i) for i in range(2)]

    print("sbuf remaining", nc.sbuf_bytes_remaining, "att_end", att_end, "fin_end", a_off[0])
    PS = [nc.alloc_psum_tensor("ps%d" % i, [128, 512], F32) for i in range(8)]
    b_PS = [S.buf("ps%d" % i, excl=True) for i in range(8)]
    rPJ = Rot([0, 1]); rNB = Rot([2, 3]); rST = Rot([4, 5]); rPJ6 = Rot([0, 1, 4, 5])
    UA, UB = 6, 7
    rWST = Rot([0, 1]); rWBF = Rot([0, 1, 2, 3])
    rEXPT = Rot([0, 1]); rPT = Rot([0, 1]); rSQ = Rot([0, 1]); rNL = Rot([0, 1])

    def fence():
        evs = []
        for e in (PE, ACT, DVE, POOL):
            n = len(S.streams[e])
            for i in range(n - 1, -1, -1):
                r = S.streams[e][i]
                if r["fn"] is not None and r["dma"] is None:
                    evs.append(Ev("e", e, i))
                    break
        evs = evs + list(S.recent_dma)
        S.recent_dma = []
        for e in (PE, ACT, DVE, POOL, SP):
            S.wait_all(e, evs)
    fence.dma_evs = []

    def load_const(dst, src, cols, conv=None):
        ev = S.dma(SP, lambda e: e.dma_start(out=cst[:, 0:cols], in_=src), writes=[b_const], semowner=b_const)
        if conv == "bf":
            S.op(DVE, lambda e: e.tensor_copy(out=dst, in_=cst[:, 0:cols]), reads=[b_const], writes=[b_const])
        else:
            S.op(DVE, lambda e: e.tensor_copy(out=dst, in_=cst[:, 0:cols]), reads=[b_const], writes=[b_const])

    load_const(identF[:, :], ident_d, 128); load_const(identB[:, :], ident_d, 128)
    load_const(bonesB[:, :], bones_d, 128); load_const(swapF[:, :], swap_d, 128)
    load_const(MASK[:, :, :].rearrange("p a b -> p (a b)"), masks_d, 2048)
    load_const(gainS[:, :], gain_d, 8); load_const(qkgS[:, :], qkg_d, 8); load_const(skS[:, :], sk_d, 4)
    load_const(flagS[:, :], flag_d, 1); load_const(dbiasS[:, :], dbias_d, 8)
    S.op(DVE, lambda e: e.memset(epsS[:, :], EPS), writes=[b_const])
    S.op(DVE, lambda e: e.memset(oneS[:, :], 1.0), writes=[b_const])
    S.op(DVE, lambda e: e.memset(onesB[:, :], 1.0), writes=[b_const])
    S.op(ACT, lambda e: e.activation(out=skS[:, :], in_=skS[:, :], func=AF.Exp), reads=[b_const], writes=[b_const])
    C = [b_const]

    bulk = []
    if with_copy:
        def add_copy(src, dst):
            bulk.append((src, dst))
        for n in range(NS):
            for g, L in ((2, 2048), (1, 512), (0, 128)):
                r = 1
                while r < L:
                    nr = min(256, L - r)
                    for (a, bnd) in ((16, None),):
                        pass
                    o = 16 if nr % 16 == 0 else (15 if nr % 15 == 0 else 1)
                    sv = ca_d[g][n, r:r + nr, :].rearrange("(o i) c -> o (i c)", o=o)
                    dv = sa_d[g][n, r - 1:r - 1 + nr, :].rearrange("(o i) c -> o (i c)", o=o)
                    add_copy(sv, dv)
                    r += nr
        sv = cb_d[:, 1:128, :].rearrange("n r c -> n (r c)")
        dv = sb_d[:, 0:127, :].rearrange("n r c -> n (r c)")
        add_copy(sv, dv)

    def issue_bulk(k=1):
        for _ in range(k):
            if bulk:
                sv, dv = bulk.pop(0)
                S.dma(ACT, lambda e, sv=sv, dv=dv: e.dma_start(out=dv, in_=sv))

    conv_eng = [POOL]

    def load_w(ch, dup=None, src=None, scale=1.0):
        issue_bulk(1)
        CE = conv_eng[0]
        si = rWST.next(); bi = rWBF.next()
        S.dma(SP, lambda e: e.dma_start(out=WST[si][:, :, :].rearrange("p a b -> p (a b)"), in_=win_d[ch]),
              writes=[b_WST[si]], semowner=b_WST[si])
        g_bc = gainS[:, :].unsqueeze(2).to_broadcast([128, 8, 128])
        if dup is None:
            S.op(CE, lambda e: e.tensor_tensor(out=WBF[bi][:, :, :], in0=WST[si][:, :, :], in1=g_bc, op=ALU.mult),
                 reads=[b_WST[si]] + C, writes=[b_WBF[bi]])
        else:
            g64 = gainS[:, :].unsqueeze(2).to_broadcast([128, 8, 64])
            for half in (0, 1):
                S.op(CE, lambda e, half=half: e.tensor_tensor(out=WBF[bi][:, :, 64 * half:64 * half + 64],
                                                               in0=WST[si][:, :, 64 * dup:64 * dup + 64], in1=g64, op=ALU.mult),
                     reads=[b_WST[si]] + C, writes=[b_WBF[bi]])
        return WBF[bi], b_WBF[bi]

    def proj_fm(w, bw, bank, ncols, rhs_fn):
        for k in range(8):
            S.op(PE, lambda e, k=k: e.matmul(PS[bank][:, 0:ncols], lhsT=w[:, k, :], rhs=rhs_fn(k), start=(k == 0), stop=(k == 7)),
                 reads=[bw, b_xT], writes=[b_PS[bank]])

    def norm_evac(bank, ncols, gcol, out_ap, out_bufs, f32_out=None):
        si = rSQ.next(); ni = rNL.next(); nb = rNB.next()
        S.op(ACT, lambda e: e.activation(out=SQ[si][:, 0:ncols], in_=PS[bank][:, 0:ncols], func=AF.Square),
             reads=[b_PS[bank]], writes=[b_SQ[si]])
        S.op(PE, lambda e: e.matmul(PS[nb][:, 0:ncols], lhsT=bonesB[:, :], rhs=SQ[si][:, 0:ncols], start=True, stop=True),
             reads=[b_SQ[si]] + C, writes=[b_PS[nb]])
        S.op(ACT, lambda e: e.activation(out=NL[ni][:, 0:ncols], in_=PS[nb][:, 0:ncols], func=AF.Ln, bias=epsS[:, 0:1], scale=1.0),
             reads=[b_PS[nb]] + C, writes=[b_NL[ni]])
        S.op(ACT, lambda e: e.activation(out=NL[ni][:, 0:ncols], in_=NL[ni][:, 0:ncols], func=AF.Exp, scale=-0.5),
             reads=[b_NL[ni]], writes=[b_NL[ni]])
        S.op(DVE, lambda e: e.scalar_tensor_tensor(out=out_ap, in0=PS[bank][:, 0:ncols], scalar=qkgS[:, gcol:gcol + 1],
                                                   in1=NL[ni][:, 0:ncols], op0=ALU.mult, op1=ALU.mult),
             reads=[b_PS[bank], b_NL[ni]] + C, writes=out_bufs)
        if f32_out is not None:
            fo, fb = f32_out
            S.op(DVE, lambda e: e.scalar_tensor_tensor(out=fo, in0=PS[bank][:, 0:ncols], scalar=qkgS[:, gcol:gcol + 1],
                                                       in1=NL[ni][:, 0:ncols], op0=ALU.mult, op1=ALU.mult),
                 reads=[b_PS[bank], b_NL[ni]] + C, writes=fb)

    def xT_win(w):
        return lambda k: xT[:, k, 512 * w:512 * w + 512]

    def tok_ap(k, off, step, n=128):
        return xT[:, k, off:off + step * (n - 1) + 1:step]

    def proj_v_blocks(w, bw, blocks, dst, dbuf, scale_flag=False, vout=None, vcols=128):
        d0 = blocks[0][2]; nb_ = len(blocks)
        assert [b[2] for b in blocks] == list(range(d0, d0 + nb_))
        if scale_flag:
            S.op(POOL, lambda e: e.tensor_copy(out=dst[:, d0:d0 + nb_, 64:128], in_=flagS[:, 0:1].unsqueeze(1).to_broadcast([128, nb_, 64])),
                 reads=C, writes=[dbuf])
        else:
            S.op(POOL, lambda e: e.tensor_copy(out=dst[:, d0:d0 + nb_, 64:128], in_=onesB[:, :].unsqueeze(1).to_broadcast([128, nb_, 64])),
                 reads=C, writes=[dbuf])
        for i0_ in range(0, len(blocks), 4):
            grp = blocks[i0_:i0_ + 4]
            ng = len(grp)
            bank = rPJ6.next()
            for bi, (off, step, di) in enumerate(grp):
                for k in range(8):
                    S.op(PE, lambda e, k=k, bi=bi, off=off, step=step: e.matmul(
                        PS[bank][:, 128 * bi:128 * bi + 128], lhsT=tok_ap(k, off, step), rhs=w[:, k, :],
                        start=(k == 0), stop=(k == 7)), reads=[bw, b_xT], writes=[b_PS[bank]])
            dg = grp[0][2]
            src = PS[bank][:, 0:128 * ng].rearrange("p (b h d) -> p b h d", h=2, d=64)
            dsel = dst[:, dg:dg + ng, :].rearrange("p b (h d) -> p b h d", d=64)[:, :, 0:3:2, :]
            if scale_flag:
                S.op(DVE, lambda e, src=src, dsel=dsel: e.tensor_scalar(out=dsel, in0=src, scalar1=flagS[:, 0:1], scalar2=None, op0=ALU.mult),
                     reads=[b_PS[bank]] + C, writes=[dbuf])
            else:
                S.op(DVE, lambda e, src=src, dsel=dsel: e.tensor_copy(out=dsel, in_=src), reads=[b_PS[bank]], writes=[dbuf])
            if vout is not None:
                S.op(ACT, lambda e: e.activation(out=VO[:, 0:128 * ng], in_=PS[bank][:, 0:128 * ng], func=AF.Copy),
                     reads=[b_PS[bank]], writes=[b_VO])
                for bi in range(ng):
                    dap = vout(i0_ + bi)
                    if dap is None:
                        continue
                    ev = S.dma(SP, lambda e, bi=bi, dap=dap: e.dma_start(out=dap, in_=VO[:, 128 * bi:128 * bi + vcols]),
                               reads=[b_VO], semowner=b_VO)
                    out_evs.append(ev)

    def prologue(src_d, t0):
        rX = Rot([0, 1])
        for blk in range(16):
            i = rX.next()
            S.dma(SP, lambda e, blk=blk: e.dma_start(out=XST[i], in_=src_d[t0 + 128 * blk:t0 + 128 * blk + 128, :]),
                  writes=[b_XST[i]], semowner=b_XST[i])
            S.op(ACT, lambda e: e.activation(out=XB[i], in_=XST[i], func=AF.Square, accum_out=stat[:, 0:1]),
                 reads=[b_XST[i]], writes=[b_XB[i], b_stat])
            S.op(ACT, lambda e: e.activation(out=stat[:, 1:2], in_=stat[:, 0:1], func=AF.Ln, bias=epsS[:, 0:1], scale=1.0 / D),
                 reads=[b_stat] + C, writes=[b_stat])
            S.op(ACT, lambda e: e.activation(out=stat[:, 2:3], in_=stat[:, 1:2], func=AF.Exp, scale=-0.5),
                 reads=[b_stat], writes=[b_stat])
            S.op(DVE, lambda e: e.tensor_scalar(out=XB[i], in0=XST[i], scalar1=stat[:, 2:3], scalar2=None, op0=ALU.mult),
                 reads=[b_XST[i], b_stat], writes=[b_XB[i]])
            bank = rNB.next()
            pbf = PS[bank][:, :].bitcast(BF16)
            for k in range(8):
                S.op(PE, lambda e, k=k: e.transpose(pbf[:, 128 * k:128 * k + 128], XB[i][:, 128 * k:128 * k + 128], identB[:, :]),
                     reads=[b_XB[i]] + C, writes=[b_PS[bank]])
            S.op(DVE, lambda e, blk=blk: e.tensor_copy(out=xT[:, :, 128 * blk:128 * blk + 128],
                                                       in_=pbf.rearrange("p (k t) -> p k t", t=128)),
                 reads=[b_PS[bank]], writes=[b_xT])

    def attn_window(w, hg, groups, started):
        units = [(tile, head) for tile in groups for head in (0, 1)]
        state = []

        def emit_s(tile, head):
            rows = slice(64 * head, 64 * head + 64)
            stb = rST.next()
            nseg = len(tile["segs"]); n = tile["n"]
            for si, sg in enumerate(tile["segs"]):
                S.op(PE, lambda e, si=si, sg=sg, rows=rows: e.matmul(
                    PS[stb][:, n * si:n * si + n], lhsT=sg["k"][rows, :], rhs=sg["q"][rows, :], start=True, stop=True),
                    reads=[sg["kb"], sg["qb"]], writes=[b_PS[stb]])
            ei = rEXPT.next(); pi = rPT.next()
            tot = n * nseg
            S.op(ACT, lambda e, tot=tot: e.activation(out=EXPT[ei][:, 0:tot], in_=PS[stb][:, 0:tot], func=AF.Exp, scale=0.125),
                 reads=[b_PS[stb]], writes=[b_EXPT[ei]])
            m = tile["mask"](hg[head])
            nq = nseg // 2
            S.op(DVE, lambda e, tot=tot, m=m, nq=nq, n=n: e.tensor_tensor(
                out=PT[pi][:, 0:tot].rearrange("p (a b c) -> p a b c", b=2, c=n),
                in0=EXPT[ei][:, 0:tot].rearrange("p (a b c) -> p a b c", b=2, c=n),
                in1=m.unsqueeze(1).to_broadcast([128, nq, 2, n]), op=ALU.mult),
                reads=[b_EXPT[ei]] + C, writes=[b_PT[pi]])
            return pi

        def emit_pv(tile, head, pi):
            n = tile["n"]
            ub = UA if head == 0 else UB
            for si, sg in enumerate(tile["segs"]):
                vb = sg["v"]
                lhs = vb[:, 0:128] if head == 0 else vb[:, 64:192]
                first = not started[head]
                started[head] = True
                S.op(PE, lambda e, si=si, sg=sg, lhs=lhs, first=first, ub=ub: e.matmul(
                    sg["o"](PS[ub]), lhsT=lhs, rhs=PT[pi][:, n * si:n * si + n], start=first, stop=False,
                    skip_group_check=True), reads=[sg["vb"], b_PT[pi]], writes=[b_PS[ub]])

        prev = None
        for (tile, head) in units:
            pi = emit_s(tile, head)
            if prev is not None:
                emit_pv(*prev)
            prev = (tile, head, pi)
        emit_pv(*prev)

    def mask_full(h):
        return MASK[:, h, :].rearrange("p (b c) -> p b c", c=128)

    def mask_win(w):
        return lambda h: MASK[:, h, :].rearrange("p (b c) -> p b c", c=128)[:, :, 32 * w:32 * w + 32]

    def segs_g1(w, Kc, bK, Vc, bV, q, bq, qoff=0):
        tiles = []
        for half in (0, 1):
            segs = []
            for i in (0, 1):
                qb = 4 * w + 2 * half + i
                qap = q[:, 128 * qb - qoff:128 * qb - qoff + 128]
                oc = 128 * (2 * half + i)
                o = (lambda oc: (lambda ps: ps[:, oc:oc + 128]))(oc)
                segs.append(dict(k=Kc[:, 128 + 128 * qb:256 + 128 * qb], kb=bK, q=qap, qb=bq, v=Vc[:, qb + 1, :], vb=bV, o=o))
                segs.append(dict(k=Kc[:, 128 * qb:128 * qb + 128], kb=bK, q=qap, qb=bq, v=Vc[:, qb, :], vb=bV, o=o))
            tiles.append(dict(segs=segs, n=128, mask=mask_full))
        return tiles

    def strided(ap2d, off, step, n):
        return ap2d[:, off:off + step * (n - 1) + 1:step]

    def segs_g2(w, Kc, bK, Vc, bV, q, bq, qoff=0):
        tiles = []
        for half in (0, 1):
            segs = []
            for i in (0, 1):
                r = 2 * half + i
                qap = strided(q, 512 * w + r - qoff, 4, 128)
                o = (lambda r: (lambda ps: strided(ps, r, 4, 128)))(r)
                segs.append(dict(k=strided(Kc, 512 + 512 * w + r, 4, 128), kb=bK, q=qap, qb=bq, v=Vc[:, 4 + 4 * w + r, :], vb=bV, o=o))
                segs.append(dict(k=strided(Kc, 512 * w + r, 4, 128), kb=bK, q=qap, qb=bq, v=Vc[:, 4 * w + r, :], vb=bV, o=o))
            tiles.append(dict(segs=segs, n=128, mask=mask_full))
        return tiles

    def segs_g3(w, Kcur, bKc, Kprev, bKp, Vcur, bVc, Vprev, bVp, q, bq, qoff=0):
        tiles = []
        for half in (0, 1):
            segs = []
            for i in range(8):
                r = 8 * half + i
                qap = strided(q, 512 * w + r - qoff, 16, 32)
                o = (lambda r: (lambda ps: strided(ps, r, 16, 32)))(r)
                segs.append(dict(k=strided(Kcur, r, 16, 128), kb=bKc, q=qap, qb=bq, v=Vcur[:, r, :], vb=bVc, o=o))
                segs.append(dict(k=strided(Kprev, r, 16, 128), kb=bKp, q=qap, qb=bq, v=Vprev[:, r, :], vb=bVp, o=o))
            tiles.append(dict(segs=segs, n=32, mask=mask_win(w)))
        return tiles

    def finish_window(w, c, dstT, b_dst, sink_col=None):
        if sink_col is None:
            S.op(DVE, lambda e: e.reciprocal(out=RW[0:64, :], in_=PS[UB][0:64, :]), reads=[b_PS[UB]], writes=[b_RW])
            S.op(DVE, lambda e: e.reciprocal(out=RW[64:128, :], in_=PS[UA][64:128, :]), reads=[b_PS[UA]], writes=[b_RW])
        else:
            S.op(DVE, lambda e: e.tensor_copy(out=RW[0:64, :], in_=PS[UB][0:64, :]), reads=[b_PS[UB]], writes=[b_RW])
            S.op(DVE, lambda e: e.tensor_copy(out=RW[64:128, :], in_=PS[UA][64:128, :]), reads=[b_PS[UA]], writes=[b_RW])
        nb = rNB.next()
        S.op(PE, lambda e: e.matmul(PS[nb][:, :], lhsT=swapF[:, :], rhs=RW[:, :], start=True, stop=True),
             reads=[b_RW] + C, writes=[b_PS[nb]])
        S.op(DVE, lambda e: e.tensor_tensor(out=AUN[0:64, :], in0=PS[UA][0:64, :], in1=SZ[0:64, 512 * w:512 * w + 512], op=ALU.mult),
             reads=[b_PS[UA], b_SZ], writes=[b_AUN])
        S.op(DVE, lambda e: e.tensor_tensor(out=AUN[64:128, :], in0=PS[UB][64:128, :], in1=SZ[64:128, 512 * w:512 * w + 512], op=ALU.mult),
             reads=[b_PS[UB], b_SZ], writes=[b_AUN])
        if sink_col is None:
            S.op(DVE, lambda e: e.tensor_tensor(out=dstT[:, c, 512 * w:512 * w + 512], in0=AUN[:, :], in1=PS[nb][:, :], op=ALU.mult),
                 reads=[b_AUN, b_PS[nb]], writes=[b_dst[c]])
        else:
            S.op(DVE, lambda e: e.tensor_scalar(out=RW[:, :], in0=PS[nb][:, :], scalar1=skS[:, sink_col:sink_col + 1], scalar2=None, op0=ALU.add),
                 reads=[b_PS[nb]] + C, writes=[b_RW])
            S.op(DVE, lambda e: e.reciprocal(out=RW[:, :], in_=RW[:, :]), reads=[b_RW], writes=[b_RW])
            S.op(DVE, lambda e: e.tensor_tensor(out=dstT[:, c, 512 * w:512 * w + 512], in0=AUN[:, :], in1=RW[:, :], op=ALU.mult),
                 reads=[b_AUN, b_RW], writes=[b_dst[c]])

    def silu_chunk(ch):
        wz, bwz = load_w(ch)
        for w in range(4):
            bank = rPJ6.next()
            proj_fm(wz, bwz, bank, 512, xT_win(w))
            ni = rNL.next()
            S.op(ACT, lambda e: e.activation(out=NL[ni][:, :], in_=PS[bank][:, :], func=AF.Exp, scale=-1.0),
                 reads=[b_PS[bank]], writes=[b_NL[ni]])
            S.op(ACT, lambda e: e.activation(out=NL[ni][:, :], in_=NL[ni][:, :], func=AF.Ln, bias=oneS[:, 0:1], scale=1.0),
                 reads=[b_NL[ni]] + C, writes=[b_NL[ni]])
            S.op(ACT, lambda e: e.activation(out=NL[ni][:, :], in_=NL[ni][:, :], func=AF.Exp, scale=-1.0),
                 reads=[b_NL[ni]], writes=[b_NL[ni]])
            S.op(DVE, lambda e, w=w: e.tensor_tensor(out=SZ[:, 512 * w:512 * w + 512], in0=PS[bank][:, :], in1=NL[ni][:, :], op=ALU.mult),
                 reads=[b_PS[bank], b_NL[ni]], writes=[b_SZ])

    def k_out_rows(dst_d, tok_base_in_dst, c128, nblk, col0):
        nb = rNB.next()
        for b in range(nblk):
            S.op(PE, lambda e, b=b: e.transpose(PS[nb][:, 128 * b:128 * b + 128], KF[:, 128 * b:128 * b + 128], identF[:, :]),
                 reads=[b_KF] + C, writes=[b_PS[nb]])
        S.op(ACT, lambda e: e.activation(out=KO[:, 0:128 * nblk], in_=PS[nb][:, 0:128 * nblk], func=AF.Copy),
             reads=[b_PS[nb]], writes=[b_KO])
        dv = dst_d[tok_base_in_dst:tok_base_in_dst + 128 * nblk, col0:col0 + c128].rearrange("(b p) c -> p b c", p=128)
        ev = S.dma(SP, lambda e: e.dma_start(out=dv, in_=KO[:, 0:128 * nblk].rearrange("p (b c) -> p b c", c=128)[:, :, 0:c128]),
                   reads=[b_KO], semowner=b_KO)
        out_evs.append(ev)


    def load_fin_w(dst, bdst, src_d, nk, ncol, col0, width, scale):
        rW = Rot([0, 1])
        sv = src_d.rearrange("p (k c) -> p k c", c=ncol)
        for k in range(nk):
            i = rW.next()
            S.dma(SP, lambda e, k=k, i=i: e.dma_start(out=WF[i][:, 0:width], in_=sv[:, k, col0:col0 + width]),
                  writes=[b_WF[i]], semowner=b_WF[i])
            S.op(DVE if (k % 2 == 0) else ACT, (lambda e, k=k, i=i: e.tensor_copy(out=dst[:, k, 0:width], in_=WF[i][:, 0:width])) if (k % 2 == 0) else
                 (lambda e, k=k, i=i: e.activation(out=dst[:, k, 0:width], in_=WF[i][:, 0:width], func=AF.Copy)),
                 reads=[b_WF[i]], writes=[bdst])

    def sigmoid_from(bank, dst, bdst):
        S.op(ACT, lambda e: e.activation(out=dst, in_=PS[bank][:, :], func=AF.Exp, scale=-1.0), reads=[b_PS[bank]], writes=[bdst])
        S.op(ACT, lambda e: e.activation(out=dst, in_=dst, func=AF.Ln, bias=oneS[:, 0:1], scale=1.0), reads=[bdst] + C, writes=[bdst])
        S.op(ACT, lambda e: e.activation(out=dst, in_=dst, func=AF.Exp, scale=-1.0), reads=[bdst], writes=[bdst])

    def final_stage(tok0, ntok_total, x_src, y_dst, xTsrc, nwin, wcols):
        conv_eng[0] = DVE
        load_fin_w(WBA, b_WBA, wba_d, 4, 1024, 0, 1024, 1.0)
        load_fin_w(WBB, b_WBB, wbb_d, 4, 1024, 0, 1024, 1.0)
        for w in range(nwin):
            n = wcols
            cs = slice(wcols * w, wcols * w + n)
            for j in range(8):
                wga, bwga = load_w(CH_GA(j))
                wgb, bwgb = load_w(CH_GB(j))
                bka = rPJ.next()
                proj_fm(wga, bwga, bka, n, lambda k: xTsrc[:, k, cs])
                sigmoid_from_n(bka, SG[0][:, 0:n], b_SG[0], n)
                bkb = rPJ.next()
                proj_fm(wgb, bwgb, bkb, n, lambda k: xTsrc[:, k, cs])
                sigmoid_from_n(bkb, SG[1][:, 0:n], b_SG[1], n)
                ba = rST.next()
                for cc in range(4):
                    S.op(PE, lambda e, cc=cc, j=j: e.matmul(PS[ba][:, 0:n], lhsT=WBA[:, cc, 128 * j:128 * j + 128], rhs=aT[:, cc, cs],
                                                          start=(cc == 0), stop=(cc == 3)), reads=[b_WBA] + b_aT, writes=[b_PS[ba]])
                bb = rST.next()
                for cc in range(4):
                    S.op(PE, lambda e, cc=cc, j=j: e.matmul(PS[bb][:, 0:n], lhsT=WBB[:, cc, 128 * j:128 * j + 128], rhs=bT[:, cc, cs],
                                                          start=(cc == 0), stop=(cc == 3)), reads=[b_WBB] + b_bT, writes=[b_PS[bb]])
                S.op(DVE, lambda e: e.tensor_tensor(out=T1[0][:, 0:n], in0=PS[ba][:, 0:n], in1=SG[0][:, 0:n], op=ALU.mult),
                     reads=[b_PS[ba], b_SG[0]], writes=[b_T1[0]])
                S.op(DVE, lambda e: e.tensor_tensor(out=T1[1][:, 0:n], in0=PS[bb][:, 0:n], in1=SG[1][:, 0:n], op=ALU.mult),
                     reads=[b_PS[bb], b_SG[1]], writes=[b_T1[1]])
                S.op(DVE, lambda e, j=j: e.tensor_tensor(out=MIX[:, j, 0:n], in0=T1[0][:, 0:n], in1=T1[1][:, 0:n], op=ALU.add),
                     reads=[b_T1[0], b_T1[1]], writes=[b_MIX[j]])
            for half in range(2):
                load_fin_w(WOUT, b_WOUT, wout_d, 8, 1024, 512 * half, 512, 1.0)
                nblk = (n + 127) // 128
                for b in range(nblk):
                    nt = min(128, n - 128 * b)
                    yb = UA if (b % 2 == 0) else UB
                    for k in range(8):
                        S.op(PE, lambda e, k=k, b=b, nt=nt, yb=yb: e.matmul(PS[yb][0:nt, :], lhsT=MIX[:, k, 128 * b:128 * b + nt], rhs=WOUT[:, k, :],
                                                                 start=(k == 0), stop=(k == 7)), reads=b_MIX + [b_WOUT], writes=[b_PS[yb]])
                    xi = (b % 2)
                    r0 = tok0 + wcols * w + 128 * b
                    S.dma(SP, lambda e, r0=r0, nt=nt, xi=xi, half=half: e.dma_start(out=XR[xi][0:nt, :], in_=x_src[r0:r0 + nt, 512 * half:512 * half + 512]),
                          writes=[b_XR[xi]], semowner=b_XR[xi])
                    S.op(DVE, lambda e, nt=nt, xi=xi, yb=yb: e.tensor_tensor(out=XR[xi][0:nt, :], in0=XR[xi][0:nt, :], in1=PS[yb][0:nt, :], op=ALU.add),
                         reads=[b_XR[xi], b_PS[yb]], writes=[b_XR[xi]])
                    ev = S.dma(SP, lambda e, r0=r0, nt=nt, xi=xi, half=half: e.dma_start(out=y_dst[r0:r0 + nt, 512 * half:512 * half + 512], in_=XR[xi][0:nt, :]),
                               reads=[b_XR[xi]], semowner=b_XR[xi])
                    out_evs.append(ev)

    def sigmoid_from_n(bank, dst, bdst, n):
        S.op(ACT, lambda e: e.activation(out=dst, in_=PS[bank][:, 0:n], func=AF.Exp, scale=-1.0), reads=[b_PS[bank]], writes=[bdst])
        S.op(ACT, lambda e: e.activation(out=dst, in_=dst, func=AF.Ln, bias=oneS[:, 0:1], scale=1.0), reads=[bdst] + C, writes=[bdst])
        S.op(ACT, lambda e: e.activation(out=dst, in_=dst, func=AF.Exp, scale=-1.0), reads=[bdst], writes=[bdst])

    import os as _os
    STOP = int(_os.environ.get("MK_STOP", "0"))

    def finish():
        while bulk:
            issue_bulk(1)
        evs = list(out_evs) + list(S.recent_dma)
        if S.bulk_cnt:
            evs.append(Ev("d", S.bulk_sem, S.bulk_cnt))
        for e in (PE, ACT, DVE, POOL):
            for i in range(len(S.streams[e]) - 1, -1, -1):
                r = S.streams[e][i]
                if r["fn"] is not None and r["dma"] is None:
                    evs.append(Ev("e", e, i))
                    break
        S.wait_all(SP, evs)
        S.emit()
        st.close()
        return nc

    slot_prev = [0, 1, 2, 3]
    spare = [4]

    prologue(xh_d, 0)
    fence()
    if STOP == 1:
        return finish()
    for c in range(4):
        wk, bwk = load_w(CH_K(3, c))
        sl = slot_prev[c]
        for w in range(4):
            bank = rPJ6.next()
            proj_fm(wk, bwk, bank, 512, xT_win(w))
            norm_evac(bank, 512, 5, K3[sl][:, 512 * w:512 * w + 512], [b_K3[sl]])
        wv, bwv = load_w(CH_V(3, c))
        proj_v_blocks(wv, bwv, [(r, 16, r) for r in range(16)], V3[sl], b_V3[sl], scale_flag=True)
        wk, bwk = load_w(CH_K(2, c))
        bank = rPJ6.next()
        proj_fm(wk, bwk, bank, 512, xT_win(3))
        norm_evac(bank, 512, 3, K2t[c][:, :], [b_K2t[c]])
        wv, bwv = load_w(CH_V(2, c))
        proj_v_blocks(wv, bwv, [(1536 + r, 4, r) for r in range(4)], V2t[c], b_V2t[c], scale_flag=True)
        wk, bwk = load_w(CH_K(1, c))
        bank = rPJ6.next()
        proj_fm(wk, bwk, bank, 128, lambda k: xT[:, k, SBT - 128:SBT])
        norm_evac(bank, 128, 1, K1t[c][:, :], [b_K1t[c]])
        wv, bwv = load_w(CH_V(1, c))
        proj_v_blocks(wv, bwv, [(SBT - 128, 1, 0)], V1t[c], b_V1t[c], scale_flag=True)
    for kvh in range(2):
        wk, bwk = load_w(CH_BK, dup=kvh)
        bank = rPJ6.next()
        proj_fm(wk, bwk, bank, 128, lambda k: xT[:, k, SBT - 128:SBT])
        norm_evac(bank, 128, 7, KBt[kvh][:, :], [b_KBt[kvh]])
        wv, bwv = load_w(CH_BV, dup=kvh)
        proj_v_blocks(wv, bwv, [(SBT - 128, 1, 0)], VBt[kvh], b_VBt[kvh], scale_flag=True)
    fence()
    if STOP == 2:
        return finish()

    def k_tail_out(dst_ap64or128, ncols):
        nb = rNB.next()
        S.op(PE, lambda e: e.transpose(PS[nb][:, 0:128], KF[:, 384:512], identF[:, :]), reads=[b_KF] + C, writes=[b_PS[nb]])
        S.op(ACT, lambda e: e.activation(out=KO[:, 0:128], in_=PS[nb][:, 0:128], func=AF.Copy), reads=[b_PS[nb]], writes=[b_KO])
        ev = S.dma(SP, lambda e: e.dma_start(out=dst_ap64or128, in_=KO[:, 0:ncols]), reads=[b_KO], semowner=b_KO)
        out_evs.append(ev)

    for s in range(NSB):
        last = (s == NSB - 1)
        conv_eng[0] = POOL
        prologue(x_d, s * SBT)
        fence()
        for kvh in range(2):
            S.op(POOL, lambda e: e.tensor_copy(out=K1c[:, 0:128], in_=KBt[kvh][:, :]), reads=[b_KBt[kvh]], writes=[b_K1c])
            S.op(POOL, lambda e: e.tensor_copy(out=V1c[:, 0:1, :], in_=VBt[kvh][:, :, :]), reads=[b_VBt[kvh]], writes=[b_V1c])
            wk, bwk = load_w(CH_BK, dup=kvh)
            for w in range(4):
                bank = rPJ6.next()
                proj_fm(wk, bwk, bank, 512, xT_win(w))
                f32o = (KF[:, :], [b_KF]) if (last and w == 3) else None
                norm_evac(bank, 512, 7, K1c[:, 128 + 512 * w:128 + 512 * w + 512], [b_K1c], f32_out=f32o)
                if last and w == 3:
                    k_tail_out(pb_d[:, 64 * kvh:64 * kvh + 64], 64)
            wv, bwv = load_w(CH_BV, dup=kvh)
            vob = (lambda bi, kvh=kvh: (pb_d[:, 128 + 64 * kvh:128 + 64 * kvh + 64] if bi == 15 else None)) if last else None
            proj_v_blocks(wv, bwv, [(128 * b, 1, b + 1) for b in range(16)], V1c, b_V1c, vout=vob, vcols=64)
            S.op(POOL, lambda e: e.tensor_copy(out=KBt[kvh][:, :], in_=K1c[:, SBT:SBT + 128]), reads=[b_K1c], writes=[b_KBt[kvh]])
            S.op(POOL, lambda e: e.tensor_copy(out=VBt[kvh][:, :, :], in_=V1c[:, 16:17, :]), reads=[b_V1c], writes=[b_VBt[kvh]])
            for c in (2 * kvh, 2 * kvh + 1):
                silu_chunk(CH_ZB(c))
                wq, bwq = load_w(CH_BQ(c))
                for w in range(4):
                    bank = rPJ.next()
                    proj_fm(wq, bwq, bank, 512, xT_win(w))
                    norm_evac(bank, 512, 6, QT[0][:, :], [b_QT[0]])
                    started = [False, False]
                    attn_window(w, (2 * c, 2 * c + 1), segs_g1(w, K1c, b_K1c, V1c, b_V1c, QT[0], b_QT[0], qoff=512 * w), started)
                    finish_window(w, c, bT, b_bT, sink_col=c)
        if STOP == 3:
            return finish()
        for c in range(4):
            silu_chunk(CH_ZA(c))
            S.op(POOL, lambda e: e.tensor_copy(out=K1c[:, 0:128], in_=K1t[c][:, :]), reads=[b_K1t[c]], writes=[b_K1c])
            S.op(POOL, lambda e: e.tensor_copy(out=V1c[:, 0:1, :], in_=V1t[c][:, :, :]), reads=[b_V1t[c]], writes=[b_V1c])
            S.op(POOL, lambda e: e.tensor_copy(out=K2c[:, 0:512], in_=K2t[c][:, :]), reads=[b_K2t[c]], writes=[b_K2c])
            S.op(POOL, lambda e: e.tensor_copy(out=V2c[:, 0:4, :], in_=V2t[c][:, :, :]), reads=[b_V2t[c]], writes=[b_V2c])
            slc = spare[0]; slp = slot_prev[c]
            plan = [(1, K1c, b_K1c, 128, 1), (2, K2c, b_K2c, 512, 3), (3, K3[slc], b_K3[slc], 0, 5)]
            for (g, Kd, bKd, koff, gcol) in plan:
                wk, bwk = load_w(CH_K(g, c))
                for w in range(4):
                    bank = rPJ6.next()
                    proj_fm(wk, bwk, bank, 512, xT_win(w))
                    need_out = last and (g == 3 or w == 3)
                    f32o = (KF[:, :], [b_KF]) if need_out else None
                    norm_evac(bank, 512, gcol, Kd[:, koff + 512 * w:koff + 512 * w + 512], [bKd], f32_out=f32o)
                    if need_out:
                        if g == 3:
                            k_out_rows(pa_d[2], 512 * w, 128, 4, 128 * c)
                        elif g == 2:
                            k_out_rows(pa_d[1], 0, 128, 4, 128 * c)
                        else:
                            k_tail_out(pa_d[0][:, 128 * c:128 * c + 128], 128)
            wv, bwv = load_w(CH_V(1, c))
            vo1 = (lambda bi, c=c: (pa_d[0][:, 512 + 128 * c:512 + 128 * c + 128] if bi == 15 else None)) if last else None
            proj_v_blocks(wv, bwv, [(128 * b, 1, b + 1) for b in range(16)], V1c, b_V1c, vout=vo1)
            wv, bwv = load_w(CH_V(2, c))

            def vo2f(bi, c=c):
                j, r = bi // 4, bi % 4
                if j != 3:
                    return None
                return pa_d[1][:, 512 + 128 * c:512 + 128 * c + 128].rearrange("(i r) c -> r i c", r=4)[r]
            proj_v_blocks(wv, bwv, [(512 * j + r, 4, 4 + 4 * j + r) for j in range(4) for r in range(4)], V2c, b_V2c,
                          vout=(vo2f if last else None))
            wv, bwv = load_w(CH_V(3, c))

            def vo3f(bi, c=c):
                return pa_d[2][:, 512 + 128 * c:512 + 128 * c + 128].rearrange("(i r) c -> r i c", r=16)[bi]
            proj_v_blocks(wv, bwv, [(r, 16, r) for r in range(16)], V3[slc], b_V3[slc], vout=(vo3f if last else None))
            wq = [load_w(CH_Q(g, c)) for g in (1, 2, 3)]
            for w in range(4):
                started = [False, False]
                hg = (2 * c, 2 * c + 1)
                for gi in range(3):
                    bank = rPJ.next()
                    proj_fm(wq[gi][0], wq[gi][1], bank, 512, xT_win(w))
                    norm_evac(bank, 512, 2 * gi, QT[gi][:, :], [b_QT[gi]])
                attn_window(w, hg, segs_g1(w, K1c, b_K1c, V1c, b_V1c, QT[0], b_QT[0], qoff=512 * w)
                            + segs_g2(w, K2c, b_K2c, V2c, b_V2c, QT[1], b_QT[1], qoff=512 * w)
                            + segs_g3(w, K3[slc], b_K3[slc], K3[slp], b_K3[slp], V3[slc], b_V3[slc], V3[slp], b_V3[slp],
                                      QT[2], b_QT[2], qoff=512 * w), started)
                finish_window(w, c, aT, b_aT)
            S.op(POOL, lambda e: e.tensor_copy(out=K1t[c][:, :], in_=K1c[:, SBT:SBT + 128]), reads=[b_K1c], writes=[b_K1t[c]])
            S.op(POOL, lambda e: e.tensor_copy(out=V1t[c][:, :, :], in_=V1c[:, 16:17, :]), reads=[b_V1c], writes=[b_V1t[c]])
            S.op(POOL, lambda e: e.tensor_copy(out=K2t[c][:, :], in_=K2c[:, SBT:SBT + 512]), reads=[b_K2c], writes=[b_K2t[c]])
            S.op(POOL, lambda e: e.tensor_copy(out=V2t[c][:, :, :], in_=V2c[:, 16:20, :]), reads=[b_V2c], writes=[b_V2t[c]])
            spare[0] = slp
            slot_prev[c] = slc
        fence()
        if STOP == 4:
            return finish()
        final_stage(s * SBT, SBT, x_d, y_d, xT, 4, 512)
        fence()
        if STOP == 5:
            return finish()


    if with_sample:
        conv_eng[0] = DVE
        a_off[0] = 0
        CT = [carve([128, 1024], F32) for _ in range(2)]; b_CT = [S.buf("ct%d" % i) for i in range(2)]
        TMP = [carve([128, 512], F32) for _ in range(2)]; b_TMP = [S.buf("tmp%d" % i) for i in range(2)]
        PBC = carve([128, 512], F32); b_PBC = S.buf("pbc")
        SC = carve([128, 16], F32); b_SC = S.buf("sc")
        QS = carve([128, 16, NS], F32); b_QS = S.buf("qs")
        KS = carve([128, 14, NS], F32); b_KS = S.buf("ks")
        VS = carve([128, 14, NS], F32); b_VS = S.buf("vs")
        ZS = carve([128, 8, NS], F32); b_ZS = S.buf("zs")
        P0 = carve([128, 16, NS], F32); b_P0 = S.buf("p0")
        UT = carve([128, 8, NS], F32); b_UT = S.buf("ut")
        ZT = carve([128, 8, NS], F32); b_ZT = S.buf("zt")
        NR = carve([NS, 1024], F32); b_NR = S.buf("nr")
        NRB = carve([NS, 256], F32); b_NRB = S.buf("nrb")
        XSS = carve([NS, 1024], F32); b_XSS = S.buf("xss")
        XSB = carve([NS, 1024], BF16); b_XSB = S.buf("xsb")
        onesF = carve([128, 1], F32)
        bonesF = carve([128, 128], F32)
        assert a_off[0] <= ARENA_BYTES
        S.op(DVE, lambda e: e.memset(onesF, 1.0), writes=[b_SC])
        S.dma(SP, lambda e: e.dma_start(out=bonesF, in_=bones_d), writes=[b_PBC], semowner=b_PBC)
        S.dma(SP, lambda e: e.dma_start(out=XSS, in_=xs_d), writes=[b_XSS], semowner=b_XSS)
        S.op(ACT, lambda e: e.activation(out=XSB, in_=XSS, func=AF.Square, accum_out=stat[0:NS, 0:1]), reads=[b_XSS], writes=[b_XSB, b_stat])
        S.op(ACT, lambda e: e.activation(out=stat[0:NS, 1:2], in_=stat[0:NS, 0:1], func=AF.Ln, bias=epsS[0:NS, 0:1], scale=1.0 / D), reads=[b_stat] + C, writes=[b_stat])
        S.op(ACT, lambda e: e.activation(out=stat[0:NS, 2:3], in_=stat[0:NS, 1:2], func=AF.Exp, scale=-0.5), reads=[b_stat], writes=[b_stat])
        S.op(DVE, lambda e: e.tensor_scalar(out=XSB, in0=XSS, scalar1=stat[0:NS, 2:3], scalar2=None, op0=ALU.mult), reads=[b_XSS, b_stat], writes=[b_XSB])
        bank = rNB.next()
        pbf = PS[bank][:, :].bitcast(BF16)
        for k in range(8):
            S.op(PE, lambda e, k=k: e.transpose(pbf[:, NS * k:NS * k + NS], XSB[:, 128 * k:128 * k + 128], identB[0:NS, 0:NS]),
                 reads=[b_XSB] + C, writes=[b_PS[bank]])
        S.op(DVE, lambda e: e.tensor_copy(out=xT[:, :, 0:NS], in_=pbf[:, 0:8 * NS].rearrange("p (k t) -> p k t", t=NS)),
             reads=[b_PS[bank]], writes=[b_xT])
        xs_rhs = lambda k: xT[:, k, 0:NS]

        def proj_s(ch, dup=None):
            wq_, bw_ = load_w(ch, dup=dup)
            bank = rPJ.next()
            proj_fm(wq_, bw_, bank, NS, xs_rhs)
            return bank

        def norm_s(bank, gcol, dst, bdst):
            ti = 0
            S.op(ACT, lambda e: e.activation(out=TMP[ti][:, 0:NS], in_=PS[bank][:, 0:NS], func=AF.Square), reads=[b_PS[bank]], writes=[b_TMP[ti]])
            nb = rNB.next()
            S.op(PE, lambda e: e.matmul(PS[nb][:, 0:NS], lhsT=bonesF, rhs=TMP[ti][:, 0:NS], start=True, stop=True), reads=[b_TMP[ti], b_PBC], writes=[b_PS[nb]])
            S.op(ACT, lambda e: e.activation(out=TMP[ti][:, 0:NS], in_=PS[nb][:, 0:NS], func=AF.Ln, bias=epsS[:, 0:1], scale=1.0), reads=[b_PS[nb]] + C, writes=[b_TMP[ti]])
            S.op(ACT, lambda e: e.activation(out=TMP[ti][:, 0:NS], in_=TMP[ti][:, 0:NS], func=AF.Exp, scale=-0.5), reads=[b_TMP[ti]], writes=[b_TMP[ti]])
            S.op(DVE, lambda e: e.scalar_tensor_tensor(out=dst, in0=PS[bank][:, 0:NS], scalar=qkgS[:, gcol:gcol + 1], in1=TMP[ti][:, 0:NS], op0=ALU.mult, op1=ALU.mult),
                 reads=[b_PS[bank], b_TMP[ti]] + C, writes=[bdst])

        for g in (1, 2, 3):
            for c in range(4):
                norm_s(proj_s(CH_Q(g, c)), 2 * (g - 1), QS[:, 4 * (g - 1) + c, :], b_QS)
                norm_s(proj_s(CH_K(g, c)), 2 * (g - 1) + 1, KS[:, 4 * (g - 1) + c, :], b_KS)
                bank = proj_s(CH_V(g, c))
                S.op(DVE, lambda e, g=g, c=c, bank=bank: e.tensor_copy(out=VS[:, 4 * (g - 1) + c, :], in_=PS[bank][:, 0:NS]), reads=[b_PS[bank]], writes=[b_VS])
        for c in range(4):
            norm_s(proj_s(CH_BQ(c)), 6, QS[:, 12 + c, :], b_QS)
        for kvh in range(2):
            norm_s(proj_s(CH_BK, dup=kvh), 7, KS[:, 12 + kvh, :], b_KS)
            bank = proj_s(CH_BV, dup=kvh)
            S.op(DVE, lambda e, kvh=kvh, bank=bank: e.tensor_copy(out=VS[:, 12 + kvh, :], in_=PS[bank][:, 0:NS]), reads=[b_PS[bank]], writes=[b_VS])
        for i, ch in enumerate([CH_ZA(c) for c in range(4)] + [CH_ZB(c) for c in range(4)]):
            bank = proj_s(ch)
            ti = 1
            S.op(ACT, lambda e, bank=bank: e.activation(out=TMP[ti][:, 0:NS], in_=PS[bank][:, 0:NS], func=AF.Exp, scale=-1.0), reads=[b_PS[bank]], writes=[b_TMP[ti]])
            S.op(ACT, lambda e: e.activation(out=TMP[ti][:, 0:NS], in_=TMP[ti][:, 0:NS], func=AF.Ln, bias=oneS[:, 0:1], scale=1.0), reads=[b_TMP[ti]] + C, writes=[b_TMP[ti]])
            S.op(ACT, lambda e: e.activation(out=TMP[ti][:, 0:NS], in_=TMP[ti][:, 0:NS], func=AF.Exp, scale=-1.0), reads=[b_TMP[ti]], writes=[b_TMP[ti]])
            S.op(DVE, lambda e, i=i, bank=bank: e.tensor_tensor(out=ZS[:, i, :], in0=PS[bank][:, 0:NS], in1=TMP[ti][:, 0:NS], op=ALU.mult),
                 reads=[b_PS[bank], b_TMP[ti]], writes=[b_ZS])

        for g in range(3):
            nb = rNB.next()
            for c in range(4):
                S.op(PE, lambda e, g=g, c=c: e.transpose(PS[nb][0:NS, 128 * c:128 * c + 128], KS[:, 4 * g + c, :], identF[:, :]), reads=[b_KS] + C, writes=[b_PS[nb]])
            S.op(DVE, lambda e: e.tensor_copy(out=NR[:, 0:512], in_=PS[nb][0:NS, :]), reads=[b_PS[nb]], writes=[b_NR])
            nb2 = rNB.next()
            for c in range(4):
                S.op(PE, lambda e, g=g, c=c: e.transpose(PS[nb2][0:NS, 128 * c:128 * c + 128], VS[:, 4 * g + c, :], identF[:, :]), reads=[b_VS] + C, writes=[b_PS[nb2]])
            S.op(DVE, lambda e: e.tensor_copy(out=NR[:, 512:1024], in_=PS[nb2][0:NS, :]), reads=[b_PS[nb2]], writes=[b_NR])
            L = (128, 512, 2048)[g]
            ev = S.dma(SP, lambda e, g=g, L=L: e.dma_start(out=sa_d[g][:, L - 1, :], in_=NR[:, :]), reads=[b_NR], semowner=b_NR)
            out_evs.append(ev)
        nb = rNB.next()
        for kvh in range(2):
            S.op(PE, lambda e, kvh=kvh: e.transpose(PS[nb][0:NS, 128 * kvh:128 * kvh + 128], KS[:, 12 + kvh, :], identF[:, :]), reads=[b_KS] + C, writes=[b_PS[nb]])
            S.op(PE, lambda e, kvh=kvh: e.transpose(PS[nb][0:NS, 256 + 128 * kvh:256 + 128 * kvh + 128], VS[:, 12 + kvh, :], identF[:, :]), reads=[b_VS] + C, writes=[b_PS[nb]])
        S.op(DVE, lambda e: e.tensor_copy(out=NRB[:, :].rearrange("p (a b) -> p a b", b=64),
                                          in_=PS[nb][0:NS, :].rearrange("p (a b) -> p a b", b=128)[:, :, 0:64]), reads=[b_PS[nb]], writes=[b_NRB])
        ev = S.dma(SP, lambda e: e.dma_start(out=sb_d[:, 127, :], in_=NRB[:, :]), reads=[b_NRB], semowner=b_NRB)
        out_evs.append(ev)

        for qc in range(16):
            kc = qc if qc < 12 else 12 + (qc - 12) // 2
            S.op(DVE, lambda e, qc=qc, kc=kc: e.tensor_tensor(out=TMP[0][:, 0:NS], in0=QS[:, qc, :], in1=KS[:, kc, :], op=ALU.mult), reads=[b_QS, b_KS], writes=[b_TMP[0]])
            nb = rNB.next()
            S.op(PE, lambda e: e.matmul(PS[nb][:, 0:NS], lhsT=bonesF, rhs=TMP[0][:, 0:NS], start=True, stop=True), reads=[b_TMP[0], b_PBC], writes=[b_PS[nb]])
            S.op(ACT, lambda e, qc=qc: e.activation(out=P0[:, qc, :], in_=PS[nb][:, 0:NS], func=AF.Exp, scale=8.0), reads=[b_PS[nb]], writes=[b_P0])

        PU, PZ = UA, UB
        first_u = [True]
        rCT = Rot([0, 1]); rTMP = Rot([0, 1])
        for n in range(NS):
            for g in range(4):
                ci = rCT.next()
                if g < 3:
                    dil = (1, 4, 16)[g]; L = (128, 512, 2048)[g]
                    S.dma(SP, lambda e, g=g, n=n, L=L, dil=dil, ci=ci: e.dma_start(out=CT[ci][:, :], in_=ca_d[g][n, 0:L:dil, :]), writes=[b_CT[ci]], semowner=b_CT[ci])
                    kview = CT[ci][:, 0:512]
                    vview = CT[ci][:, 512:1024]
                else:
                    S.dma(SP, lambda e, n=n, ci=ci: e.dma_start(out=CT[ci][:, 0:256], in_=cb_d[n, :, :]), writes=[b_CT[ci]], semowner=b_CT[ci])
                    kview = CT[ci][:, 0:128].rearrange("p (a d) -> p a d", d=64).unsqueeze(2).to_broadcast([128, 2, 4, 64])
                    vview = CT[ci][:, 128:256].rearrange("p (a d) -> p a d", d=64).unsqueeze(2).to_broadcast([128, 2, 4, 64])
                qb_bank = rST.next()
                for c in range(4):
                    S.op(PE, lambda e, g=g, c=c, n=n: e.matmul(PS[qb_bank][:, 128 * c:128 * c + 128], lhsT=QS[:, 4 * g + c, n:n + 1].to_broadcast([128, 128]),
                                                             rhs=identF[:, :], start=True, stop=True), reads=[b_QS] + C, writes=[b_PS[qb_bank]])
                ti = rTMP.next()
                if g < 3:
                    S.op(DVE, lambda e, kview=kview, ti=ti: e.tensor_tensor(out=TMP[ti][:, :], in0=kview, in1=PS[qb_bank][:, :], op=ALU.mult),
                         reads=[b_CT[ci], b_PS[qb_bank]], writes=[b_TMP[ti]])
                else:
                    S.op(DVE, lambda e, kview=kview, ti=ti: e.tensor_tensor(out=TMP[ti][:, :].rearrange("p (a r d) -> p a r d", r=4, d=64), in0=kview,
                                                                           in1=PS[qb_bank][:, :].rearrange("p (a r d) -> p a r d", r=4, d=64), op=ALU.mult),
                         reads=[b_CT[ci], b_PS[qb_bank]], writes=[b_TMP[ti]])
                S.op(DVE, lambda e, ti=ti: e.tensor_reduce(out=SC[:, 0:8], in_=TMP[ti][:, :].rearrange("p (h d) -> p h d", d=64), axis=AX.X, op=ALU.add),
                     reads=[b_TMP[ti]], writes=[b_SC])
                S.op(DVE, lambda e: e.scalar_tensor_tensor(out=SC[:, 0:8], in0=SC[:, 0:8], scalar=0.125, in1=dbiasS[:, :], op0=ALU.mult, op1=ALU.add),
                     reads=[b_SC] + C, writes=[b_SC])
                S.op(ACT, lambda e: e.activation(out=SC[:, 8:16], in_=SC[:, 0:8], func=AF.Exp), reads=[b_SC], writes=[b_SC])
                S.op(DVE, lambda e: e.tensor_copy(out=PBC[:, :].rearrange("p (h d) -> p h d", d=64), in_=SC[:, 8:16].unsqueeze(2).to_broadcast([128, 8, 64])),
                     reads=[b_SC], writes=[b_PBC])
                ti2 = rTMP.next()
                if g < 3:
                    S.op(DVE, lambda e, vview=vview, ti2=ti2: e.tensor_tensor(out=TMP[ti2][:, :], in0=vview, in1=PBC[:, :], op=ALU.mult),
                         reads=[b_CT[ci], b_PBC], writes=[b_TMP[ti2]])
                else:
                    S.op(DVE, lambda e, vview=vview, ti2=ti2: e.tensor_tensor(out=TMP[ti2][:, :].rearrange("p (a r d) -> p a r d", r=4, d=64), in0=vview,
                                                                             in1=PBC[:, :].rearrange("p (a r d) -> p a r d", r=4, d=64), op=ALU.mult),
                         reads=[b_CT[ci], b_PBC], writes=[b_TMP[ti2]])
                colbase = 0 if g < 3 else 4
                for c in range(4):
                    col = (colbase + c) * NS + n
                    fu = first_u[0]
                    S.op(PE, lambda e, c=c, col=col, fu=fu, ti2=ti2: e.matmul(PS[PU][:, col:col + 1], lhsT=TMP[ti2][:, 128 * c:128 * c + 128], rhs=onesF,
                                                                         start=fu, stop=False, skip_group_check=True), reads=[b_TMP[ti2], b_SC], writes=[b_PS[PU]])
                    S.op(PE, lambda e, c=c, col=col, fu=fu: e.matmul(PS[PZ][:, col:col + 1], lhsT=PBC[:, 128 * c:128 * c + 128], rhs=onesF,
                                                               start=fu, stop=False, skip_group_check=True), reads=[b_PBC, b_SC], writes=[b_PS[PZ]])
                    first_u[0] = False
        S.op(DVE, lambda e: e.tensor_copy(out=UT[:, :, :].rearrange("p a b -> p (a b)"), in_=PS[PU][:, 0:8 * NS]), reads=[b_PS[PU]], writes=[b_UT])
        S.op(DVE, lambda e: e.tensor_copy(out=ZT[:, :, :].rearrange("p a b -> p (a b)"), in_=PS[PZ][:, 0:8 * NS]), reads=[b_PS[PZ]], writes=[b_ZT])
        for qc in range(16):
            uc = (qc % 4) if qc < 12 else 4 + (qc - 12)
            vc = qc if qc < 12 else 12 + (qc - 12) // 2
            S.op(DVE, lambda e, qc=qc, vc=vc: e.tensor_tensor(out=TMP[0][:, 0:NS], in0=P0[:, qc, :], in1=VS[:, vc, :], op=ALU.mult), reads=[b_P0, b_VS], writes=[b_TMP[0]])
            S.op(DVE, lambda e, uc=uc: e.tensor_tensor(out=UT[:, uc, :], in0=UT[:, uc, :], in1=TMP[0][:, 0:NS], op=ALU.add), reads=[b_UT, b_TMP[0]], writes=[b_UT])
            S.op(DVE, lambda e, uc=uc, qc=qc: e.tensor_tensor(out=ZT[:, uc, :], in0=ZT[:, uc, :], in1=P0[:, qc, :], op=ALU.add), reads=[b_ZT, b_P0], writes=[b_ZT])
        for c in range(4):
            S.op(DVE, lambda e, c=c: e.tensor_scalar(out=ZT[:, 4 + c, :], in0=ZT[:, 4 + c, :], scalar1=skS[:, c:c + 1], scalar2=None, op0=ALU.add), reads=[b_ZT] + C, writes=[b_ZT])
        S.op(DVE, lambda e: e.reciprocal(out=ZT[:, :, :], in_=ZT[:, :, :]), reads=[b_ZT], writes=[b_ZT])
        S.op(DVE, lambda e: e.tensor_tensor(out=UT[:, :, :], in0=UT[:, :, :], in1=ZT[:, :, :], op=ALU.mult), reads=[b_UT, b_ZT], writes=[b_UT])
        S.op(DVE, lambda e: e.tensor_tensor(out=aT[:, :, 0:NS], in0=UT[:, 0:4, :], in1=ZS[:, 0:4, :], op=ALU.mult), reads=[b_UT, b_ZS], writes=b_aT)
        S.op(DVE, lambda e: e.tensor_tensor(out=bT[:, :, 0:NS], in0=UT[:, 4:8, :], in1=ZS[:, 4:8, :], op=ALU.mult), reads=[b_UT, b_ZS], writes=b_bT)
        fence()
        final_stage(0, NS, xs_d, ys_d, xT, 1, NS)
        fence()

    return finish()

N_CORES = 8
NSB_FULL = 2
NS_FULL = 16
_prog_cache = {}


def _consts():
    ident = np.eye(128, dtype=np.float32)
    bones = np.zeros((128, 128), np.float32)
    bones[:64, :64] = 1.0 / 64
    bones[64:, 64:] = 1.0 / 64
    swapm = np.zeros((128, 128), np.float32)
    for m in range(128):
        swapm[(m + 64) % 128, m] = 1.0
    kk = np.arange(128)[:, None].astype(np.float64)
    qq = np.arange(128)[None, :].astype(np.float64)
    masks = np.zeros((128, 8, 256), np.float32)
    for h in range(8):
        m = 2.0 ** -(h + 1)
        md = np.where(qq >= kk, np.exp(-m * (qq - kk)), 0.0)
        mp = np.where(kk >= qq, np.exp(-m * (qq - kk + 128)), 0.0)
        masks[:, h, :128] = md
        masks[:, h, 128:] = mp
    dbias = np.zeros((128, 8), np.float32)
    for h in range(8):
        dbias[:, h] = -(2.0 ** -(h + 1)) * (128 - np.arange(128))
    return ident, bones, swapm, masks.reshape(128, 2048), dbias


def make_in_maps(inp, n_cores, NSB, NS, seq_len):
    f = lambda a: np.ascontiguousarray(a, dtype=np.float32)
    w_in = inp["w_in"][0]
    win = f(w_in.reshape(8, 128, NCH, 128).transpose(2, 1, 0, 3).reshape(NCH, 128, 1024))
    gain = f(inp["norm_gain"][0].reshape(8, 128).T)
    wba = f(inp["w_branch_a"][0].reshape(4, 128, 1024).transpose(1, 0, 2).reshape(128, 4096))
    wbb = f(inp["w_branch_b"][0].reshape(4, 128, 1024).transpose(1, 0, 2).reshape(128, 4096))
    wout = f(inp["w_out"][0].reshape(8, 128, 1024).transpose(1, 0, 2).reshape(128, 8192))
    qa = inp["qk_norm_a"][0]
    qb = inp["qk_norm_b"][0]
    cols = [qa[0, 0], qa[0, 1], qa[1, 0], qa[1, 1], qa[2, 0], qa[2, 1], qb[0], qb[1]]
    qkg = f(np.stack([np.tile(c, 2) for c in cols], axis=1))
    sinks = inp["b_sinks"][0]
    sk = f(np.stack([np.repeat(sinks[2 * c:2 * c + 2], 64) for c in range(4)], axis=1))
    ident, bones, swapm, masks, dbias = _consts()
    T = NSB * SBT
    halves = seq_len // T
    maps = []
    for core in range(n_cores):
        b, hf = core // halves, core % halves
        x = inp["x_prompt"][b, hf * T:(hf + 1) * T]
        if hf == 0:
            xh = np.zeros((SBT, D), np.float32)
            flag = np.zeros((128, 1), np.float32)
        else:
            xh = inp["x_prompt"][b, hf * T - SBT:hf * T]
            flag = np.ones((128, 1), np.float32)
        sl = slice(core * NS, (core + 1) * NS)
        m = dict(x=f(x), xh=f(xh), flag=flag, xs=f(inp["x_sample"][sl, 0]), win=win, gain=gain, wba=wba, wbb=wbb, wout=wout,
                 qkg=qkg, sk=sk, ident=ident, bones=bones, swapm=swapm, masks=masks, dbias=dbias,
                 ca1=f(inp["cache_a1_kv"][0, sl].reshape(NS, 128, 1024)),
                 ca2=f(inp["cache_a2_kv"][0, sl].reshape(NS, 512, 1024)),
                 ca3=f(inp["cache_a3_kv"][0, sl].reshape(NS, 2048, 1024)),
                 cb=f(inp["cache_b_kv"][0, sl].reshape(NS, 128, 256)))
        maps.append(m)
    return maps


def kernel(**inputs):
    key = (NSB_FULL, NS_FULL)
    if key not in _prog_cache:
        _prog_cache[key] = build_program(NSB_FULL, NS_FULL)
    nc = _prog_cache[key]
    maps = make_in_maps(inputs, N_CORES, NSB_FULL, NS_FULL, 8192)
    res = run_bass_kernel_spmd(nc, maps, core_ids=list(range(N_CORES)))
    R = res.results
    T = NSB_FULL * SBT
    y = np.stack([np.concatenate([R[2 * b]["y"], R[2 * b + 1]["y"]], axis=0) for b in range(4)], axis=0)
    ys = np.concatenate([R[c]["ys"] for c in range(N_CORES)], axis=0)[:, None, :]
    pa = [np.stack([R[2 * b + 1][n].reshape(-1, 2, 8, 64) for b in range(4)], axis=0)[None] for n in ("pa1", "pa2", "pa3")]
    pb = np.stack([R[2 * b + 1]["pb"].reshape(-1, 2, 2, 64) for b in range(4)], axis=0)[None]
    sa = [np.concatenate([R[c][n] for c in range(N_CORES)], axis=0).reshape(128, -1, 2, 8, 64)[None] for n in ("sa1", "sa2", "sa3")]
    sbo = np.concatenate([R[c]["sb"] for c in range(N_CORES)], axis=0).reshape(128, -1, 2, 2, 64)[None]
    return (y.astype(np.float32), ys.astype(np.float32), pa[0], pa[1], pa[2], pb, sa[0], sa[1], sa[2], sbo)
```

```python
import numpy as np
from contextlib import ExitStack
import concourse.bass as bass
import concourse.mybir as mybir
from concourse.bass_utils import run_bass_kernel_spmd

F32 = mybir.dt.float32
BF16 = mybir.dt.bfloat16
ALU = mybir.AluOpType
AF = mybir.ActivationFunctionType
AX = mybir.AxisListType

PE, ACT, DVE, POOL, SP = "tensor", "scalar", "vector", "gpsimd", "sync"
ENGS = (PE, ACT, DVE, POOL, SP)


class Buf:
    __slots__ = ("name", "w", "r", "dsem", "dcnt", "excl")

    def __init__(self, name, excl=False):
        self.name = name
        self.excl = excl
        self.w = None
        self.r = {}
        self.dsem = None
        self.dcnt = 0


class Ev:
    __slots__ = ("kind", "a", "b")

    def __init__(self, kind, a, b):
        self.kind, self.a, self.b = kind, a, b


class _Rec:
    def __init__(self):
        self.call = None

    def __getattr__(self, name):
        def f(*args, **kwargs):
            self.call = (name, args, kwargs)
            return self
        return f


def _bind(fn):
    r = _Rec()
    fn(r)
    assert r.call is not None
    name, args, kwargs = r.call
    return lambda eng: getattr(eng, name)(*args, **kwargs)


class Sched:
    def __init__(self, nc, stack):
        self.nc = nc
        self.stack = stack
        self.streams = {e: [] for e in ENGS}
        self.esem = {e: stack.enter_context(nc.semaphore("es_" + e)) for e in ENGS}
        self.nsem = len(ENGS)
        self.bulk_sem = stack.enter_context(nc.semaphore("bulk"))
        self.bulk_cnt = 0
        self.final_waits = []
        self.recent_dma = []

    def buf(self, name, excl=False):
        return Buf(name, excl)

    def _deps(self, eng, reads, writes):
        deps = []
        for b in reads:
            if b.w is not None:
                deps.append(b.w)
            if b.excl:
                deps.extend(v for k, v in b.r.items() if k != eng)
        for b in writes:
            if b.w is not None:
                deps.append(b.w)
            deps.extend(b.r.values())
        out = []
        seen = {}
        for d in deps:
            if d.kind == "e":
                if d.a == PE and eng == PE:
                    continue
                k = ("e", d.a)
                if k not in seen or seen[k].b < d.b:
                    seen[k] = d
            else:
                k = ("d", id(d.a))
                if k not in seen or seen[k].b < d.b:
                    seen[k] = d
        return list(seen.values())

    def op(self, eng, fn, reads=(), writes=()):
        deps = self._deps(eng, reads, writes)
        idx = len(self.streams[eng])
        fn = _bind(fn)
        rec = {"fn": fn, "deps": deps, "needed": False, "dma": None}
        self.streams[eng].append(rec)
        for d in deps:
            if d.kind == "e":
                self.streams[d.a][d.b]["needed"] = True
        ev = Ev("e", eng, idx)
        for b in reads:
            b.r[eng] = ev
        for b in writes:
            b.w = ev
            b.r = {}
        return ev

    def dma(self, q, fn, reads=(), writes=(), semowner=None):
        deps = self._deps(q, reads, writes)
        for d in deps:
            if d.kind == "e":
                self.streams[d.a][d.b]["needed"] = True
        if semowner is None:
            sem = self.bulk_sem
            self.bulk_cnt += 16
            val = self.bulk_cnt
        else:
            if semowner.dsem is None:
                semowner.dsem = self.stack.enter_context(self.nc.semaphore("ds_" + semowner.name))
                self.nsem += 1
            semowner.dcnt += 16
            sem, val = semowner.dsem, semowner.dcnt
        fn = _bind(fn)
        rec = {"fn": fn, "deps": deps, "needed": False, "dma": (sem, 16)}
        self.streams[q].append(rec)
        ev = Ev("d", sem, val)
        if semowner is not None:
            self.recent_dma.append(ev)
        for b in reads:
            b.r[("d", id(sem))] = ev
        for b in writes:
            b.w = ev
            b.r = {}
        return ev

    def wait_all(self, eng, evs):
        best = {}
        for d in evs:
            k = ("e", d.a) if d.kind == "e" else ("d", id(d.a))
            if k not in best or best[k].b < d.b:
                best[k] = d
        evs = list(best.values())
        rec = {"fn": None, "deps": list(evs), "needed": False, "dma": None}
        for d in evs:
            if d.kind == "e":
                self.streams[d.a][d.b]["needed"] = True
        self.streams[eng].append(rec)

    def emit(self):
        nc = self.nc
        val = {}
        for e in ENGS:
            c = 0
            for i, rec in enumerate(self.streams[e]):
                if rec["dma"] is None and rec["fn"] is not None and rec["needed"]:
                    c += 1
                    val[(e, i)] = c
        self.maxval = {e: max([v for (ee, _), v in val.items() if ee == e], default=0) for e in ENGS}
        with nc.Block() as block:
            def run(eng_name):
                def body(eng):
                    waited = {}
                    for i, rec in enumerate(self.streams[eng_name]):
                        for d in rec["deps"]:
                            if d.kind == "e":
                                sem, v = self.esem[d.a], val[(d.a, d.b)]
                            else:
                                sem, v = d.a, d.b
                            k = id(sem)
                            if waited.get(k, 0) >= v:
                                continue
                            waited[k] = v
                            eng.wait_ge(sem, v)
                        if rec["fn"] is None:
                            continue
                        ins = rec["fn"](eng)
                        if rec["dma"] is not None:
                            ins.then_inc(rec["dma"][0], 16)
                        elif rec["needed"]:
                            ins.then_inc(self.esem[eng_name], 1)
                return body
            block.tensor(run(PE))
            block.scalar(run(ACT))
            block.vector(run(DVE))
            block.gpsimd(run(POOL))
            block.sync(run(SP))

D = 1024
SBT = 2048
NCH = 66
EPS = 1e-6


def CH_Q(g, c): return 12 * (g - 1) + c
def CH_K(g, c): return 12 * (g - 1) + 4 + c
def CH_V(g, c): return 12 * (g - 1) + 8 + c
def CH_BQ(c): return 36 + c
CH_BK, CH_BV = 40, 41
def CH_ZA(c): return 42 + c
def CH_ZB(c): return 46 + c
def CH_GA(j): return 50 + j
def CH_GB(j): return 58 + j


class Rot:
    def __init__(self, items):
        self.items, self.i = items, 0

    def next(self):
        it = self.items[self.i % len(self.items)]
        self.i += 1
        return it


def build_program(NSB, NS, with_sample=True, with_copy=True):
    nc = bass.Bass("TRN2", target_bir_lowering=False)
    T = NSB * SBT

    def din(name, shape):
        return nc.dram_tensor(name, list(shape), F32, kind="ExternalInput").ap()

    def dout(name, shape):
        return nc.dram_tensor(name, list(shape), F32, kind="ExternalOutput").ap()

    x_d = din("x", [T, D]); xh_d = din("xh", [SBT, D]); flag_d = din("flag", [128, 1])
    xs_d = din("xs", [NS, D])
    win_d = din("win", [NCH, 128, 8 * 128])
    gain_d = din("gain", [128, 8])
    wba_d = din("wba", [128, 4 * 1024]); wbb_d = din("wbb", [128, 4 * 1024]); wout_d = din("wout", [128, 8 * 1024])
    qkg_d = din("qkg", [128, 8]); sk_d = din("sk", [128, 4])
    ident_d = din("ident", [128, 128]); bones_d = din("bones", [128, 128]); swap_d = din("swapm", [128, 128])
    masks_d = din("masks", [128, 8 * 256])
    dbias_d = din("dbias", [128, 8])
    ca_d = [din("ca1", [NS, 128, 1024]), din("ca2", [NS, 512, 1024]), din("ca3", [NS, 2048, 1024])]
    cb_d = din("cb", [NS, 128, 256])
    y_d = dout("y", [T, D]); ys_d = dout("ys", [NS, D])
    pa_d = [dout("pa1", [128, 1024]), dout("pa2", [512, 1024]), dout("pa3", [2048, 1024])]
    pb_d = dout("pb", [128, 256])
    sa_d = [dout("sa1", [NS, 128, 1024]), dout("sa2", [NS, 512, 1024]), dout("sa3", [NS, 2048, 1024])]
    sb_d = dout("sb", [NS, 128, 256])

    scr_g = nc.dram_tensor("scr_g", [16, 128, 1024], BF16).ap()
    scr_o = nc.dram_tensor("scr_o", [2, 128, 4096], BF16).ap()
    st = ExitStack()
    S = Sched(nc, st)
    out_evs = []

    def sb(name, shape, dt=F32):
        return nc.alloc_sbuf_tensor(name, list(shape), dt)

    xT = sb("xT", [128, 8, SBT], BF16); b_xT = S.buf("xT")
    WST = [sb("wst%d" % i, [128, 8, 128], F32) for i in range(2)]; b_WST = [S.buf("wst%d" % i) for i in range(2)]
    WBF = [sb("wbf%d" % i, [128, 8, 128], BF16) for i in range(4)]; b_WBF = [S.buf("wbf%d" % i) for i in range(4)]
    K3 = [sb("k3_%d" % i, [128, SBT], BF16) for i in range(5)]; b_K3 = [S.buf("k3_%d" % i) for i in range(5)]
    V3 = [sb("v3_%d" % i, [128, 16, 192], BF16) for i in range(5)]; b_V3 = [S.buf("v3_%d" % i) for i in range(5)]
    K2t = [sb("k2t%d" % i, [128, 512], BF16) for i in range(4)]; b_K2t = [S.buf("k2t%d" % i) for i in range(4)]
    V2t = [sb("v2t%d" % i, [128, 4, 192], BF16) for i in range(4)]; b_V2t = [S.buf("v2t%d" % i) for i in range(4)]
    K1t = [sb("k1t%d" % i, [128, 128], BF16) for i in range(4)]; b_K1t = [S.buf("k1t%d" % i) for i in range(4)]
    V1t = [sb("v1t%d" % i, [128, 1, 192], BF16) for i in range(4)]; b_V1t = [S.buf("v1t%d" % i) for i in range(4)]
    KBt = [sb("kbt%d" % i, [128, 128], BF16) for i in range(2)]; b_KBt = [S.buf("kbt%d" % i) for i in range(2)]
    VBt = [sb("vbt%d" % i, [128, 1, 192], BF16) for i in range(2)]; b_VBt = [S.buf("vbt%d" % i) for i in range(2)]
    aT = sb("aT", [128, 4, SBT], BF16); b_aT = [S.buf("aT%d" % i) for i in range(4)]
    bT = sb("bT", [128, 4, SBT], BF16); b_bT = [S.buf("bT%d" % i) for i in range(4)]
    MASK = sb("mask", [128, 8, 256], BF16); b_const = S.buf("const")
    identF = sb("identF", [128, 128], F32); identB = sb("identB", [128, 128], BF16)
    bonesB = sb("bonesB", [128, 128], BF16); swapF = sb("swapF", [128, 128], F32)
    gainS = sb("gainS", [128, 8], F32); qkgS = sb("qkgS", [128, 8], F32); skS = sb("skS", [128, 4], F32)
    flagS = sb("flagS", [128, 1], F32); epsS = sb("epsS", [128, 1], F32); oneS = sb("oneS", [128, 1], F32)
    onesB = sb("onesB", [128, 64], BF16)
    dbiasS = sb("dbiasS", [128, 8], F32)
    stat = sb("stat", [128, 8], F32); b_stat = S.buf("stat")
    stats4 = [sb("stat4_%d" % i, [128, 4], F32) for i in range(4)]; b_stats4 = [S.buf("stat4_%d" % i) for i in range(4)]

    ARENA_BYTES = 56 * 1024 + 512
    arena = sb("arena", [128, ARENA_BYTES // 2], BF16)
    a_off = [0]

    def carve(shape, dt):
        n = int(np.prod(shape[1:]))
        nb = n * (4 if dt == F32 else 2)
        nb_al = (nb + 63) // 64 * 64
        o = a_off[0]
        assert o + nb_al <= ARENA_BYTES, (o, nb_al)
        a_off[0] += nb_al
        ap = arena[0:shape[0], o // 2:(o + nb) // 2]
        if dt == F32:
            ap = ap.bitcast(F32)
        if len(shape) == 3:
            ap = ap.rearrange("p (a b) -> p a b", b=shape[2])
        return ap

    a_off[0] = 0
    XST = [carve([128, 1024], F32) for _ in range(4)]; b_XST = [S.buf("xst%d" % i) for i in range(4)]
    XB = [carve([128, 1024], BF16) for _ in range(4)]; b_XB = [S.buf("xb%d" % i) for i in range(4)]
    cst = carve([128, 8 * 256], F32)
    a_off[0] = 0
    K2c = carve([128, 512 + SBT], BF16); b_K2c = S.buf("k2c")
    V2c = carve([128, 20, 192], BF16); b_V2c = S.buf("v2c")
    K1c = carve([128, 128 + SBT], BF16); b_K1c = S.buf("k1c")
    V1c = carve([128, 17, 192], BF16); b_V1c = S.buf("v1c")
    QT = [carve([128, 512], BF16) for _ in range(6)]; b_QT = [S.buf("qt%d" % i) for i in range(6)]
    EXPT = [carve([128, 512], F32) for _ in range(2)]; b_EXPT = [S.buf("expt%d" % i) for i in range(2)]
    PT = [carve([128, 512], BF16) for _ in range(2)]; b_PT = [S.buf("pt%d" % i) for i in range(2)]
    SQ = [carve([128, 512], BF16) for _ in range(2)]; b_SQ = [S.buf("sq%d" % i) for i in range(2)]
    NL = [carve([128, 512], F32) for _ in range(2)]; b_NL = [S.buf("nl%d" % i) for i in range(2)]
    SZ = carve([128, SBT], BF16); b_SZ = S.buf("sz")
    RW = carve([128, 512], F32); b_RW = S.buf("rw")
    AUN = carve([128, 512], F32); b_AUN = S.buf("aun")
    KF = carve([128, 512], F32); b_KF = S.buf("kf")
    KO = carve([128, 512], F32); b_KO = S.buf("ko")
    VO = carve([128, 512], F32); b_VO = S.buf("vo")
    att_end = a_off[0]
    a_off[0] = 0
    WBA = carve([128, 4, 1024], BF16); WBB = carve([128, 4, 1024], BF16); WOUT = carve([128, 8, 512], BF16)
    b_WBA = S.buf("wba"); b_WBB = S.buf("wbb"); b_WOUT = S.buf("wout")
    MIX = carve([128, 8, 512], BF16); b_MIX = [S.buf("mix%d" % i) for i in range(8)]
    SG = [carve([128, 512], F32) for _ in range(2)]; b_SG = [S.buf("sg%d" % i) for i in range(2)]
    T1 = [carve([128, 512], F32) for _ in range(2)]; b_T1 = [S.buf("t1%d" % i) for i in range(2)]
    XR = [carve([128, 512], F32) for _ in range(2)]; b_XR = [S.buf("xr%d" % i) for i in range(2)]
    WF = [carve([128, 1024], F32) for _ in range(2)]; b_WF = [S.buf("wf%d" % i) for i in range(2)]

    print("sbuf remaining", nc.sbuf_bytes_remaining, "att_end", att_end, "fin_end", a_off[0])
    PS = [nc.alloc_psum_tensor("ps%d" % i, [128, 512], F32) for i in range(8)]
    b_PS = [S.buf("ps%d" % i, excl=True) for i in range(8)]
    rPJ = Rot([0, 1]); rNB = Rot([2, 3]); rST = Rot([4, 5]); rPJ6 = Rot([0, 1, 4, 5])
    UA, UB = 6, 7
    rWST = Rot([0, 1]); rWBF = Rot([0, 1, 2, 3])
    rEXPT = Rot([0, 1]); rPT = Rot([0, 1]); rSQ = Rot([0, 1]); rNL = Rot([0, 1])

    def fence():
        evs = []
        for e in (PE, ACT, DVE, POOL):
            n = len(S.streams[e])
            for i in range(n - 1, -1, -1):
                r = S.streams[e][i]
                if r["fn"] is not None and r["dma"] is None:
                    evs.append(Ev("e", e, i))
                    break
        evs = evs + list(S.recent_dma)
        S.recent_dma = []
        for e in (PE, ACT, DVE, POOL, SP):
            S.wait_all(e, evs)
    fence.dma_evs = []

    def load_const(dst, src, cols, conv=None):
        ev = S.dma(SP, lambda e: e.dma_start(out=cst[:, 0:cols], in_=src), writes=[b_const], semowner=b_const)
        if conv == "bf":
            S.op(DVE, lambda e: e.tensor_copy(out=dst, in_=cst[:, 0:cols]), reads=[b_const], writes=[b_const])
        else:
            S.op(DVE, lambda e: e.tensor_copy(out=dst, in_=cst[:, 0:cols]), reads=[b_const], writes=[b_const])

    load_const(identF[:, :], ident_d, 128); load_const(identB[:, :], ident_d, 128)
    load_const(bonesB[:, :], bones_d, 128); load_const(swapF[:, :], swap_d, 128)
    load_const(MASK[:, :, :].rearrange("p a b -> p (a b)"), masks_d, 2048)
    load_const(gainS[:, :], gain_d, 8); load_const(qkgS[:, :], qkg_d, 8); load_const(skS[:, :], sk_d, 4)
    load_const(flagS[:, :], flag_d, 1); load_const(dbiasS[:, :], dbias_d, 8)
    S.op(DVE, lambda e: e.memset(epsS[:, :], EPS), writes=[b_const])
    S.op(DVE, lambda e: e.memset(oneS[:, :], 1.0), writes=[b_const])
    S.op(DVE, lambda e: e.memset(onesB[:, :], 1.0), writes=[b_const])
    S.op(ACT, lambda e: e.activation(out=skS[:, :], in_=skS[:, :], func=AF.Exp), reads=[b_const], writes=[b_const])
    C = [b_const]

    bulk = []
    if with_copy:
        def add_copy(src, dst):
            bulk.append((src, dst))
        for n in range(NS):
            for g, L in ((2, 2048), (1, 512), (0, 128)):
                r = 1
                while r < L:
                    nr = min(256, L - r)
                    for (a, bnd) in ((16, None),):
                        pass
                    o = 16 if nr % 16 == 0 else (15 if nr % 15 == 0 else 1)
                    sv = ca_d[g][n, r:r + nr, :].rearrange("(o i) c -> o (i c)", o=o)
                    dv = sa_d[g][n, r - 1:r - 1 + nr, :].rearrange("(o i) c -> o (i c)", o=o)
                    add_copy(sv, dv)
                    r += nr
        sv = cb_d[:, 1:128, :].rearrange("n r c -> n (r c)")
        dv = sb_d[:, 0:127, :].rearrange("n r c -> n (r c)")
        add_copy(sv, dv)

    def issue_bulk(k=1):
        for _ in range(k):
            if bulk:
                sv, dv = bulk.pop(0)
                S.dma(ACT, lambda e, sv=sv, dv=dv: e.dma_start(out=dv, in_=sv))

    conv_eng = [POOL]
    bulk_rate = [2]
    b_scr_g = [S.buf("scrg%d" % i) for i in range(16)]
    b_scr_o = [S.buf("scro%d" % i) for i in range(2)]
    scr_ready = {"g": [False] * 16, "o": [False] * 2}

    def load_g(i):
        if not scr_ready["g"][i]:
            wv_, bw_ = load_w(50 + i)
            S.dma(SP, lambda e: e.dma_start(out=scr_g[i], in_=wv_[:, :, :].rearrange("p a b -> p (a b)")), reads=[bw_], writes=[b_scr_g[i]], semowner=b_scr_g[i])
            scr_ready["g"][i] = True
            return wv_, bw_
        bi = rWBF.next()
        S.dma(SP, lambda e: e.dma_start(out=WBF[bi][:, :, :].rearrange("p a b -> p (a b)"), in_=scr_g[i]), reads=[b_scr_g[i]], writes=[b_WBF[bi]], semowner=b_WBF[bi])
        return WBF[bi], b_WBF[bi]

    def load_w(ch, dup=None, src=None, scale=1.0):
        issue_bulk(bulk_rate[0])
        CE = conv_eng[0]
        si = rWST.next(); bi = rWBF.next()
        S.dma(SP, lambda e: e.dma_start(out=WST[si][:, :, :].rearrange("p a b -> p (a b)"), in_=win_d[ch]),
              writes=[b_WST[si]], semowner=b_WST[si])
        g_bc = gainS[:, :].unsqueeze(2).to_broadcast([128, 8, 128])
        if dup is None:
            S.op(CE, lambda e: e.tensor_tensor(out=WBF[bi][:, :, :], in0=WST[si][:, :, :], in1=g_bc, op=ALU.mult),
                 reads=[b_WST[si]] + C, writes=[b_WBF[bi]])
        else:
            g64 = gainS[:, :].unsqueeze(2).to_broadcast([128, 8, 64])
            for half in (0, 1):
                S.op(CE, lambda e, half=half: e.tensor_tensor(out=WBF[bi][:, :, 64 * half:64 * half + 64],
                                                               in0=WST[si][:, :, 64 * dup:64 * dup + 64], in1=g64, op=ALU.mult),
                     reads=[b_WST[si]] + C, writes=[b_WBF[bi]])
        return WBF[bi], b_WBF[bi]

    def proj_fm(w, bw, bank, ncols, rhs_fn):
        for k in range(8):
            S.op(PE, lambda e, k=k: e.matmul(PS[bank][:, 0:ncols], lhsT=w[:, k, :], rhs=rhs_fn(k), start=(k == 0), stop=(k == 7)),
                 reads=[bw, b_xT], writes=[b_PS[bank]])

    def norm_evac(bank, ncols, gcol, out_ap, out_bufs, f32_out=None):
        si = rSQ.next(); ni = rNL.next(); nb = rNB.next()
        S.op(ACT, lambda e: e.activation(out=SQ[si][:, 0:ncols], in_=PS[bank][:, 0:ncols], func=AF.Square),
             reads=[b_PS[bank]], writes=[b_SQ[si]])
        S.op(PE, lambda e: e.matmul(PS[nb][:, 0:ncols], lhsT=bonesB[:, :], rhs=SQ[si][:, 0:ncols], start=True, stop=True),
             reads=[b_SQ[si]] + C, writes=[b_PS[nb]])
        S.op(ACT, lambda e: e.activation(out=NL[ni][:, 0:ncols], in_=PS[nb][:, 0:ncols], func=AF.Ln, bias=epsS[:, 0:1], scale=1.0),
             reads=[b_PS[nb]] + C, writes=[b_NL[ni]])
        S.op(ACT, lambda e: e.activation(out=NL[ni][:, 0:ncols], in_=NL[ni][:, 0:ncols], func=AF.Exp, scale=-0.5),
             reads=[b_NL[ni]], writes=[b_NL[ni]])
        S.op(DVE, lambda e: e.scalar_tensor_tensor(out=out_ap, in0=PS[bank][:, 0:ncols], scalar=qkgS[:, gcol:gcol + 1],
                                                   in1=NL[ni][:, 0:ncols], op0=ALU.mult, op1=ALU.mult),
             reads=[b_PS[bank], b_NL[ni]] + C, writes=out_bufs)
        if f32_out is not None:
            fo, fb = f32_out
            S.op(DVE, lambda e: e.scalar_tensor_tensor(out=fo, in0=PS[bank][:, 0:ncols], scalar=qkgS[:, gcol:gcol + 1],
                                                       in1=NL[ni][:, 0:ncols], op0=ALU.mult, op1=ALU.mult),
                 reads=[b_PS[bank], b_NL[ni]] + C, writes=fb)

    def xT_win(w):
        return lambda k: xT[:, k, 512 * w:512 * w + 512]

    def tok_ap(k, off, step, n=128):
        return xT[:, k, off:off + step * (n - 1) + 1:step]

    def proj_v_blocks(w, bw, blocks, dst, dbuf, scale_flag=False, vout=None, vcols=128):
        d0 = blocks[0][2]; nb_ = len(blocks)
        assert [b[2] for b in blocks] == list(range(d0, d0 + nb_))
        if scale_flag:
            S.op(POOL, lambda e: e.tensor_copy(out=dst[:, d0:d0 + nb_, 64:128], in_=flagS[:, 0:1].unsqueeze(1).to_broadcast([128, nb_, 64])),
                 reads=C, writes=[dbuf])
        else:
            S.op(POOL, lambda e: e.tensor_copy(out=dst[:, d0:d0 + nb_, 64:128], in_=onesB[:, :].unsqueeze(1).to_broadcast([128, nb_, 64])),
                 reads=C, writes=[dbuf])
        for i0_ in range(0, len(blocks), 4):
            grp = blocks[i0_:i0_ + 4]
            ng = len(grp)
            bank = rPJ6.next()
            for bi, (off, step, di) in enumerate(grp):
                for k in range(8):
                    S.op(PE, lambda e, k=k, bi=bi, off=off, step=step: e.matmul(
                        PS[bank][:, 128 * bi:128 * bi + 128], lhsT=tok_ap(k, off, step), rhs=w[:, k, :],
                        start=(k == 0), stop=(k == 7)), reads=[bw, b_xT], writes=[b_PS[bank]])
            dg = grp[0][2]
            src = PS[bank][:, 0:128 * ng].rearrange("p (b h d) -> p b h d", h=2, d=64)
            dsel = dst[:, dg:dg + ng, :].rearrange("p b (h d) -> p b h d", d=64)[:, :, 0:3:2, :]
            if scale_flag:
                S.op(DVE, lambda e, src=src, dsel=dsel: e.tensor_scalar(out=dsel, in0=src, scalar1=flagS[:, 0:1], scalar2=None, op0=ALU.mult),
                     reads=[b_PS[bank]] + C, writes=[dbuf])
            else:
                S.op(DVE, lambda e, src=src, dsel=dsel: e.tensor_copy(out=dsel, in_=src), reads=[b_PS[bank]], writes=[dbuf])
            if vout is not None:
                S.op(ACT, lambda e: e.activation(out=VO[:, 0:128 * ng], in_=PS[bank][:, 0:128 * ng], func=AF.Copy),
                     reads=[b_PS[bank]], writes=[b_VO])
                for bi in range(ng):
                    dap = vout(i0_ + bi)
                    if dap is None:
                        continue
                    ev = S.dma(POOL, lambda e, bi=bi, dap=dap: e.dma_start(out=dap, in_=VO[:, 128 * bi:128 * bi + vcols]),
                               reads=[b_VO], semowner=b_VO)
                    out_evs.append(ev)

    def prologue(src_d, t0):
        rX = Rot([0, 1, 2, 3])
        for blk in range(16):
            i = rX.next()
            S.dma(SP, lambda e, blk=blk: e.dma_start(out=XST[i], in_=src_d[t0 + 128 * blk:t0 + 128 * blk + 128, :]),
                  writes=[b_XST[i]], semowner=b_XST[i])
            stt_ = stats4[i]; bst_ = b_stats4[i]
            S.op(ACT, lambda e: e.activation(out=XB[i], in_=XST[i], func=AF.Square, accum_out=stt_[:, 0:1]),
                 reads=[b_XST[i]], writes=[b_XB[i], bst_])
            S.op(ACT, lambda e: e.activation(out=stt_[:, 1:2], in_=stt_[:, 0:1], func=AF.Ln, bias=epsS[:, 0:1], scale=1.0 / D),
                 reads=[bst_] + C, writes=[bst_])
            S.op(ACT, lambda e: e.activation(out=stt_[:, 2:3], in_=stt_[:, 1:2], func=AF.Exp, scale=-0.5),
                 reads=[bst_], writes=[bst_])
            S.op(DVE, lambda e: e.tensor_scalar(out=XB[i], in0=XST[i], scalar1=stt_[:, 2:3], scalar2=None, op0=ALU.mult),
                 reads=[b_XST[i], bst_], writes=[b_XB[i]])
            bank = rNB.next()
            pbf = PS[bank][:, :].bitcast(BF16)
            for k in range(8):
                S.op(PE, lambda e, k=k: e.transpose(pbf[:, 128 * k:128 * k + 128], XB[i][:, 128 * k:128 * k + 128], identB[:, :]),
                     reads=[b_XB[i]] + C, writes=[b_PS[bank]])
            S.op(DVE, lambda e, blk=blk: e.tensor_copy(out=xT[:, :, 128 * blk:128 * blk + 128],
                                                       in_=pbf.rearrange("p (k t) -> p k t", t=128)),
                 reads=[b_PS[bank]], writes=[b_xT])

    def attn_window(w, hg, groups, started):
        units = [(tile, head) for tile in groups for head in (0, 1)]
        state = []

        def emit_s(tile, head):
            rows = slice(64 * head, 64 * head + 64)
            stb = rST.next()
            nseg = len(tile["segs"]); n = tile["n"]
            for si, sg in enumerate(tile["segs"]):
                S.op(PE, lambda e, si=si, sg=sg, rows=rows: e.matmul(
                    PS[stb][:, n * si:n * si + n], lhsT=sg["k"][rows, :], rhs=sg["q"][rows, :], start=True, stop=True),
                    reads=[sg["kb"], sg["qb"]], writes=[b_PS[stb]])
            ei = rEXPT.next(); pi = rPT.next()
            tot = n * nseg
            S.op(ACT, lambda e, tot=tot: e.activation(out=EXPT[ei][:, 0:tot], in_=PS[stb][:, 0:tot], func=AF.Exp, scale=0.125),
                 reads=[b_PS[stb]], writes=[b_EXPT[ei]])
            m = tile["mask"](hg[head])
            nq = nseg // 2
            S.op(DVE, lambda e, tot=tot, m=m, nq=nq, n=n: e.tensor_tensor(
                out=PT[pi][:, 0:tot].rearrange("p (a b c) -> p a b c", b=2, c=n),
                in0=EXPT[ei][:, 0:tot].rearrange("p (a b c) -> p a b c", b=2, c=n),
                in1=m.unsqueeze(1).to_broadcast([128, nq, 2, n]), op=ALU.mult),
                reads=[b_EXPT[ei]] + C, writes=[b_PT[pi]])
            return pi

        def emit_pv(tile, head, pi):
            n = tile["n"]
            ub = UA if head == 0 else UB
            for si, sg in enumerate(tile["segs"]):
                vb = sg["v"]
                lhs = vb[:, 0:128] if head == 0 else vb[:, 64:192]
                first = not started[head]
                started[head] = True
                S.op(PE, lambda e, si=si, sg=sg, lhs=lhs, first=first, ub=ub: e.matmul(
                    sg["o"](PS[ub]), lhsT=lhs, rhs=PT[pi][:, n * si:n * si + n], start=first, stop=False,
                    skip_group_check=True), reads=[sg["vb"], b_PT[pi]], writes=[b_PS[ub]])

        prev = None
        for (tile, head) in units:
            pi = emit_s(tile, head)
            if prev is not None:
                emit_pv(*prev)
            prev = (tile, head, pi)
        emit_pv(*prev)

    def mask_full(h):
        return MASK[:, h, :].rearrange("p (b c) -> p b c", c=128)

    def mask_win(w):
        return lambda h: MASK[:, h, :].rearrange("p (b c) -> p b c", c=128)[:, :, 32 * w:32 * w + 32]

    def segs_g1(w, Kc, bK, Vc, bV, q, bq, qoff=0):
        tiles = []
        for half in (0, 1):
            segs = []
            for i in (0, 1):
                qb = 4 * w + 2 * half + i
                qap = q[:, 128 * qb - qoff:128 * qb - qoff + 128]
                oc = 128 * (2 * half + i)
                o = (lambda oc: (lambda ps: ps[:, oc:oc + 128]))(oc)
                segs.append(dict(k=Kc[:, 128 + 128 * qb:256 + 128 * qb], kb=bK, q=qap, qb=bq, v=Vc[:, qb + 1, :], vb=bV, o=o))
                segs.append(dict(k=Kc[:, 128 * qb:128 * qb + 128], kb=bK, q=qap, qb=bq, v=Vc[:, qb, :], vb=bV, o=o))
            tiles.append(dict(segs=segs, n=128, mask=mask_full))
        return tiles

    def strided(ap2d, off, step, n):
        return ap2d[:, off:off + step * (n - 1) + 1:step]

    def segs_g2(w, Kc, bK, Vc, bV, q, bq, qoff=0):
        tiles = []
        for half in (0, 1):
            segs = []
            for i in (0, 1):
                r = 2 * half + i
                qap = strided(q, 512 * w + r - qoff, 4, 128)
                o = (lambda r: (lambda ps: strided(ps, r, 4, 128)))(r)
                segs.append(dict(k=strided(Kc, 512 + 512 * w + r, 4, 128), kb=bK, q=qap, qb=bq, v=Vc[:, 4 + 4 * w + r, :], vb=bV, o=o))
                segs.append(dict(k=strided(Kc, 512 * w + r, 4, 128), kb=bK, q=qap, qb=bq, v=Vc[:, 4 * w + r, :], vb=bV, o=o))
            tiles.append(dict(segs=segs, n=128, mask=mask_full))
        return tiles

    def segs_g3(w, Kcur, bKc, Kprev, bKp, Vcur, bVc, Vprev, bVp, q, bq, qoff=0):
        tiles = []
        for half in (0, 1):
            segs = []
            for i in range(8):
                r = 8 * half + i
                qap = strided(q, 512 * w + r - qoff, 16, 32)
                o = (lambda r: (lambda ps: strided(ps, r, 16, 32)))(r)
                segs.append(dict(k=strided(Kcur, r, 16, 128), kb=bKc, q=qap, qb=bq, v=Vcur[:, r, :], vb=bVc, o=o))
                segs.append(dict(k=strided(Kprev, r, 16, 128), kb=bKp, q=qap, qb=bq, v=Vprev[:, r, :], vb=bVp, o=o))
            tiles.append(dict(segs=segs, n=32, mask=mask_win(w)))
        return tiles

    def finish_window(w, c, dstT, b_dst, sink_col=None):
        if sink_col is None:
            S.op(DVE, lambda e: e.reciprocal(out=RW[0:64, :], in_=PS[UB][0:64, :]), reads=[b_PS[UB]], writes=[b_RW])
            S.op(DVE, lambda e: e.reciprocal(out=RW[64:128, :], in_=PS[UA][64:128, :]), reads=[b_PS[UA]], writes=[b_RW])
        else:
            S.op(DVE, lambda e: e.tensor_copy(out=RW[0:64, :], in_=PS[UB][0:64, :]), reads=[b_PS[UB]], writes=[b_RW])
            S.op(DVE, lambda e: e.tensor_copy(out=RW[64:128, :], in_=PS[UA][64:128, :]), reads=[b_PS[UA]], writes=[b_RW])
        nb = rNB.next()
        S.op(PE, lambda e: e.matmul(PS[nb][:, :], lhsT=swapF[:, :], rhs=RW[:, :], start=True, stop=True),
             reads=[b_RW] + C, writes=[b_PS[nb]])
        S.op(DVE, lambda e: e.tensor_tensor(out=AUN[0:64, :], in0=PS[UA][0:64, :], in1=SZ[0:64, 512 * w:512 * w + 512], op=ALU.mult),
             reads=[b_PS[UA], b_SZ], writes=[b_AUN])
        S.op(DVE, lambda e: e.tensor_tensor(out=AUN[64:128, :], in0=PS[UB][64:128, :], in1=SZ[64:128, 512 * w:512 * w + 512], op=ALU.mult),
             reads=[b_PS[UB], b_SZ], writes=[b_AUN])
        if sink_col is None:
            S.op(DVE, lambda e: e.tensor_tensor(out=dstT[:, c, 512 * w:512 * w + 512], in0=AUN[:, :], in1=PS[nb][:, :], op=ALU.mult),
                 reads=[b_AUN, b_PS[nb]], writes=[b_dst[c]])
        else:
            S.op(DVE, lambda e: e.tensor_scalar(out=RW[:, :], in0=PS[nb][:, :], scalar1=skS[:, sink_col:sink_col + 1], scalar2=None, op0=ALU.add),
                 reads=[b_PS[nb]] + C, writes=[b_RW])
            S.op(DVE, lambda e: e.reciprocal(out=RW[:, :], in_=RW[:, :]), reads=[b_RW], writes=[b_RW])
            S.op(DVE, lambda e: e.tensor_tensor(out=dstT[:, c, 512 * w:512 * w + 512], in0=AUN[:, :], in1=RW[:, :], op=ALU.mult),
                 reads=[b_AUN, b_RW], writes=[b_dst[c]])

    def silu_chunk(ch):
        wz, bwz = load_w(ch)
        for w in range(4):
            bank = rPJ6.next()
            proj_fm(wz, bwz, bank, 512, xT_win(w))
            ni = rNL.next()
            S.op(ACT, lambda e: e.activation(out=NL[ni][:, :], in_=PS[bank][:, :], func=AF.Exp, scale=-1.0),
                 reads=[b_PS[bank]], writes=[b_NL[ni]])
            S.op(ACT, lambda e: e.activation(out=NL[ni][:, :], in_=NL[ni][:, :], func=AF.Ln, bias=oneS[:, 0:1], scale=1.0),
                 reads=[b_NL[ni]] + C, writes=[b_NL[ni]])
            S.op(ACT, lambda e: e.activation(out=NL[ni][:, :], in_=NL[ni][:, :], func=AF.Exp, scale=-1.0),
                 reads=[b_NL[ni]], writes=[b_NL[ni]])
            S.op(DVE, lambda e, w=w: e.tensor_tensor(out=SZ[:, 512 * w:512 * w + 512], in0=PS[bank][:, :], in1=NL[ni][:, :], op=ALU.mult),
                 reads=[b_PS[bank], b_NL[ni]], writes=[b_SZ])

    def k_out_rows(dst_d, tok_base_in_dst, c128, nblk, col0):
        nb = rNB.next()
        for b in range(nblk):
            S.op(PE, lambda e, b=b: e.transpose(PS[nb][:, 128 * b:128 * b + 128], KF[:, 128 * b:128 * b + 128], identF[:, :]),
                 reads=[b_KF] + C, writes=[b_PS[nb]])
        S.op(ACT, lambda e: e.activation(out=KO[:, 0:128 * nblk], in_=PS[nb][:, 0:128 * nblk], func=AF.Copy),
             reads=[b_PS[nb]], writes=[b_KO])
        dv = dst_d[tok_base_in_dst:tok_base_in_dst + 128 * nblk, col0:col0 + c128].rearrange("(b p) c -> p b c", p=128)
        ev = S.dma(POOL, lambda e: e.dma_start(out=dv, in_=KO[:, 0:128 * nblk].rearrange("p (b c) -> p b c", c=128)[:, :, 0:c128]),
                   reads=[b_KO], semowner=b_KO)
        out_evs.append(ev)


    def load_fin_w(dst, bdst, src_d, nk, ncol, col0, width, scale):
        rW = Rot([0, 1])
        sv = src_d.rearrange("p (k c) -> p k c", c=ncol)
        for k in range(nk):
            i = rW.next()
            S.dma(SP, lambda e, k=k, i=i: e.dma_start(out=WF[i][:, 0:width], in_=sv[:, k, col0:col0 + width]),
                  writes=[b_WF[i]], semowner=b_WF[i])
            S.op(DVE if (k % 2 == 0) else ACT, (lambda e, k=k, i=i: e.tensor_copy(out=dst[:, k, 0:width], in_=WF[i][:, 0:width])) if (k % 2 == 0) else
                 (lambda e, k=k, i=i: e.activation(out=dst[:, k, 0:width], in_=WF[i][:, 0:width], func=AF.Copy)),
                 reads=[b_WF[i]], writes=[bdst])

    def sigmoid_from(bank, dst, bdst):
        S.op(ACT, lambda e: e.activation(out=dst, in_=PS[bank][:, :], func=AF.Exp, scale=-1.0), reads=[b_PS[bank]], writes=[bdst])
        S.op(ACT, lambda e: e.activation(out=dst, in_=dst, func=AF.Ln, bias=oneS[:, 0:1], scale=1.0), reads=[bdst] + C, writes=[bdst])
        S.op(ACT, lambda e: e.activation(out=dst, in_=dst, func=AF.Exp, scale=-1.0), reads=[bdst], writes=[bdst])

    def final_stage(tok0, ntok_total, x_src, y_dst, xTsrc, nwin, wcols):
        conv_eng[0] = DVE
        bulk_rate[0] = 0
        load_fin_w(WBA, b_WBA, wba_d, 4, 1024, 0, 1024, 1.0)
        load_fin_w(WBB, b_WBB, wbb_d, 4, 1024, 0, 1024, 1.0)
        for w in range(nwin):
            n = wcols
            cs = slice(wcols * w, wcols * w + n)
            for j in range(8):
                wga, bwga = load_g(j)
                wgb, bwgb = load_g(8 + j)
                bka = rPJ.next()
                proj_fm(wga, bwga, bka, n, lambda k: xTsrc[:, k, cs])
                sigmoid_from_n(bka, SG[0][:, 0:n], b_SG[0], n)
                bkb = rPJ.next()
                proj_fm(wgb, bwgb, bkb, n, lambda k: xTsrc[:, k, cs])
                sigmoid_from_n(bkb, SG[1][:, 0:n], b_SG[1], n)
                ba = rST.next()
                for cc in range(4):
                    S.op(PE, lambda e, cc=cc, j=j: e.matmul(PS[ba][:, 0:n], lhsT=WBA[:, cc, 128 * j:128 * j + 128], rhs=aT[:, cc, cs],
                                                          start=(cc == 0), stop=(cc == 3)), reads=[b_WBA] + b_aT, writes=[b_PS[ba]])
                bb = rST.next()
                for cc in range(4):
                    S.op(PE, lambda e, cc=cc, j=j: e.matmul(PS[bb][:, 0:n], lhsT=WBB[:, cc, 128 * j:128 * j + 128], rhs=bT[:, cc, cs],
                                                          start=(cc == 0), stop=(cc == 3)), reads=[b_WBB] + b_bT, writes=[b_PS[bb]])
                S.op(DVE, lambda e: e.tensor_tensor(out=T1[0][:, 0:n], in0=PS[ba][:, 0:n], in1=SG[0][:, 0:n], op=ALU.mult),
                     reads=[b_PS[ba], b_SG[0]], writes=[b_T1[0]])
                S.op(DVE, lambda e: e.tensor_tensor(out=T1[1][:, 0:n], in0=PS[bb][:, 0:n], in1=SG[1][:, 0:n], op=ALU.mult),
                     reads=[b_PS[bb], b_SG[1]], writes=[b_T1[1]])
                S.op(DVE, lambda e, j=j: e.tensor_tensor(out=MIX[:, j, 0:n], in0=T1[0][:, 0:n], in1=T1[1][:, 0:n], op=ALU.add),
                     reads=[b_T1[0], b_T1[1]], writes=[b_MIX[j]])
            for half in range(2):
                if not scr_ready["o"][half]:
                    load_fin_w(WOUT, b_WOUT, wout_d, 8, 1024, 512 * half, 512, 1.0)
                    S.dma(SP, lambda e, half=half: e.dma_start(out=scr_o[half], in_=WOUT[:, :, :].rearrange("p a b -> p (a b)")), reads=[b_WOUT], writes=[b_scr_o[half]], semowner=b_scr_o[half])
                    scr_ready["o"][half] = True
                else:
                    S.dma(SP, lambda e, half=half: e.dma_start(out=WOUT[:, :, :].rearrange("p a b -> p (a b)"), in_=scr_o[half]), reads=[b_scr_o[half]], writes=[b_WOUT], semowner=b_WOUT)
                nblk = (n + 127) // 128
                for b in range(nblk):
                    nt = min(128, n - 128 * b)
                    yb = UA if (b % 2 == 0) else UB
                    for k in range(8):
                        S.op(PE, lambda e, k=k, b=b, nt=nt, yb=yb: e.matmul(PS[yb][0:nt, :], lhsT=MIX[:, k, 128 * b:128 * b + nt], rhs=WOUT[:, k, :],
                                                                 start=(k == 0), stop=(k == 7)), reads=b_MIX + [b_WOUT], writes=[b_PS[yb]])
                    xi = (b % 2)
                    r0 = tok0 + wcols * w + 128 * b
                    S.dma(POOL, lambda e, r0=r0, nt=nt, xi=xi, half=half: e.dma_start(out=XR[xi][0:nt, :], in_=x_src[r0:r0 + nt, 512 * half:512 * half + 512]),
                          writes=[b_XR[xi]], semowner=b_XR[xi])
                    S.op(DVE, lambda e, nt=nt, xi=xi, yb=yb: e.tensor_tensor(out=XR[xi][0:nt, :], in0=XR[xi][0:nt, :], in1=PS[yb][0:nt, :], op=ALU.add),
                         reads=[b_XR[xi], b_PS[yb]], writes=[b_XR[xi]])
                    ev = S.dma(POOL, lambda e, r0=r0, nt=nt, xi=xi, half=half: e.dma_start(out=y_dst[r0:r0 + nt, 512 * half:512 * half + 512], in_=XR[xi][0:nt, :]),
                               reads=[b_XR[xi]], semowner=b_XR[xi])
                    out_evs.append(ev)

    def sigmoid_from_n(bank, dst, bdst, n):
        S.op(ACT, lambda e: e.activation(out=dst, in_=PS[bank][:, 0:n], func=AF.Exp, scale=-1.0), reads=[b_PS[bank]], writes=[bdst])
        S.op(ACT, lambda e: e.activation(out=dst, in_=dst, func=AF.Ln, bias=oneS[:, 0:1], scale=1.0), reads=[bdst] + C, writes=[bdst])
        S.op(ACT, lambda e: e.activation(out=dst, in_=dst, func=AF.Exp, scale=-1.0), reads=[bdst], writes=[bdst])

    import os as _os
    STOP = int(_os.environ.get("MK_STOP", "0"))

    def finish():
        while bulk:
            issue_bulk(1)
        evs = list(out_evs) + list(S.recent_dma)
        if S.bulk_cnt:
            evs.append(Ev("d", S.bulk_sem, S.bulk_cnt))
        for e in (PE, ACT, DVE, POOL):
            for i in range(len(S.streams[e]) - 1, -1, -1):
                r = S.streams[e][i]
                if r["fn"] is not None and r["dma"] is None:
                    evs.append(Ev("e", e, i))
                    break
        S.wait_all(SP, evs)
        S.emit()
        st.close()
        return nc

    slot_prev = [0, 1, 2, 3]
    spare = [4]

    prologue(xh_d, 0)
    fence()
    if STOP == 1:
        return finish()
    for c in range(4):
        wk, bwk = load_w(CH_K(3, c))
        sl = slot_prev[c]
        for w in range(4):
            bank = rPJ6.next()
            proj_fm(wk, bwk, bank, 512, xT_win(w))
            norm_evac(bank, 512, 5, K3[sl][:, 512 * w:512 * w + 512], [b_K3[sl]])
        wv, bwv = load_w(CH_V(3, c))
        proj_v_blocks(wv, bwv, [(r, 16, r) for r in range(16)], V3[sl], b_V3[sl], scale_flag=True)
        wk, bwk = load_w(CH_K(2, c))
        bank = rPJ6.next()
        proj_fm(wk, bwk, bank, 512, xT_win(3))
        norm_evac(bank, 512, 3, K2t[c][:, :], [b_K2t[c]])
        wv, bwv = load_w(CH_V(2, c))
        proj_v_blocks(wv, bwv, [(1536 + r, 4, r) for r in range(4)], V2t[c], b_V2t[c], scale_flag=True)
        wk, bwk = load_w(CH_K(1, c))
        bank = rPJ6.next()
        proj_fm(wk, bwk, bank, 128, lambda k: xT[:, k, SBT - 128:SBT])
        norm_evac(bank, 128, 1, K1t[c][:, :], [b_K1t[c]])
        wv, bwv = load_w(CH_V(1, c))
        proj_v_blocks(wv, bwv, [(SBT - 128, 1, 0)], V1t[c], b_V1t[c], scale_flag=True)
    for kvh in range(2):
        wk, bwk = load_w(CH_BK, dup=kvh)
        bank = rPJ6.next()
        proj_fm(wk, bwk, bank, 128, lambda k: xT[:, k, SBT - 128:SBT])
        norm_evac(bank, 128, 7, KBt[kvh][:, :], [b_KBt[kvh]])
        wv, bwv = load_w(CH_BV, dup=kvh)
        proj_v_blocks(wv, bwv, [(SBT - 128, 1, 0)], VBt[kvh], b_VBt[kvh], scale_flag=True)
    fence()
    if STOP == 2:
        return finish()

    def k_tail_out(dst_ap64or128, ncols):
        nb = rNB.next()
        S.op(PE, lambda e: e.transpose(PS[nb][:, 0:128], KF[:, 384:512], identF[:, :]), reads=[b_KF] + C, writes=[b_PS[nb]])
        S.op(ACT, lambda e: e.activation(out=KO[:, 0:128], in_=PS[nb][:, 0:128], func=AF.Copy), reads=[b_PS[nb]], writes=[b_KO])
        ev = S.dma(POOL, lambda e: e.dma_start(out=dst_ap64or128, in_=KO[:, 0:ncols]), reads=[b_KO], semowner=b_KO)
        out_evs.append(ev)

    for s in range(NSB):
        last = (s == NSB - 1)
        conv_eng[0] = POOL
        bulk_rate[0] = 2
        prologue(x_d, s * SBT)
        fence()
        for kvh in range(2):
            S.op(POOL, lambda e: e.tensor_copy(out=K1c[:, 0:128], in_=KBt[kvh][:, :]), reads=[b_KBt[kvh]], writes=[b_K1c])
            S.op(POOL, lambda e: e.tensor_copy(out=V1c[:, 0:1, :], in_=VBt[kvh][:, :, :]), reads=[b_VBt[kvh]], writes=[b_V1c])
            wk, bwk = load_w(CH_BK, dup=kvh)
            for w in range(4):
                bank = rPJ6.next()
                proj_fm(wk, bwk, bank, 512, xT_win(w))
                f32o = (KF[:, :], [b_KF]) if (last and w == 3) else None
                norm_evac(bank, 512, 7, K1c[:, 128 + 512 * w:128 + 512 * w + 512], [b_K1c], f32_out=f32o)
                if last and w == 3:
                    k_tail_out(pb_d[:, 64 * kvh:64 * kvh + 64], 64)
            wv, bwv = load_w(CH_BV, dup=kvh)
            vob = (lambda bi, kvh=kvh: (pb_d[:, 128 + 64 * kvh:128 + 64 * kvh + 64] if bi == 15 else None)) if last else None
            proj_v_blocks(wv, bwv, [(128 * b, 1, b + 1) for b in range(16)], V1c, b_V1c, vout=vob, vcols=64)
            S.op(POOL, lambda e: e.tensor_copy(out=KBt[kvh][:, :], in_=K1c[:, SBT:SBT + 128]), reads=[b_K1c], writes=[b_KBt[kvh]])
            S.op(POOL, lambda e: e.tensor_copy(out=VBt[kvh][:, :, :], in_=V1c[:, 16:17, :]), reads=[b_V1c], writes=[b_VBt[kvh]])
            for c in (2 * kvh, 2 * kvh + 1):
                silu_chunk(CH_ZB(c))
                wq, bwq = load_w(CH_BQ(c))
                def qprojb(w):
                    bank = rPJ.next()
                    proj_fm(wq, bwq, bank, 512, xT_win(w))
                    norm_evac(bank, 512, 6, QT[w % 2][:, :], [b_QT[w % 2]])
                qprojb(0)
                for w in range(4):
                    if w < 3:
                        qprojb(w + 1)
                    started = [False, False]
                    attn_window(w, (2 * c, 2 * c + 1), segs_g1(w, K1c, b_K1c, V1c, b_V1c, QT[w % 2], b_QT[w % 2], qoff=512 * w), started)
                    finish_window(w, c, bT, b_bT, sink_col=c)
        if STOP == 3:
            return finish()
        for c in range(4):
            silu_chunk(CH_ZA(c))
            S.op(POOL, lambda e: e.tensor_copy(out=K1c[:, 0:128], in_=K1t[c][:, :]), reads=[b_K1t[c]], writes=[b_K1c])
            S.op(POOL, lambda e: e.tensor_copy(out=V1c[:, 0:1, :], in_=V1t[c][:, :, :]), reads=[b_V1t[c]], writes=[b_V1c])
            S.op(POOL, lambda e: e.tensor_copy(out=K2c[:, 0:512], in_=K2t[c][:, :]), reads=[b_K2t[c]], writes=[b_K2c])
            S.op(POOL, lambda e: e.tensor_copy(out=V2c[:, 0:4, :], in_=V2t[c][:, :, :]), reads=[b_V2t[c]], writes=[b_V2c])
            slc = spare[0]; slp = slot_prev[c]
            plan = [(1, K1c, b_K1c, 128, 1), (2, K2c, b_K2c, 512, 3), (3, K3[slc], b_K3[slc], 0, 5)]
            for (g, Kd, bKd, koff, gcol) in plan:
                wk, bwk = load_w(CH_K(g, c))
                for w in range(4):
                    bank = rPJ6.next()
                    proj_fm(wk, bwk, bank, 512, xT_win(w))
                    need_out = last and (g == 3 or w == 3)
                    f32o = (KF[:, :], [b_KF]) if need_out else None
                    norm_evac(bank, 512, gcol, Kd[:, koff + 512 * w:koff + 512 * w + 512], [bKd], f32_out=f32o)
                    if need_out:
                        if g == 3:
                            k_out_rows(pa_d[2], 512 * w, 128, 4, 128 * c)
                        elif g == 2:
                            k_out_rows(pa_d[1], 0, 128, 4, 128 * c)
                        else:
                            k_tail_out(pa_d[0][:, 128 * c:128 * c + 128], 128)
            wv, bwv = load_w(CH_V(1, c))
            vo1 = (lambda bi, c=c: (pa_d[0][:, 512 + 128 * c:512 + 128 * c + 128] if bi == 15 else None)) if last else None
            proj_v_blocks(wv, bwv, [(128 * b, 1, b + 1) for b in range(16)], V1c, b_V1c, vout=vo1)
            wv, bwv = load_w(CH_V(2, c))

            def vo2f(bi, c=c):
                j, r = bi // 4, bi % 4
                if j != 3:
                    return None
                return pa_d[1][:, 512 + 128 * c:512 + 128 * c + 128].rearrange("(i r) c -> r i c", r=4)[r]
            proj_v_blocks(wv, bwv, [(512 * j + r, 4, 4 + 4 * j + r) for j in range(4) for r in range(4)], V2c, b_V2c,
                          vout=(vo2f if last else None))
            wv, bwv = load_w(CH_V(3, c))

            def vo3f(bi, c=c):
                return pa_d[2][:, 512 + 128 * c:512 + 128 * c + 128].rearrange("(i r) c -> r i c", r=16)[bi]
            proj_v_blocks(wv, bwv, [(r, 16, r) for r in range(16)], V3[slc], b_V3[slc], vout=(vo3f if last else None))
            wq = [load_w(CH_Q(g, c)) for g in (1, 2, 3)]

            def qproj(w):
                for gi in range(3):
                    bank = rPJ.next()
                    qi = 3 * (w % 2) + gi
                    proj_fm(wq[gi][0], wq[gi][1], bank, 512, xT_win(w))
                    norm_evac(bank, 512, 2 * gi, QT[qi][:, :], [b_QT[qi]])
            qproj(0)
            for w in range(4):
                started = [False, False]
                hg = (2 * c, 2 * c + 1)
                if w < 3:
                    qproj(w + 1)
                q0 = 3 * (w % 2)
                attn_window(w, hg, segs_g1(w, K1c, b_K1c, V1c, b_V1c, QT[q0], b_QT[q0], qoff=512 * w)
                            + segs_g2(w, K2c, b_K2c, V2c, b_V2c, QT[q0 + 1], b_QT[q0 + 1], qoff=512 * w)
                            + segs_g3(w, K3[slc], b_K3[slc], K3[slp], b_K3[slp], V3[slc], b_V3[slc], V3[slp], b_V3[slp],
                                      QT[q0 + 2], b_QT[q0 + 2], qoff=512 * w), started)
                finish_window(w, c, aT, b_aT)
            S.op(POOL, lambda e: e.tensor_copy(out=K1t[c][:, :], in_=K1c[:, SBT:SBT + 128]), reads=[b_K1c], writes=[b_K1t[c]])
            S.op(POOL, lambda e: e.tensor_copy(out=V1t[c][:, :, :], in_=V1c[:, 16:17, :]), reads=[b_V1c], writes=[b_V1t[c]])
            S.op(POOL, lambda e: e.tensor_copy(out=K2t[c][:, :], in_=K2c[:, SBT:SBT + 512]), reads=[b_K2c], writes=[b_K2t[c]])
            S.op(POOL, lambda e: e.tensor_copy(out=V2t[c][:, :, :], in_=V2c[:, 16:20, :]), reads=[b_V2c], writes=[b_V2t[c]])
            spare[0] = slp
            slot_prev[c] = slc
        fence()
        if STOP == 4:
            return finish()
        final_stage(s * SBT, SBT, x_d, y_d, xT, 4, 512)
        fence()
        if STOP == 5:
            return finish()


    if with_sample:
        conv_eng[0] = DVE
        a_off[0] = 0
        CT = [carve([128, 1024], F32) for _ in range(3)]; b_CT = [S.buf("ct%d" % i) for i in range(3)]
        TMP = [carve([128, 512], F32) for _ in range(4)]; b_TMP = [S.buf("tmp%d" % i) for i in range(4)]
        PBC = carve([128, 512], F32); b_PBC = S.buf("pbc")
        PBCs = [PBC, carve([128, 512], F32)]; b_PBCs = [S.buf("pbcs%d" % i) for i in range(2)]
        SC = carve([128, 16], F32); b_SC = S.buf("sc")
        SCs = [SC, carve([128, 16], F32)]; b_SCs = [S.buf("scs%d" % i) for i in range(2)]
        QS = carve([128, 16, NS], F32); b_QS = S.buf("qs")
        KS = carve([128, 14, NS], F32); b_KS = S.buf("ks")
        VS = carve([128, 14, NS], F32); b_VS = S.buf("vs")
        ZS = carve([128, 8, NS], F32); b_ZS = S.buf("zs")
        P0 = carve([128, 16, NS], F32); b_P0 = S.buf("p0")
        UT = carve([128, 8, NS], F32); b_UT = S.buf("ut")
        ZT = carve([128, 8, NS], F32); b_ZT = S.buf("zt")
        NR = carve([NS, 1024], F32); b_NR = S.buf("nr")
        NRB = carve([NS, 256], F32); b_NRB = S.buf("nrb")
        XSS = carve([NS, 1024], F32); b_XSS = S.buf("xss")
        XSB = carve([NS, 1024], BF16); b_XSB = S.buf("xsb")
        onesF = carve([128, 1], F32)
        bonesF = carve([128, 128], F32)
        assert a_off[0] <= ARENA_BYTES
        S.op(DVE, lambda e: e.memset(onesF, 1.0), writes=[b_SC])
        S.dma(SP, lambda e: e.dma_start(out=bonesF, in_=bones_d), writes=[b_PBC], semowner=b_PBC)
        S.dma(SP, lambda e: e.dma_start(out=XSS, in_=xs_d), writes=[b_XSS], semowner=b_XSS)
        S.op(ACT, lambda e: e.activation(out=XSB, in_=XSS, func=AF.Square, accum_out=stat[0:NS, 0:1]), reads=[b_XSS], writes=[b_XSB, b_stat])
        S.op(ACT, lambda e: e.activation(out=stat[0:NS, 1:2], in_=stat[0:NS, 0:1], func=AF.Ln, bias=epsS[0:NS, 0:1], scale=1.0 / D), reads=[b_stat] + C, writes=[b_stat])
        S.op(ACT, lambda e: e.activation(out=stat[0:NS, 2:3], in_=stat[0:NS, 1:2], func=AF.Exp, scale=-0.5), reads=[b_stat], writes=[b_stat])
        S.op(DVE, lambda e: e.tensor_scalar(out=XSB, in0=XSS, scalar1=stat[0:NS, 2:3], scalar2=None, op0=ALU.mult), reads=[b_XSS, b_stat], writes=[b_XSB])
        bank = rNB.next()
        pbf = PS[bank][:, :].bitcast(BF16)
        for k in range(8):
            S.op(PE, lambda e, k=k: e.transpose(pbf[:, NS * k:NS * k + NS], XSB[:, 128 * k:128 * k + 128], identB[0:NS, 0:NS]),
                 reads=[b_XSB] + C, writes=[b_PS[bank]])
        S.op(DVE, lambda e: e.tensor_copy(out=xT[:, :, 0:NS], in_=pbf[:, 0:8 * NS].rearrange("p (k t) -> p k t", t=NS)),
             reads=[b_PS[bank]], writes=[b_xT])
        xs_rhs = lambda k: xT[:, k, 0:NS]

        def proj_s(ch, dup=None):
            wq_, bw_ = load_w(ch, dup=dup)
            bank = rPJ.next()
            proj_fm(wq_, bw_, bank, NS, xs_rhs)
            return bank

        def norm_s(bank, gcol, dst, bdst):
            ti = 0
            S.op(ACT, lambda e: e.activation(out=TMP[ti][:, 0:NS], in_=PS[bank][:, 0:NS], func=AF.Square), reads=[b_PS[bank]], writes=[b_TMP[ti]])
            nb = rNB.next()
            S.op(PE, lambda e: e.matmul(PS[nb][:, 0:NS], lhsT=bonesF, rhs=TMP[ti][:, 0:NS], start=True, stop=True), reads=[b_TMP[ti], b_PBC], writes=[b_PS[nb]])
            S.op(ACT, lambda e: e.activation(out=TMP[ti][:, 0:NS], in_=PS[nb][:, 0:NS], func=AF.Ln, bias=epsS[:, 0:1], scale=1.0), reads=[b_PS[nb]] + C, writes=[b_TMP[ti]])
            S.op(ACT, lambda e: e.activation(out=TMP[ti][:, 0:NS], in_=TMP[ti][:, 0:NS], func=AF.Exp, scale=-0.5), reads=[b_TMP[ti]], writes=[b_TMP[ti]])
            S.op(DVE, lambda e: e.scalar_tensor_tensor(out=dst, in0=PS[bank][:, 0:NS], scalar=qkgS[:, gcol:gcol + 1], in1=TMP[ti][:, 0:NS], op0=ALU.mult, op1=ALU.mult),
                 reads=[b_PS[bank], b_TMP[ti]] + C, writes=[bdst])

        for g in (1, 2, 3):
            for c in range(4):
                norm_s(proj_s(CH_Q(g, c)), 2 * (g - 1), QS[:, 4 * (g - 1) + c, :], b_QS)
                norm_s(proj_s(CH_K(g, c)), 2 * (g - 1) + 1, KS[:, 4 * (g - 1) + c, :], b_KS)
                bank = proj_s(CH_V(g, c))
                S.op(DVE, lambda e, g=g, c=c, bank=bank: e.tensor_copy(out=VS[:, 4 * (g - 1) + c, :], in_=PS[bank][:, 0:NS]), reads=[b_PS[bank]], writes=[b_VS])
        for c in range(4):
            norm_s(proj_s(CH_BQ(c)), 6, QS[:, 12 + c, :], b_QS)
        for kvh in range(2):
            norm_s(proj_s(CH_BK, dup=kvh), 7, KS[:, 12 + kvh, :], b_KS)
            bank = proj_s(CH_BV, dup=kvh)
            S.op(DVE, lambda e, kvh=kvh, bank=bank: e.tensor_copy(out=VS[:, 12 + kvh, :], in_=PS[bank][:, 0:NS]), reads=[b_PS[bank]], writes=[b_VS])
        for i, ch in enumerate([CH_ZA(c) for c in range(4)] + [CH_ZB(c) for c in range(4)]):
            bank = proj_s(ch)
            ti = 1
            S.op(ACT, lambda e, bank=bank: e.activation(out=TMP[ti][:, 0:NS], in_=PS[bank][:, 0:NS], func=AF.Exp, scale=-1.0), reads=[b_PS[bank]], writes=[b_TMP[ti]])
            S.op(ACT, lambda e: e.activation(out=TMP[ti][:, 0:NS], in_=TMP[ti][:, 0:NS], func=AF.Ln, bias=oneS[:, 0:1], scale=1.0), reads=[b_TMP[ti]] + C, writes=[b_TMP[ti]])
            S.op(ACT, lambda e: e.activation(out=TMP[ti][:, 0:NS], in_=TMP[ti][:, 0:NS], func=AF.Exp, scale=-1.0), reads=[b_TMP[ti]], writes=[b_TMP[ti]])
            S.op(DVE, lambda e, i=i, bank=bank: e.tensor_tensor(out=ZS[:, i, :], in0=PS[bank][:, 0:NS], in1=TMP[ti][:, 0:NS], op=ALU.mult),
                 reads=[b_PS[bank], b_TMP[ti]], writes=[b_ZS])

        for g in range(3):
            nb = rNB.next()
            for c in range(4):
                S.op(PE, lambda e, g=g, c=c: e.transpose(PS[nb][0:NS, 128 * c:128 * c + 128], KS[:, 4 * g + c, :], identF[:, :]), reads=[b_KS] + C, writes=[b_PS[nb]])
            S.op(DVE, lambda e: e.tensor_copy(out=NR[:, 0:512], in_=PS[nb][0:NS, :]), reads=[b_PS[nb]], writes=[b_NR])
            nb2 = rNB.next()
            for c in range(4):
                S.op(PE, lambda e, g=g, c=c: e.transpose(PS[nb2][0:NS, 128 * c:128 * c + 128], VS[:, 4 * g + c, :], identF[:, :]), reads=[b_VS] + C, writes=[b_PS[nb2]])
            S.op(DVE, lambda e: e.tensor_copy(out=NR[:, 512:1024], in_=PS[nb2][0:NS, :]), reads=[b_PS[nb2]], writes=[b_NR])
            L = (128, 512, 2048)[g]
            ev = S.dma(SP, lambda e, g=g, L=L: e.dma_start(out=sa_d[g][:, L - 1, :], in_=NR[:, :]), reads=[b_NR], semowner=b_NR)
            out_evs.append(ev)
        nb = rNB.next()
        for kvh in range(2):
            S.op(PE, lambda e, kvh=kvh: e.transpose(PS[nb][0:NS, 128 * kvh:128 * kvh + 128], KS[:, 12 + kvh, :], identF[:, :]), reads=[b_KS] + C, writes=[b_PS[nb]])
            S.op(PE, lambda e, kvh=kvh: e.transpose(PS[nb][0:NS, 256 + 128 * kvh:256 + 128 * kvh + 128], VS[:, 12 + kvh, :], identF[:, :]), reads=[b_VS] + C, writes=[b_PS[nb]])
        S.op(DVE, lambda e: e.tensor_copy(out=NRB[:, :].rearrange("p (a b) -> p a b", b=64),
                                          in_=PS[nb][0:NS, :].rearrange("p (a b) -> p a b", b=128)[:, :, 0:64]), reads=[b_PS[nb]], writes=[b_NRB])
        ev = S.dma(SP, lambda e: e.dma_start(out=sb_d[:, 127, :], in_=NRB[:, :]), reads=[b_NRB], semowner=b_NRB)
        out_evs.append(ev)

        for qc in range(16):
            kc = qc if qc < 12 else 12 + (qc - 12) // 2
            S.op(DVE, lambda e, qc=qc, kc=kc: e.tensor_tensor(out=TMP[0][:, 0:NS], in0=QS[:, qc, :], in1=KS[:, kc, :], op=ALU.mult), reads=[b_QS, b_KS], writes=[b_TMP[0]])
            nb = rNB.next()
            S.op(PE, lambda e: e.matmul(PS[nb][:, 0:NS], lhsT=bonesF, rhs=TMP[0][:, 0:NS], start=True, stop=True), reads=[b_TMP[0], b_PBC], writes=[b_PS[nb]])
            S.op(ACT, lambda e, qc=qc: e.activation(out=P0[:, qc, :], in_=PS[nb][:, 0:NS], func=AF.Exp, scale=8.0), reads=[b_PS[nb]], writes=[b_P0])

        PU, PZ = UA, UB
        first_u = [True]
        rCT = Rot([0, 1, 2]); rTMP = Rot([0, 1, 2, 3]); rPBC = Rot([0, 1]); rSC = Rot([0, 1])

        def stage1(n, g):
            ci = rCT.next()
            if g < 3:
                dil = (1, 4, 16)[g]; L = (128, 512, 2048)[g]
                S.dma(SP, lambda e: e.dma_start(out=CT[ci][:, :], in_=ca_d[g][n, 0:L:dil, :]), writes=[b_CT[ci]], semowner=b_CT[ci])
                kview = CT[ci][:, 0:512]
                vview = CT[ci][:, 512:1024]
            else:
                S.dma(SP, lambda e: e.dma_start(out=CT[ci][:, 0:256], in_=cb_d[n, :, :]), writes=[b_CT[ci]], semowner=b_CT[ci])
                kview = CT[ci][:, 0:128].rearrange("p (a d) -> p a d", d=64).unsqueeze(2).to_broadcast([128, 2, 4, 64])
                vview = CT[ci][:, 128:256].rearrange("p (a d) -> p a d", d=64).unsqueeze(2).to_broadcast([128, 2, 4, 64])
            qb_bank = rST.next()
            for c in range(4):
                S.op(PE, lambda e, c=c: e.matmul(PS[qb_bank][:, 128 * c:128 * c + 128], lhsT=QS[:, 4 * g + c, n:n + 1].to_broadcast([128, 128]),
                                                 rhs=identF[:, :], start=True, stop=True), reads=[b_QS] + C, writes=[b_PS[qb_bank]])
            ti = rTMP.next(); sci = rSC.next(); pbi = rPBC.next()
            SCv = SCs[sci]; PBv = PBCs[pbi]
            if g < 3:
                S.op(DVE, lambda e: e.tensor_tensor(out=TMP[ti][:, :], in0=kview, in1=PS[qb_bank][:, :], op=ALU.mult),
                     reads=[b_CT[ci], b_PS[qb_bank]], writes=[b_TMP[ti]])
            else:
                S.op(DVE, lambda e: e.tensor_tensor(out=TMP[ti][:, :].rearrange("p (a r d) -> p a r d", r=4, d=64), in0=kview,
                                                    in1=PS[qb_bank][:, :].rearrange("p (a r d) -> p a r d", r=4, d=64), op=ALU.mult),
                     reads=[b_CT[ci], b_PS[qb_bank]], writes=[b_TMP[ti]])
            S.op(DVE, lambda e: e.tensor_reduce(out=SCv[:, 0:8], in_=TMP[ti][:, :].rearrange("p (h d) -> p h d", d=64), axis=AX.X, op=ALU.add),
                 reads=[b_TMP[ti]], writes=[b_SCs[sci]])
            S.op(DVE, lambda e: e.scalar_tensor_tensor(out=SCv[:, 0:8], in0=SCv[:, 0:8], scalar=0.125, in1=dbiasS[:, :], op0=ALU.mult, op1=ALU.add),
                 reads=[b_SCs[sci]] + C, writes=[b_SCs[sci]])
            S.op(ACT, lambda e: e.activation(out=SCv[:, 8:16], in_=SCv[:, 0:8], func=AF.Exp), reads=[b_SCs[sci]], writes=[b_SCs[sci]])
            S.op(DVE, lambda e: e.tensor_copy(out=PBv[:, :].rearrange("p (h d) -> p h d", d=64), in_=SCv[:, 8:16].unsqueeze(2).to_broadcast([128, 8, 64])),
                 reads=[b_SCs[sci]], writes=[b_PBCs[pbi]])
            ti2 = rTMP.next()
            if g < 3:
                S.op(DVE, lambda e: e.tensor_tensor(out=TMP[ti2][:, :], in0=vview, in1=PBv[:, :], op=ALU.mult),
                     reads=[b_CT[ci], b_PBCs[pbi]], writes=[b_TMP[ti2]])
            else:
                S.op(DVE, lambda e: e.tensor_tensor(out=TMP[ti2][:, :].rearrange("p (a r d) -> p a r d", r=4, d=64), in0=vview,
                                                    in1=PBv[:, :].rearrange("p (a r d) -> p a r d", r=4, d=64), op=ALU.mult),
                     reads=[b_CT[ci], b_PBCs[pbi]], writes=[b_TMP[ti2]])
            return (n, g, ti2, pbi)

        def stage2(n, g, ti2, pbi):
            colbase = 0 if g < 3 else 4
            for c in range(4):
                col = (colbase + c) * NS + n
                fu = first_u[0]
                S.op(PE, lambda e, c=c, col=col, fu=fu: e.matmul(PS[PU][:, col:col + 1], lhsT=TMP[ti2][:, 128 * c:128 * c + 128], rhs=onesF,
                                                                 start=fu, stop=False, skip_group_check=True), reads=[b_TMP[ti2], b_SC], writes=[b_PS[PU]])
                S.op(PE, lambda e, c=c, col=col, fu=fu: e.matmul(PS[PZ][:, col:col + 1], lhsT=PBCs[pbi][:, 128 * c:128 * c + 128], rhs=onesF,
                                                                 start=fu, stop=False, skip_group_check=True), reads=[b_PBCs[pbi], b_SC], writes=[b_PS[PZ]])
                first_u[0] = False

        prev_u = None
        for n in range(NS):
            for g in range(4):
                cur = stage1(n, g)
                if prev_u is not None:
                    stage2(*prev_u)
                prev_u = cur
        stage2(*prev_u)
        S.op(DVE, lambda e: e.tensor_copy(out=UT[:, :, :].rearrange("p a b -> p (a b)"), in_=PS[PU][:, 0:8 * NS]), reads=[b_PS[PU]], writes=[b_UT])
        S.op(DVE, lambda e: e.tensor_copy(out=ZT[:, :, :].rearrange("p a b -> p (a b)"), in_=PS[PZ][:, 0:8 * NS]), reads=[b_PS[PZ]], writes=[b_ZT])
        for qc in range(16):
            uc = (qc % 4) if qc < 12 else 4 + (qc - 12)
            vc = qc if qc < 12 else 12 + (qc - 12) // 2
            S.op(DVE, lambda e, qc=qc, vc=vc: e.tensor_tensor(out=TMP[0][:, 0:NS], in0=P0[:, qc, :], in1=VS[:, vc, :], op=ALU.mult), reads=[b_P0, b_VS], writes=[b_TMP[0]])
            S.op(DVE, lambda e, uc=uc: e.tensor_tensor(out=UT[:, uc, :], in0=UT[:, uc, :], in1=TMP[0][:, 0:NS], op=ALU.add), reads=[b_UT, b_TMP[0]], writes=[b_UT])
            S.op(DVE, lambda e, uc=uc, qc=qc: e.tensor_tensor(out=ZT[:, uc, :], in0=ZT[:, uc, :], in1=P0[:, qc, :], op=ALU.add), reads=[b_ZT, b_P0], writes=[b_ZT])
        for c in range(4):
            S.op(DVE, lambda e, c=c: e.tensor_scalar(out=ZT[:, 4 + c, :], in0=ZT[:, 4 + c, :], scalar1=skS[:, c:c + 1], scalar2=None, op0=ALU.add), reads=[b_ZT] + C, writes=[b_ZT])
        S.op(DVE, lambda e: e.reciprocal(out=ZT[:, :, :], in_=ZT[:, :, :]), reads=[b_ZT], writes=[b_ZT])
        S.op(DVE, lambda e: e.tensor_tensor(out=UT[:, :, :], in0=UT[:, :, :], in1=ZT[:, :, :], op=ALU.mult), reads=[b_UT, b_ZT], writes=[b_UT])
        S.op(DVE, lambda e: e.tensor_tensor(out=aT[:, :, 0:NS], in0=UT[:, 0:4, :], in1=ZS[:, 0:4, :], op=ALU.mult), reads=[b_UT, b_ZS], writes=b_aT)
        S.op(DVE, lambda e: e.tensor_tensor(out=bT[:, :, 0:NS], in0=UT[:, 4:8, :], in1=ZS[:, 4:8, :], op=ALU.mult), reads=[b_UT, b_ZS], writes=b_bT)
        fence()
        final_stage(0, NS, xs_d, ys_d, xT, 1, NS)
        fence()

    return finish()

N_CORES = 8
NSB_FULL = 2
NS_FULL = 16
_prog_cache = {}


def _consts():
    ident = np.eye(128, dtype=np.float32)
    bones = np.zeros((128, 128), np.float32)
    bones[:64, :64] = 1.0 / 64
    bones[64:, 64:] = 1.0 / 64
    swapm = np.zeros((128, 128), np.float32)
    for m in range(128):
        swapm[(m + 64) % 128, m] = 1.0
    kk = np.arange(128)[:, None].astype(np.float64)
    qq = np.arange(128)[None, :].astype(np.float64)
    masks = np.zeros((128, 8, 256), np.float32)
    for h in range(8):
        m = 2.0 ** -(h + 1)
        md = np.where(qq >= kk, np.exp(-m * (qq - kk)), 0.0)
        mp = np.where(kk >= qq, np.exp(-m * (qq - kk + 128)), 0.0)
        masks[:, h, :128] = md
        masks[:, h, 128:] = mp
    dbias = np.zeros((128, 8), np.float32)
    for h in range(8):
        dbias[:, h] = -(2.0 ** -(h + 1)) * (128 - np.arange(128))
    return ident, bones, swapm, masks.reshape(128, 2048), dbias


def make_in_maps(inp, n_cores, NSB, NS, seq_len):
    f = lambda a: np.ascontiguousarray(a, dtype=np.float32)
    w_in = inp["w_in"][0]
    win = f(w_in.reshape(8, 128, NCH, 128).transpose(2, 1, 0, 3).reshape(NCH, 128, 1024))
    gain = f(inp["norm_gain"][0].reshape(8, 128).T)
    wba = f(inp["w_branch_a"][0].reshape(4, 128, 1024).transpose(1, 0, 2).reshape(128, 4096))
    wbb = f(inp["w_branch_b"][0].reshape(4, 128, 1024).transpose(1, 0, 2).reshape(128, 4096))
    wout = f(inp["w_out"][0].reshape(8, 128, 1024).transpose(1, 0, 2).reshape(128, 8192))
    qa = inp["qk_norm_a"][0]
    qb = inp["qk_norm_b"][0]
    cols = [qa[0, 0], qa[0, 1], qa[1, 0], qa[1, 1], qa[2, 0], qa[2, 1], qb[0], qb[1]]
    qkg = f(np.stack([np.tile(c, 2) for c in cols], axis=1))
    sinks = inp["b_sinks"][0]
    sk = f(np.stack([np.repeat(sinks[2 * c:2 * c + 2], 64) for c in range(4)], axis=1))
    ident, bones, swapm, masks, dbias = _consts()
    T = NSB * SBT
    halves = seq_len // T
    maps = []
    for core in range(n_cores):
        b, hf = core // halves, core % halves
        x = inp["x_prompt"][b, hf * T:(hf + 1) * T]
        if hf == 0:
            xh = np.zeros((SBT, D), np.float32)
            flag = np.zeros((128, 1), np.float32)
        else:
            xh = inp["x_prompt"][b, hf * T - SBT:hf * T]
            flag = np.ones((128, 1), np.float32)
        sl = slice(core * NS, (core + 1) * NS)
        m = dict(x=f(x), xh=f(xh), flag=flag, xs=f(inp["x_sample"][sl, 0]), win=win, gain=gain, wba=wba, wbb=wbb, wout=wout,
                 qkg=qkg, sk=sk, ident=ident, bones=bones, swapm=swapm, masks=masks, dbias=dbias,
                 ca1=f(inp["cache_a1_kv"][0, sl].reshape(NS, 128, 1024)),
                 ca2=f(inp["cache_a2_kv"][0, sl].reshape(NS, 512, 1024)),
                 ca3=f(inp["cache_a3_kv"][0, sl].reshape(NS, 2048, 1024)),
                 cb=f(inp["cache_b_kv"][0, sl].reshape(NS, 128, 256)))
        maps.append(m)
    return maps


def kernel(**inputs):
    key = (NSB_FULL, NS_FULL)
    if key not in _prog_cache:
        _prog_cache[key] = build_program(NSB_FULL, NS_FULL)
    nc = _prog_cache[key]
    maps = make_in_maps(inputs, N_CORES, NSB_FULL, NS_FULL, 8192)
    res = run_bass_kernel_spmd(nc, maps, core_ids=list(range(N_CORES)))
    R = res.results
    T = NSB_FULL * SBT
    y = np.stack([np.concatenate([R[2 * b]["y"], R[2 * b + 1]["y"]], axis=0) for b in range(4)], axis=0)
    ys = np.concatenate([R[c]["ys"] for c in range(N_CORES)], axis=0)[:, None, :]
    pa = [np.stack([R[2 * b + 1][n].reshape(-1, 2, 8, 64) for b in range(4)], axis=0)[None] for n in ("pa1", "pa2", "pa3")]
    pb = np.stack([R[2 * b + 1]["pb"].reshape(-1, 2, 2, 64) for b in range(4)], axis=0)[None]
    sa = [np.concatenate([R[c][n] for c in range(N_CORES)], axis=0).reshape(128, -1, 2, 8, 64)[None] for n in ("sa1", "sa2", "sa3")]
    sbo = np.concatenate([R[c]["sb"] for c in range(N_CORES)], axis=0).reshape(128, -1, 2, 2, 64)[None]
    return (y.astype(np.float32), ys.astype(np.float32), pa[0], pa[1], pa[2], pb, sa[0], sa[1], sa[2], sbo)
```

```python
import numpy as np
from contextlib import ExitStack
import concourse.bass as bass
import concourse.mybir as mybir
from concourse.bass_utils import run_bass_kernel_spmd

F32 = mybir.dt.float32
BF16 = mybir.dt.bfloat16
ALU = mybir.AluOpType
AF = mybir.ActivationFunctionType
AX = mybir.AxisListType

PE, ACT, DVE, POOL, SP = "tensor", "scalar", "vector", "gpsimd", "sync"
ENGS = (PE, ACT, DVE, POOL, SP)


class Buf:
    __slots__ = ("name", "w", "r", "dsem", "dcnt", "excl")

    def __init__(self, name, excl=False):
        self.name = name
        self.excl = excl
        self.w = None
        self.r = {}
        self.dsem = None
        self.dcnt = 0


class Ev:
    __slots__ = ("kind", "a", "b")

    def __init__(self, kind, a, b):
        self.kind, self.a, self.b = kind, a, b


class _Rec:
    def __init__(self):
        self.call = None

    def __getattr__(self, name):
        def f(*args, **kwargs):
            self.call = (name, args, kwargs)
            return self
        return f


def _bind(fn):
    r = _Rec()
    fn(r)
    assert r.call is not None
    name, args, kwargs = r.call
    return lambda eng: getattr(eng, name)(*args, **kwargs)


class Sched:
    def __init__(self, nc, stack):
        self.nc = nc
        self.stack = stack
        self.streams = {e: [] for e in ENGS}
        self.esem = {e: stack.enter_context(nc.semaphore("es_" + e)) for e in ENGS}
        self.nsem = len(ENGS)
        self.bulk_sem = stack.enter_context(nc.semaphore("bulk"))
        self.bulk_cnt = 0
        self.final_waits = []
        self.recent_dma = []

    def buf(self, name, excl=False):
        return Buf(name, excl)

    def _deps(self, eng, reads, writes):
        deps = []
        for b in reads:
            if b.w is not None:
                deps.append(b.w)
            if b.excl:
                deps.extend(v for k, v in b.r.items() if k != eng)
        for b in writes:
            if b.w is not None:
                deps.append(b.w)
            deps.extend(b.r.values())
        out = []
        seen = {}
        for d in deps:
            if d.kind == "e":
                if d.a == PE and eng == PE:
                    continue
                k = ("e", d.a)
                if k not in seen or seen[k].b < d.b:
                    seen[k] = d
            else:
                k = ("d", id(d.a))
                if k not in seen or seen[k].b < d.b:
                    seen[k] = d
        return list(seen.values())

    def op(self, eng, fn, reads=(), writes=()):
        deps = self._deps(eng, reads, writes)
        idx = len(self.streams[eng])
        fn = _bind(fn)
        rec = {"fn": fn, "deps": deps, "needed": False, "dma": None}
        self.streams[eng].append(rec)
        for d in deps:
            if d.kind == "e":
                self.streams[d.a][d.b]["needed"] = True
        ev = Ev("e", eng, idx)
        for b in reads:
            b.r[eng] = ev
        for b in writes:
            b.w = ev
            b.r = {}
        return ev

    def dma(self, q, fn, reads=(), writes=(), semowner=None):
        deps = self._deps(q, reads, writes)
        for d in deps:
            if d.kind == "e":
                self.streams[d.a][d.b]["needed"] = True
        if semowner is None:
            sem = self.bulk_sem
            self.bulk_cnt += 16
            val = self.bulk_cnt
        else:
            if semowner.dsem is None:
                semowner.dsem = self.stack.enter_context(self.nc.semaphore("ds_" + semowner.name))
                self.nsem += 1
            semowner.dcnt += 16
            sem, val = semowner.dsem, semowner.dcnt
        fn = _bind(fn)
        rec = {"fn": fn, "deps": deps, "needed": False, "dma": (sem, 16)}
        self.streams[q].append(rec)
        ev = Ev("d", sem, val)
        if semowner is not None:
            self.recent_dma.append(ev)
        for b in reads:
            b.r[("d", id(sem))] = ev
        for b in writes:
            b.w = ev
            b.r = {}
        return ev

    def wait_all(self, eng, evs):
        best = {}
        for d in evs:
            k = ("e", d.a) if d.kind == "e" else ("d", id(d.a))
            if k not in best or best[k].b < d.b:
                best[k] = d
        evs = list(best.values())
        rec = {"fn": None, "deps": list(evs), "needed": False, "dma": None}
        for d in evs:
            if d.kind == "e":
                self.streams[d.a][d.b]["needed"] = True
        self.streams[eng].append(rec)

    def emit(self):
        nc = self.nc
        val = {}
        for e in ENGS:
            c = 0
            for i, rec in enumerate(self.streams[e]):
                if rec["dma"] is None and rec["fn"] is not None and rec["needed"]:
                    c += 1
                    val[(e, i)] = c
        self.maxval = {e: max([v for (ee, _), v in val.items() if ee == e], default=0) for e in ENGS}
        with nc.Block() as block:
            def run(eng_name):
                def body(eng):
                    waited = {}
                    for i, rec in enumerate(self.streams[eng_name]):
                        for d in rec["deps"]:
                            if d.kind == "e":
                                sem, v = self.esem[d.a], val[(d.a, d.b)]
                            else:
                                sem, v = d.a, d.b
                            k = id(sem)
                            if waited.get(k, 0) >= v:
                                continue
                            waited[k] = v
                            eng.wait_ge(sem, v)
                        if rec["fn"] is None:
                            continue
                        ins = rec["fn"](eng)
                        if rec["dma"] is not None:
                            ins.then_inc(rec["dma"][0], 16)
                        elif rec["needed"]:
                            ins.then_inc(self.esem[eng_name], 1)
                return body
            block.tensor(run(PE))
            block.scalar(run(ACT))
            block.vector(run(DVE))
            block.gpsimd(run(POOL))
            block.sync(run(SP))

D = 1024
SBT = 2048
NCH = 66
EPS = 1e-6


def CH_Q(g, c): return 12 * (g - 1) + c
def CH_K(g, c): return 12 * (g - 1) + 4 + c
def CH_V(g, c): return 12 * (g - 1) + 8 + c
def CH_BQ(c): return 36 + c
CH_BK, CH_BV = 40, 41
def CH_ZA(c): return 42 + c
def CH_ZB(c): return 46 + c
def CH_GA(j): return 50 + j
def CH_GB(j): return 58 + j


class Rot:
    def __init__(self, items):
        self.items, self.i = items, 0

    def next(self):
        it = self.items[self.i % len(self.items)]
        self.i += 1
        return it


def build_program(NSB, NS, with_sample=True, with_copy=True):
    nc = bass.Bass("TRN2", target_bir_lowering=False)
    T = NSB * SBT

    def din(name, shape):
        return nc.dram_tensor(name, list(shape), F32, kind="ExternalInput").ap()

    def dout(name, shape):
        return nc.dram_tensor(name, list(shape), F32, kind="ExternalOutput").ap()

    x_d = din("x", [T, D]); xh_d = din("xh", [SBT, D]); flag_d = din("flag", [128, 1])
    xs_d = din("xs", [NS, D])
    win_d = din("win", [NCH, 128, 8 * 128])
    gain_d = din("gain", [128, 8])
    wba_d = din("wba", [128, 4 * 1024]); wbb_d = din("wbb", [128, 4 * 1024]); wout_d = din("wout", [128, 8 * 1024])
    qkg_d = din("qkg", [128, 8]); sk_d = din("sk", [128, 4])
    ident_d = din("ident", [128, 128]); bones_d = din("bones", [128, 128]); swap_d = din("swapm", [128, 128])
    masks_d = din("masks", [128, 8 * 256])
    dbias_d = din("dbias", [128, 8])
    ca_d = [din("ca1", [NS, 128, 1024]), din("ca2", [NS, 512, 1024]), din("ca3", [NS, 2048, 1024])]
    cb_d = din("cb", [NS, 128, 256])
    y_d = dout("y", [T, D]); ys_d = dout("ys", [NS, D])
    pa_d = [dout("pa1", [128, 1024]), dout("pa2", [512, 1024]), dout("pa3", [2048, 1024])]
    pb_d = dout("pb", [128, 256])
    sa_d = [dout("sa1", [NS, 128, 1024]), dout("sa2", [NS, 512, 1024]), dout("sa3", [NS, 2048, 1024])]
    sb_d = dout("sb", [NS, 128, 256])

    scr_g = nc.dram_tensor("scr_g", [16, 128, 1024], BF16).ap()
    scr_o = nc.dram_tensor("scr_o", [2, 128, 4096], BF16).ap()
    st = ExitStack()
    S = Sched(nc, st)
    out_evs = []

    def sb(name, shape, dt=F32):
        return nc.alloc_sbuf_tensor(name, list(shape), dt)

    xT = sb("xT", [128, 8, SBT], BF16); b_xT = S.buf("xT")
    WST = [sb("wst%d" % i, [128, 8, 128], F32) for i in range(2)]; b_WST = [S.buf("wst%d" % i) for i in range(2)]
    WBF = [sb("wbf%d" % i, [128, 8, 128], BF16) for i in range(4)]; b_WBF = [S.buf("wbf%d" % i) for i in range(4)]
    K3 = [sb("k3_%d" % i, [128, SBT], BF16) for i in range(5)]; b_K3 = [S.buf("k3_%d" % i) for i in range(5)]
    V3 = [sb("v3_%d" % i, [128, 16, 192], BF16) for i in range(5)]; b_V3 = [S.buf("v3_%d" % i) for i in range(5)]
    K2t = [sb("k2t%d" % i, [128, 512], BF16) for i in range(4)]; b_K2t = [S.buf("k2t%d" % i) for i in range(4)]
    V2t = [sb("v2t%d" % i, [128, 4, 192], BF16) for i in range(4)]; b_V2t = [S.buf("v2t%d" % i) for i in range(4)]
    K1t = [sb("k1t%d" % i, [128, 128], BF16) for i in range(4)]; b_K1t = [S.buf("k1t%d" % i) for i in range(4)]
    V1t = [sb("v1t%d" % i, [128, 1, 192], BF16) for i in range(4)]; b_V1t = [S.buf("v1t%d" % i) for i in range(4)]
    KBt = [sb("kbt%d" % i, [128, 128], BF16) for i in range(2)]; b_KBt = [S.buf("kbt%d" % i) for i in range(2)]
    VBt = [sb("vbt%d" % i, [128, 1, 192], BF16) for i in range(2)]; b_VBt = [S.buf("vbt%d" % i) for i in range(2)]
    aT = sb("aT", [128, 4, SBT], BF16); b_aT = [S.buf("aT%d" % i) for i in range(4)]
    bT = sb("bT", [128, 4, SBT], BF16); b_bT = [S.buf("bT%d" % i) for i in range(4)]
    MASK = sb("mask", [128, 8, 256], BF16); b_const = S.buf("const")
    identF = sb("identF", [128, 128], F32); identB = sb("identB", [128, 128], BF16)
    bonesB = sb("bonesB", [128, 128], BF16); swapF = sb("swapF", [128, 128], F32)
    gainS = sb("gainS", [128, 8], F32); qkgS = sb("qkgS", [128, 8], F32); skS = sb("skS", [128, 4], F32)
    flagS = sb("flagS", [128, 1], F32); epsS = sb("epsS", [128, 1], F32); oneS = sb("oneS", [128, 1], F32)
    onesB = sb("onesB", [128, 64], BF16)
    dbiasS = sb("dbiasS", [128, 8], F32)
    stat = sb("stat", [128, 8], F32); b_stat = S.buf("stat")
    stats4 = [sb("stat4_%d" % i, [128, 4], F32) for i in range(4)]; b_stats4 = [S.buf("stat4_%d" % i) for i in range(4)]

    ARENA_BYTES = 56 * 1024 + 512
    arena = sb("arena", [128, ARENA_BYTES // 2], BF16)
    a_off = [0]

    def carve(shape, dt):
        n = int(np.prod(shape[1:]))
        nb = n * (4 if dt == F32 else 2)
        nb_al = (nb + 63) // 64 * 64
        o = a_off[0]
        assert o + nb_al <= ARENA_BYTES, (o, nb_al)
        a_off[0] += nb_al
        ap = arena[0:shape[0], o // 2:(o + nb) // 2]
        if dt == F32:
            ap = ap.bitcast(F32)
        if len(shape) == 3:
            ap = ap.rearrange("p (a b) -> p a b", b=shape[2])
        return ap

    a_off[0] = 0
    XST = [carve([128, 1024], F32) for _ in range(4)]; b_XST = [S.buf("xst%d" % i) for i in range(4)]
    XB = [carve([128, 1024], BF16) for _ in range(4)]; b_XB = [S.buf("xb%d" % i) for i in range(4)]
    cst = carve([128, 8 * 256], F32)
    a_off[0] = 0
    K2c = carve([128, 512 + SBT], BF16); b_K2c = S.buf("k2c")
    V2c = carve([128, 20, 192], BF16); b_V2c = S.buf("v2c")
    K1c = carve([128, 128 + SBT], BF16); b_K1c = S.buf("k1c")
    V1c = carve([128, 17, 192], BF16); b_V1c = S.buf("v1c")
    QT = [carve([128, 512], BF16) for _ in range(6)]; b_QT = [S.buf("qt%d" % i) for i in range(6)]
    EXPT = [carve([128, 512], F32) for _ in range(2)]; b_EXPT = [S.buf("expt%d" % i) for i in range(2)]
    PT = [carve([128, 512], BF16) for _ in range(2)]; b_PT = [S.buf("pt%d" % i) for i in range(2)]
    SQ = [carve([128, 512], BF16) for _ in range(2)]; b_SQ = [S.buf("sq%d" % i) for i in range(2)]
    NL = [carve([128, 512], F32) for _ in range(2)]; b_NL = [S.buf("nl%d" % i) for i in range(2)]
    SZ = carve([128, SBT], BF16); b_SZ = S.buf("sz")
    RW = carve([128, 512], F32); b_RW = S.buf("rw")
    AUN = carve([128, 512], F32); b_AUN = S.buf("aun")
    KF = carve([128, 512], F32); b_KF = S.buf("kf")
    KO = carve([128, 512], F32); b_KO = S.buf("ko")
    VO = carve([128, 512], F32); b_VO = S.buf("vo")
    att_end = a_off[0]
    a_off[0] = 0
    WBA = carve([128, 4, 1024], BF16); WBB = carve([128, 4, 1024], BF16); WOUT = carve([128, 8, 512], BF16)
    b_WBA = S.buf("wba"); b_WBB = S.buf("wbb"); b_WOUT = S.buf("wout")
    MIX = carve([128, 8, 512], BF16); b_MIX = [S.buf("mix%d" % i) for i in range(8)]
    SG = [carve([128, 512], F32) for _ in range(2)]; b_SG = [S.buf("sg%d" % i) for i in range(2)]
    T1 = [carve([128, 512], F32) for _ in range(2)]; b_T1 = [S.buf("t1%d" % i) for i in range(2)]
    XR = [carve([128, 512], F32) for _ in range(2)]; b_XR = [S.buf("xr%d" % i) for i in range(2)]
    WF = [carve([128, 1024], F32) for _ in range(2)]; b_WF = [S.buf("wf%d" % i) for i in range(2)]

    print("sbuf remaining", nc.sbuf_bytes_remaining, "att_end", att_end, "fin_end", a_off[0])
    PS = [nc.alloc_psum_tensor("ps%d" % i, [128, 512], F32) for i in range(8)]
    b_PS = [S.buf("ps%d" % i, excl=True) for i in range(8)]
    rPJ = Rot([0, 1]); rNB = Rot([2, 3]); rST = Rot([4, 5]); rPJ6 = Rot([0, 1, 4, 5])
    UA, UB = 6, 7
    rWST = Rot([0, 1]); rWBF = Rot([0, 1, 2, 3])
    rEXPT = Rot([0, 1]); rPT = Rot([0, 1]); rSQ = Rot([0, 1]); rNL = Rot([0, 1])

    def fence():
        evs = []
        for e in (PE, ACT, DVE, POOL):
            n = len(S.streams[e])
            for i in range(n - 1, -1, -1):
                r = S.streams[e][i]
                if r["fn"] is not None and r["dma"] is None:
                    evs.append(Ev("e", e, i))
                    break
        evs = evs + list(S.recent_dma)
        S.recent_dma = []
        for e in (PE, ACT, DVE, POOL, SP):
            S.wait_all(e, evs)
    fence.dma_evs = []

    def load_const(dst, src, cols, conv=None):
        ev = S.dma(SP, lambda e: e.dma_start(out=cst[:, 0:cols], in_=src), writes=[b_const], semowner=b_const)
        if conv == "bf":
            S.op(DVE, lambda e: e.tensor_copy(out=dst, in_=cst[:, 0:cols]), reads=[b_const], writes=[b_const])
        else:
            S.op(DVE, lambda e: e.tensor_copy(out=dst, in_=cst[:, 0:cols]), reads=[b_const], writes=[b_const])

    load_const(identF[:, :], ident_d, 128); load_const(identB[:, :], ident_d, 128)
    load_const(bonesB[:, :], bones_d, 128); load_const(swapF[:, :], swap_d, 128)
    load_const(MASK[:, :, :].rearrange("p a b -> p (a b)"), masks_d, 2048)
    load_const(gainS[:, :], gain_d, 8); load_const(qkgS[:, :], qkg_d, 8); load_const(skS[:, :], sk_d, 4)
    load_const(flagS[:, :], flag_d, 1); load_const(dbiasS[:, :], dbias_d, 8)
    S.op(DVE, lambda e: e.memset(epsS[:, :], EPS), writes=[b_const])
    S.op(DVE, lambda e: e.memset(oneS[:, :], 1.0), writes=[b_const])
    S.op(DVE, lambda e: e.memset(onesB[:, :], 1.0), writes=[b_const])
    S.op(ACT, lambda e: e.activation(out=skS[:, :], in_=skS[:, :], func=AF.Exp), reads=[b_const], writes=[b_const])
    C = [b_const]

    bulk = []
    if with_copy:
        def add_copy(src, dst):
            bulk.append((src, dst))
        for n in range(NS):
            for g, L in ((2, 2048), (1, 512), (0, 128)):
                r = 1
                while r < L:
                    nr = min(256, L - r)
                    for (a, bnd) in ((16, None),):
                        pass
                    o = 16 if nr % 16 == 0 else (15 if nr % 15 == 0 else 1)
                    sv = ca_d[g][n, r:r + nr, :].rearrange("(o i) c -> o (i c)", o=o)
                    dv = sa_d[g][n, r - 1:r - 1 + nr, :].rearrange("(o i) c -> o (i c)", o=o)
                    add_copy(sv, dv)
                    r += nr
        sv = cb_d[:, 1:128, :].rearrange("n r c -> n (r c)")
        dv = sb_d[:, 0:127, :].rearrange("n r c -> n (r c)")
        add_copy(sv, dv)

    def issue_bulk(k=1):
        for _ in range(k):
            if bulk:
                sv, dv = bulk.pop(0)
                S.dma(ACT, lambda e, sv=sv, dv=dv: e.dma_start(out=dv, in_=sv))

    conv_eng = [POOL]
    bulk_rate = [0]
    b_scr_g = [S.buf("scrg%d" % i) for i in range(16)]
    b_scr_o = [S.buf("scro%d" % i) for i in range(2)]
    scr_ready = {"g": [False] * 16, "o": [False] * 2}

    def load_g(i):
        if not scr_ready["g"][i]:
            wv_, bw_ = load_w(50 + i)
            S.dma(SP, lambda e: e.dma_start(out=scr_g[i], in_=wv_[:, :, :].rearrange("p a b -> p (a b)")), reads=[bw_], writes=[b_scr_g[i]], semowner=b_scr_g[i])
            scr_ready["g"][i] = True
            return wv_, bw_
        bi = rWBF.next()
        S.dma(SP, lambda e: e.dma_start(out=WBF[bi][:, :, :].rearrange("p a b -> p (a b)"), in_=scr_g[i]), reads=[b_scr_g[i]], writes=[b_WBF[bi]], semowner=b_WBF[bi])
        return WBF[bi], b_WBF[bi]

    def load_w(ch, dup=None, src=None, scale=1.0):
        issue_bulk(bulk_rate[0])
        CE = conv_eng[0]
        si = rWST.next(); bi = rWBF.next()
        S.dma(SP, lambda e: e.dma_start(out=WST[si][:, :, :].rearrange("p a b -> p (a b)"), in_=win_d[ch]),
              writes=[b_WST[si]], semowner=b_WST[si])
        g_bc = gainS[:, :].unsqueeze(2).to_broadcast([128, 8, 128])
        if dup is None:
            S.op(CE, lambda e: e.tensor_tensor(out=WBF[bi][:, :, :], in0=WST[si][:, :, :], in1=g_bc, op=ALU.mult),
                 reads=[b_WST[si]] + C, writes=[b_WBF[bi]])
        else:
            g64 = gainS[:, :].unsqueeze(2).to_broadcast([128, 8, 64])
            for half in (0, 1):
                S.op(CE, lambda e, half=half: e.tensor_tensor(out=WBF[bi][:, :, 64 * half:64 * half + 64],
                                                               in0=WST[si][:, :, 64 * dup:64 * dup + 64], in1=g64, op=ALU.mult),
                     reads=[b_WST[si]] + C, writes=[b_WBF[bi]])
        return WBF[bi], b_WBF[bi]

    def proj_fm(w, bw, bank, ncols, rhs_fn):
        for k in range(8):
            S.op(PE, lambda e, k=k: e.matmul(PS[bank][:, 0:ncols], lhsT=w[:, k, :], rhs=rhs_fn(k), start=(k == 0), stop=(k == 7)),
                 reads=[bw, b_xT], writes=[b_PS[bank]])

    def norm_evac(bank, ncols, gcol, out_ap, out_bufs, f32_out=None):
        si = rSQ.next(); ni = rNL.next(); nb = rNB.next()
        S.op(ACT, lambda e: e.activation(out=SQ[si][:, 0:ncols], in_=PS[bank][:, 0:ncols], func=AF.Square),
             reads=[b_PS[bank]], writes=[b_SQ[si]])
        S.op(PE, lambda e: e.matmul(PS[nb][:, 0:ncols], lhsT=bonesB[:, :], rhs=SQ[si][:, 0:ncols], start=True, stop=True),
             reads=[b_SQ[si]] + C, writes=[b_PS[nb]])
        S.op(ACT, lambda e: e.activation(out=NL[ni][:, 0:ncols], in_=PS[nb][:, 0:ncols], func=AF.Ln, bias=epsS[:, 0:1], scale=1.0),
             reads=[b_PS[nb]] + C, writes=[b_NL[ni]])
        S.op(ACT, lambda e: e.activation(out=NL[ni][:, 0:ncols], in_=NL[ni][:, 0:ncols], func=AF.Exp, scale=-0.5),
             reads=[b_NL[ni]], writes=[b_NL[ni]])
        S.op(DVE, lambda e: e.scalar_tensor_tensor(out=out_ap, in0=PS[bank][:, 0:ncols], scalar=qkgS[:, gcol:gcol + 1],
                                                   in1=NL[ni][:, 0:ncols], op0=ALU.mult, op1=ALU.mult),
             reads=[b_PS[bank], b_NL[ni]] + C, writes=out_bufs)
        if f32_out is not None:
            fo, fb = f32_out
            S.op(DVE, lambda e: e.scalar_tensor_tensor(out=fo, in0=PS[bank][:, 0:ncols], scalar=qkgS[:, gcol:gcol + 1],
                                                       in1=NL[ni][:, 0:ncols], op0=ALU.mult, op1=ALU.mult),
                 reads=[b_PS[bank], b_NL[ni]] + C, writes=fb)

    def xT_win(w):
        return lambda k: xT[:, k, 512 * w:512 * w + 512]

    def tok_ap(k, off, step, n=128):
        return xT[:, k, off:off + step * (n - 1) + 1:step]

    def proj_v_blocks(w, bw, blocks, dst, dbuf, scale_flag=False, vout=None, vcols=128):
        d0 = blocks[0][2]; nb_ = len(blocks)
        assert [b[2] for b in blocks] == list(range(d0, d0 + nb_))
        if scale_flag:
            S.op(POOL, lambda e: e.tensor_copy(out=dst[:, d0:d0 + nb_, 64:128], in_=flagS[:, 0:1].unsqueeze(1).to_broadcast([128, nb_, 64])),
                 reads=C, writes=[dbuf])
        else:
            S.op(POOL, lambda e: e.tensor_copy(out=dst[:, d0:d0 + nb_, 64:128], in_=onesB[:, :].unsqueeze(1).to_broadcast([128, nb_, 64])),
                 reads=C, writes=[dbuf])
        for i0_ in range(0, len(blocks), 4):
            grp = blocks[i0_:i0_ + 4]
            ng = len(grp)
            bank = rPJ6.next()
            for bi, (off, step, di) in enumerate(grp):
                for k in range(8):
                    S.op(PE, lambda e, k=k, bi=bi, off=off, step=step: e.matmul(
                        PS[bank][:, 128 * bi:128 * bi + 128], lhsT=tok_ap(k, off, step), rhs=w[:, k, :],
                        start=(k == 0), stop=(k == 7)), reads=[bw, b_xT], writes=[b_PS[bank]])
            dg = grp[0][2]
            src = PS[bank][:, 0:128 * ng].rearrange("p (b h d) -> p b h d", h=2, d=64)
            dsel = dst[:, dg:dg + ng, :].rearrange("p b (h d) -> p b h d", d=64)[:, :, 0:3:2, :]
            if scale_flag:
                S.op(DVE, lambda e, src=src, dsel=dsel: e.tensor_scalar(out=dsel, in0=src, scalar1=flagS[:, 0:1], scalar2=None, op0=ALU.mult),
                     reads=[b_PS[bank]] + C, writes=[dbuf])
            else:
                S.op(DVE, lambda e, src=src, dsel=dsel: e.tensor_copy(out=dsel, in_=src), reads=[b_PS[bank]], writes=[dbuf])
            if vout is not None and any(vout(i0_ + bi) is not None for bi in range(ng)):
                S.op(ACT, lambda e: e.activation(out=VO[:, 0:128 * ng], in_=PS[bank][:, 0:128 * ng], func=AF.Copy),
                     reads=[b_PS[bank]], writes=[b_VO])
                for bi in range(ng):
                    dap = vout(i0_ + bi)
                    if dap is None:
                        continue
                    ev = S.dma(POOL, lambda e, bi=bi, dap=dap: e.dma_start(out=dap, in_=VO[:, 128 * bi:128 * bi + vcols]),
                               reads=[b_VO], semowner=b_VO)
                    out_evs.append(ev)

    def prologue(src_d, t0):
        rX = Rot([0, 1, 2, 3])
        for blk in range(16):
            i = rX.next()
            S.dma(SP, lambda e, blk=blk: e.dma_start(out=XST[i], in_=src_d[t0 + 128 * blk:t0 + 128 * blk + 128, :]),
                  writes=[b_XST[i]], semowner=b_XST[i])
            stt_ = stats4[i]; bst_ = b_stats4[i]
            S.op(ACT, lambda e: e.activation(out=XB[i], in_=XST[i], func=AF.Square, accum_out=stt_[:, 0:1]),
                 reads=[b_XST[i]], writes=[b_XB[i], bst_])
            S.op(ACT, lambda e: e.activation(out=stt_[:, 1:2], in_=stt_[:, 0:1], func=AF.Ln, bias=epsS[:, 0:1], scale=1.0 / D),
                 reads=[bst_] + C, writes=[bst_])
            S.op(ACT, lambda e: e.activation(out=stt_[:, 2:3], in_=stt_[:, 1:2], func=AF.Exp, scale=-0.5),
                 reads=[bst_], writes=[bst_])
            S.op(DVE, lambda e: e.tensor_scalar(out=XB[i], in0=XST[i], scalar1=stt_[:, 2:3], scalar2=None, op0=ALU.mult),
                 reads=[b_XST[i], bst_], writes=[b_XB[i]])
            bank = rNB.next()
            pbf = PS[bank][:, :].bitcast(BF16)
            for k in range(8):
                S.op(PE, lambda e, k=k: e.transpose(pbf[:, 128 * k:128 * k + 128], XB[i][:, 128 * k:128 * k + 128], identB[:, :]),
                     reads=[b_XB[i]] + C, writes=[b_PS[bank]])
            S.op(DVE, lambda e, blk=blk: e.tensor_copy(out=xT[:, :, 128 * blk:128 * blk + 128],
                                                       in_=pbf.rearrange("p (k t) -> p k t", t=128)),
                 reads=[b_PS[bank]], writes=[b_xT])

    def attn_window(w, hg, groups, started):
        units = [(tile, head) for tile in groups for head in (0, 1)]
        state = []

        def emit_s(tile, head):
            rows = slice(64 * head, 64 * head + 64)
            stb = rST.next()
            nseg = len(tile["segs"]); n = tile["n"]
            for si, sg in enumerate(tile["segs"]):
                S.op(PE, lambda e, si=si, sg=sg, rows=rows: e.matmul(
                    PS[stb][:, n * si:n * si + n], lhsT=sg["k"][rows, :], rhs=sg["q"][rows, :], start=True, stop=True),
                    reads=[sg["kb"], sg["qb"]], writes=[b_PS[stb]])
            ei = rEXPT.next(); pi = rPT.next()
            tot = n * nseg
            S.op(ACT, lambda e, tot=tot: e.activation(out=EXPT[ei][:, 0:tot], in_=PS[stb][:, 0:tot], func=AF.Exp, scale=0.125),
                 reads=[b_PS[stb]], writes=[b_EXPT[ei]])
            m = tile["mask"](hg[head])
            nq = nseg // 2
            S.op(DVE, lambda e, tot=tot, m=m, nq=nq, n=n: e.tensor_tensor(
                out=PT[pi][:, 0:tot].rearrange("p (a b c) -> p a b c", b=2, c=n),
                in0=EXPT[ei][:, 0:tot].rearrange("p (a b c) -> p a b c", b=2, c=n),
                in1=m.unsqueeze(1).to_broadcast([128, nq, 2, n]), op=ALU.mult),
                reads=[b_EXPT[ei]] + C, writes=[b_PT[pi]])
            return pi

        def emit_pv(tile, head, pi):
            n = tile["n"]
            ub = UA if head == 0 else UB
            for si, sg in enumerate(tile["segs"]):
                vb = sg["v"]
                lhs = vb[:, 0:128] if head == 0 else vb[:, 64:192]
                first = not started[head]
                started[head] = True
                S.op(PE, lambda e, si=si, sg=sg, lhs=lhs, first=first, ub=ub: e.matmul(
                    sg["o"](PS[ub]), lhsT=lhs, rhs=PT[pi][:, n * si:n * si + n], start=first, stop=False,
                    skip_group_check=True), reads=[sg["vb"], b_PT[pi]], writes=[b_PS[ub]])

        prev = None
        for (tile, head) in units:
            pi = emit_s(tile, head)
            if prev is not None:
                emit_pv(*prev)
            prev = (tile, head, pi)
        emit_pv(*prev)

    def mask_full(h):
        return MASK[:, h, :].rearrange("p (b c) -> p b c", c=128)

    def mask_win(w):
        return lambda h: MASK[:, h, :].rearrange("p (b c) -> p b c", c=128)[:, :, 32 * w:32 * w + 32]

    def segs_g1(w, Kc, bK, Vc, bV, q, bq, qoff=0):
        tiles = []
        for half in (0, 1):
            segs = []
            for i in (0, 1):
                qb = 4 * w + 2 * half + i
                qap = q[:, 128 * qb - qoff:128 * qb - qoff + 128]
                oc = 128 * (2 * half + i)
                o = (lambda oc: (lambda ps: ps[:, oc:oc + 128]))(oc)
                segs.append(dict(k=Kc[:, 128 + 128 * qb:256 + 128 * qb], kb=bK, q=qap, qb=bq, v=Vc[:, qb + 1, :], vb=bV, o=o))
                segs.append(dict(k=Kc[:, 128 * qb:128 * qb + 128], kb=bK, q=qap, qb=bq, v=Vc[:, qb, :], vb=bV, o=o))
            tiles.append(dict(segs=segs, n=128, mask=mask_full))
        return tiles

    def strided(ap2d, off, step, n):
        return ap2d[:, off:off + step * (n - 1) + 1:step]

    def segs_g2(w, Kc, bK, Vc, bV, q, bq, qoff=0):
        tiles = []
        for half in (0, 1):
            segs = []
            for i in (0, 1):
                r = 2 * half + i
                qap = strided(q, 512 * w + r - qoff, 4, 128)
                o = (lambda r: (lambda ps: strided(ps, r, 4, 128)))(r)
                segs.append(dict(k=strided(Kc, 512 + 512 * w + r, 4, 128), kb=bK, q=qap, qb=bq, v=Vc[:, 4 + 4 * w + r, :], vb=bV, o=o))
                segs.append(dict(k=strided(Kc, 512 * w + r, 4, 128), kb=bK, q=qap, qb=bq, v=Vc[:, 4 * w + r, :], vb=bV, o=o))
            tiles.append(dict(segs=segs, n=128, mask=mask_full))
        return tiles

    def segs_g3(w, Kcur, bKc, Kprev, bKp, Vcur, bVc, Vprev, bVp, q, bq, qoff=0):
        tiles = []
        for half in (0, 1):
            segs = []
            for i in range(8):
                r = 8 * half + i
                qap = strided(q, 512 * w + r - qoff, 16, 32)
                o = (lambda r: (lambda ps: strided(ps, r, 16, 32)))(r)
                segs.append(dict(k=strided(Kcur, r, 16, 128), kb=bKc, q=qap, qb=bq, v=Vcur[:, r, :], vb=bVc, o=o))
                segs.append(dict(k=strided(Kprev, r, 16, 128), kb=bKp, q=qap, qb=bq, v=Vprev[:, r, :], vb=bVp, o=o))
            tiles.append(dict(segs=segs, n=32, mask=mask_win(w)))
        return tiles

    def finish_window(w, c, dstT, b_dst, sink_col=None):
        if sink_col is None:
            S.op(DVE, lambda e: e.reciprocal(out=RW[0:64, :], in_=PS[UB][0:64, :]), reads=[b_PS[UB]], writes=[b_RW])
            S.op(DVE, lambda e: e.reciprocal(out=RW[64:128, :], in_=PS[UA][64:128, :]), reads=[b_PS[UA]], writes=[b_RW])
        else:
            S.op(DVE, lambda e: e.tensor_copy(out=RW[0:64, :], in_=PS[UB][0:64, :]), reads=[b_PS[UB]], writes=[b_RW])
            S.op(DVE, lambda e: e.tensor_copy(out=RW[64:128, :], in_=PS[UA][64:128, :]), reads=[b_PS[UA]], writes=[b_RW])
        nb = rNB.next()
        S.op(PE, lambda e: e.matmul(PS[nb][:, :], lhsT=swapF[:, :], rhs=RW[:, :], start=True, stop=True),
             reads=[b_RW] + C, writes=[b_PS[nb]])
        S.op(DVE, lambda e: e.tensor_tensor(out=AUN[0:64, :], in0=PS[UA][0:64, :], in1=SZ[0:64, 512 * w:512 * w + 512], op=ALU.mult),
             reads=[b_PS[UA], b_SZ], writes=[b_AUN])
        S.op(DVE, lambda e: e.tensor_tensor(out=AUN[64:128, :], in0=PS[UB][64:128, :], in1=SZ[64:128, 512 * w:512 * w + 512], op=ALU.mult),
             reads=[b_PS[UB], b_SZ], writes=[b_AUN])
        if sink_col is None:
            S.op(DVE, lambda e: e.tensor_tensor(out=dstT[:, c, 512 * w:512 * w + 512], in0=AUN[:, :], in1=PS[nb][:, :], op=ALU.mult),
                 reads=[b_AUN, b_PS[nb]], writes=[b_dst[c]])
        else:
            S.op(DVE, lambda e: e.tensor_scalar(out=RW[:, :], in0=PS[nb][:, :], scalar1=skS[:, sink_col:sink_col + 1], scalar2=None, op0=ALU.add),
                 reads=[b_PS[nb]] + C, writes=[b_RW])
            S.op(DVE, lambda e: e.reciprocal(out=RW[:, :], in_=RW[:, :]), reads=[b_RW], writes=[b_RW])
            S.op(DVE, lambda e: e.tensor_tensor(out=dstT[:, c, 512 * w:512 * w + 512], in0=AUN[:, :], in1=RW[:, :], op=ALU.mult),
                 reads=[b_AUN, b_RW], writes=[b_dst[c]])

    def silu_chunk(ch):
        wz, bwz = load_w(ch)
        for w in range(4):
            bank = rPJ6.next()
            proj_fm(wz, bwz, bank, 512, xT_win(w))
            ni = rNL.next()
            S.op(ACT, lambda e: e.activation(out=NL[ni][:, :], in_=PS[bank][:, :], func=AF.Exp, scale=-1.0),
                 reads=[b_PS[bank]], writes=[b_NL[ni]])
            S.op(ACT, lambda e: e.activation(out=NL[ni][:, :], in_=NL[ni][:, :], func=AF.Ln, bias=oneS[:, 0:1], scale=1.0),
                 reads=[b_NL[ni]] + C, writes=[b_NL[ni]])
            S.op(ACT, lambda e: e.activation(out=NL[ni][:, :], in_=NL[ni][:, :], func=AF.Exp, scale=-1.0),
                 reads=[b_NL[ni]], writes=[b_NL[ni]])
            S.op(DVE, lambda e, w=w: e.tensor_tensor(out=SZ[:, 512 * w:512 * w + 512], in0=PS[bank][:, :], in1=NL[ni][:, :], op=ALU.mult),
                 reads=[b_PS[bank], b_NL[ni]], writes=[b_SZ])

    def k_out_rows(dst_d, tok_base_in_dst, c128, nblk, col0):
        nb = rNB.next()
        for b in range(nblk):
            S.op(PE, lambda e, b=b: e.transpose(PS[nb][:, 128 * b:128 * b + 128], KF[:, 128 * b:128 * b + 128], identF[:, :]),
                 reads=[b_KF] + C, writes=[b_PS[nb]])
        S.op(ACT, lambda e: e.activation(out=KO[:, 0:128 * nblk], in_=PS[nb][:, 0:128 * nblk], func=AF.Copy),
             reads=[b_PS[nb]], writes=[b_KO])
        dv = dst_d[tok_base_in_dst:tok_base_in_dst + 128 * nblk, col0:col0 + c128].rearrange("(b p) c -> p b c", p=128)
        ev = S.dma(POOL, lambda e: e.dma_start(out=dv, in_=KO[:, 0:128 * nblk].rearrange("p (b c) -> p b c", c=128)[:, :, 0:c128]),
                   reads=[b_KO], semowner=b_KO)
        out_evs.append(ev)


    def load_fin_w(dst, bdst, src_d, nk, ncol, col0, width, scale):
        rW = Rot([0, 1])
        sv = src_d.rearrange("p (k c) -> p k c", c=ncol)
        for k in range(nk):
            i = rW.next()
            S.dma(SP, lambda e, k=k, i=i: e.dma_start(out=WF[i][:, 0:width], in_=sv[:, k, col0:col0 + width]),
                  writes=[b_WF[i]], semowner=b_WF[i])
            S.op(DVE if (k % 2 == 0) else ACT, (lambda e, k=k, i=i: e.tensor_copy(out=dst[:, k, 0:width], in_=WF[i][:, 0:width])) if (k % 2 == 0) else
                 (lambda e, k=k, i=i: e.activation(out=dst[:, k, 0:width], in_=WF[i][:, 0:width], func=AF.Copy)),
                 reads=[b_WF[i]], writes=[bdst])

    def sigmoid_from(bank, dst, bdst):
        S.op(ACT, lambda e: e.activation(out=dst, in_=PS[bank][:, :], func=AF.Exp, scale=-1.0), reads=[b_PS[bank]], writes=[bdst])
        S.op(ACT, lambda e: e.activation(out=dst, in_=dst, func=AF.Ln, bias=oneS[:, 0:1], scale=1.0), reads=[bdst] + C, writes=[bdst])
        S.op(ACT, lambda e: e.activation(out=dst, in_=dst, func=AF.Exp, scale=-1.0), reads=[bdst], writes=[bdst])

    def final_stage(tok0, ntok_total, x_src, y_dst, xTsrc, nwin, wcols):
        conv_eng[0] = DVE
        bulk_rate[0] = 0
        load_fin_w(WBA, b_WBA, wba_d, 4, 1024, 0, 1024, 1.0)
        load_fin_w(WBB, b_WBB, wbb_d, 4, 1024, 0, 1024, 1.0)
        for w in range(nwin):
            n = wcols
            cs = slice(wcols * w, wcols * w + n)
            for j in range(8):
                wga, bwga = load_g(j)
                wgb, bwgb = load_g(8 + j)
                bka = rPJ.next()
                proj_fm(wga, bwga, bka, n, lambda k: xTsrc[:, k, cs])
                sigmoid_from_n(bka, SG[0][:, 0:n], b_SG[0], n)
                bkb = rPJ.next()
                proj_fm(wgb, bwgb, bkb, n, lambda k: xTsrc[:, k, cs])
                sigmoid_from_n(bkb, SG[1][:, 0:n], b_SG[1], n)
                ba = rST.next()
                for cc in range(4):
                    S.op(PE, lambda e, cc=cc, j=j: e.matmul(PS[ba][:, 0:n], lhsT=WBA[:, cc, 128 * j:128 * j + 128], rhs=aT[:, cc, cs],
                                                          start=(cc == 0), stop=(cc == 3)), reads=[b_WBA] + b_aT, writes=[b_PS[ba]])
                bb = rST.next()
                for cc in range(4):
                    S.op(PE, lambda e, cc=cc, j=j: e.matmul(PS[bb][:, 0:n], lhsT=WBB[:, cc, 128 * j:128 * j + 128], rhs=bT[:, cc, cs],
                                                          start=(cc == 0), stop=(cc == 3)), reads=[b_WBB] + b_bT, writes=[b_PS[bb]])
                S.op(DVE, lambda e: e.tensor_tensor(out=T1[0][:, 0:n], in0=PS[ba][:, 0:n], in1=SG[0][:, 0:n], op=ALU.mult),
                     reads=[b_PS[ba], b_SG[0]], writes=[b_T1[0]])
                S.op(DVE, lambda e: e.tensor_tensor(out=T1[1][:, 0:n], in0=PS[bb][:, 0:n], in1=SG[1][:, 0:n], op=ALU.mult),
                     reads=[b_PS[bb], b_SG[1]], writes=[b_T1[1]])
                S.op(DVE, lambda e, j=j: e.tensor_tensor(out=MIX[:, j, 0:n], in0=T1[0][:, 0:n], in1=T1[1][:, 0:n], op=ALU.add),
                     reads=[b_T1[0], b_T1[1]], writes=[b_MIX[j]])
            for half in range(2):
                if not scr_ready["o"][half]:
                    load_fin_w(WOUT, b_WOUT, wout_d, 8, 1024, 512 * half, 512, 1.0)
                    S.dma(SP, lambda e, half=half: e.dma_start(out=scr_o[half], in_=WOUT[:, :, :].rearrange("p a b -> p (a b)")), reads=[b_WOUT], writes=[b_scr_o[half]], semowner=b_scr_o[half])
                    scr_ready["o"][half] = True
                else:
                    S.dma(SP, lambda e, half=half: e.dma_start(out=WOUT[:, :, :].rearrange("p a b -> p (a b)"), in_=scr_o[half]), reads=[b_scr_o[half]], writes=[b_WOUT], semowner=b_WOUT)
                nblk = (n + 127) // 128
                for b in range(nblk):
                    nt = min(128, n - 128 * b)
                    yb = UA if (b % 2 == 0) else UB
                    for k in range(8):
                        S.op(PE, lambda e, k=k, b=b, nt=nt, yb=yb: e.matmul(PS[yb][0:nt, :], lhsT=MIX[:, k, 128 * b:128 * b + nt], rhs=WOUT[:, k, :],
                                                                 start=(k == 0), stop=(k == 7)), reads=b_MIX + [b_WOUT], writes=[b_PS[yb]])
                    xi = (b % 2)
                    r0 = tok0 + wcols * w + 128 * b
                    S.dma(POOL, lambda e, r0=r0, nt=nt, xi=xi, half=half: e.dma_start(out=XR[xi][0:nt, :], in_=x_src[r0:r0 + nt, 512 * half:512 * half + 512]),
                          writes=[b_XR[xi]], semowner=b_XR[xi])
                    S.op(DVE, lambda e, nt=nt, xi=xi, yb=yb: e.tensor_tensor(out=XR[xi][0:nt, :], in0=XR[xi][0:nt, :], in1=PS[yb][0:nt, :], op=ALU.add),
                         reads=[b_XR[xi], b_PS[yb]], writes=[b_XR[xi]])
                    ev = S.dma(POOL, lambda e, r0=r0, nt=nt, xi=xi, half=half: e.dma_start(out=y_dst[r0:r0 + nt, 512 * half:512 * half + 512], in_=XR[xi][0:nt, :]),
                               reads=[b_XR[xi]], semowner=b_XR[xi])
                    out_evs.append(ev)

    def sigmoid_from_n(bank, dst, bdst, n):
        S.op(ACT, lambda e: e.activation(out=dst, in_=PS[bank][:, 0:n], func=AF.Exp, scale=-1.0), reads=[b_PS[bank]], writes=[bdst])
        S.op(ACT, lambda e: e.activation(out=dst, in_=dst, func=AF.Ln, bias=oneS[:, 0:1], scale=1.0), reads=[bdst] + C, writes=[bdst])
        S.op(ACT, lambda e: e.activation(out=dst, in_=dst, func=AF.Exp, scale=-1.0), reads=[bdst], writes=[bdst])

    import os as _os
    STOP = int(_os.environ.get("MK_STOP", "0"))

    def finish():
        while bulk:
            issue_bulk(1)
        evs = list(out_evs) + list(S.recent_dma)
        if S.bulk_cnt:
            evs.append(Ev("d", S.bulk_sem, S.bulk_cnt))
        for e in (PE, ACT, DVE, POOL):
            for i in range(len(S.streams[e]) - 1, -1, -1):
                r = S.streams[e][i]
                if r["fn"] is not None and r["dma"] is None:
                    evs.append(Ev("e", e, i))
                    break
        S.wait_all(SP, evs)
        S.emit()
        st.close()
        return nc

    slot_prev = [0, 1, 2, 3]
    spare = [4]

    prologue(xh_d, 0)
    fence()
    if STOP == 1:
        return finish()
    for c in range(4):
        wk, bwk = load_w(CH_K(3, c))
        sl = slot_prev[c]
        for w in range(4):
            bank = rPJ6.next()
            proj_fm(wk, bwk, bank, 512, xT_win(w))
            norm_evac(bank, 512, 5, K3[sl][:, 512 * w:512 * w + 512], [b_K3[sl]])
        wv, bwv = load_w(CH_V(3, c))
        proj_v_blocks(wv, bwv, [(r, 16, r) for r in range(16)], V3[sl], b_V3[sl], scale_flag=True)
        wk, bwk = load_w(CH_K(2, c))
        bank = rPJ6.next()
        proj_fm(wk, bwk, bank, 512, xT_win(3))
        norm_evac(bank, 512, 3, K2t[c][:, :], [b_K2t[c]])
        wv, bwv = load_w(CH_V(2, c))
        proj_v_blocks(wv, bwv, [(1536 + r, 4, r) for r in range(4)], V2t[c], b_V2t[c], scale_flag=True)
        wk, bwk = load_w(CH_K(1, c))
        bank = rPJ6.next()
        proj_fm(wk, bwk, bank, 128, lambda k: xT[:, k, SBT - 128:SBT])
        norm_evac(bank, 128, 1, K1t[c][:, :], [b_K1t[c]])
        wv, bwv = load_w(CH_V(1, c))
        proj_v_blocks(wv, bwv, [(SBT - 128, 1, 0)], V1t[c], b_V1t[c], scale_flag=True)
    for kvh in range(2):
        wk, bwk = load_w(CH_BK, dup=kvh)
        bank = rPJ6.next()
        proj_fm(wk, bwk, bank, 128, lambda k: xT[:, k, SBT - 128:SBT])
        norm_evac(bank, 128, 7, KBt[kvh][:, :], [b_KBt[kvh]])
        wv, bwv = load_w(CH_BV, dup=kvh)
        proj_v_blocks(wv, bwv, [(SBT - 128, 1, 0)], VBt[kvh], b_VBt[kvh], scale_flag=True)
    fence()
    if STOP == 2:
        return finish()

    def k_tail_out(dst_ap64or128, ncols):
        nb = rNB.next()
        S.op(PE, lambda e: e.transpose(PS[nb][:, 0:128], KF[:, 384:512], identF[:, :]), reads=[b_KF] + C, writes=[b_PS[nb]])
        S.op(ACT, lambda e: e.activation(out=KO[:, 0:128], in_=PS[nb][:, 0:128], func=AF.Copy), reads=[b_PS[nb]], writes=[b_KO])
        ev = S.dma(POOL, lambda e: e.dma_start(out=dst_ap64or128, in_=KO[:, 0:ncols]), reads=[b_KO], semowner=b_KO)
        out_evs.append(ev)

    for s in range(NSB):
        last = (s == NSB - 1)
        conv_eng[0] = POOL
        bulk_rate[0] = 1 if (s == 0 and NSB > 1) else 2
        prologue(x_d, s * SBT)
        fence()
        for kvh in range(2):
            S.op(POOL, lambda e: e.tensor_copy(out=K1c[:, 0:128], in_=KBt[kvh][:, :]), reads=[b_KBt[kvh]], writes=[b_K1c])
            S.op(POOL, lambda e: e.tensor_copy(out=V1c[:, 0:1, :], in_=VBt[kvh][:, :, :]), reads=[b_VBt[kvh]], writes=[b_V1c])
            wk, bwk = load_w(CH_BK, dup=kvh)
            for w in range(4):
                bank = rPJ6.next()
                proj_fm(wk, bwk, bank, 512, xT_win(w))
                f32o = (KF[:, :], [b_KF]) if (last and w == 3) else None
                norm_evac(bank, 512, 7, K1c[:, 128 + 512 * w:128 + 512 * w + 512], [b_K1c], f32_out=f32o)
                if last and w == 3:
                    k_tail_out(pb_d[:, 64 * kvh:64 * kvh + 64], 64)
            wv, bwv = load_w(CH_BV, dup=kvh)
            vob = (lambda bi, kvh=kvh: (pb_d[:, 128 + 64 * kvh:128 + 64 * kvh + 64] if bi == 15 else None)) if last else None
            proj_v_blocks(wv, bwv, [(128 * b, 1, b + 1) for b in range(16)], V1c, b_V1c, vout=vob, vcols=64)
            S.op(POOL, lambda e: e.tensor_copy(out=KBt[kvh][:, :], in_=K1c[:, SBT:SBT + 128]), reads=[b_K1c], writes=[b_KBt[kvh]])
            S.op(POOL, lambda e: e.tensor_copy(out=VBt[kvh][:, :, :], in_=V1c[:, 16:17, :]), reads=[b_V1c], writes=[b_VBt[kvh]])
            for c in (2 * kvh, 2 * kvh + 1):
                silu_chunk(CH_ZB(c))
                wq, bwq = load_w(CH_BQ(c))
                def qprojb(w):
                    bank = rPJ.next()
                    proj_fm(wq, bwq, bank, 512, xT_win(w))
                    norm_evac(bank, 512, 6, QT[w % 2][:, :], [b_QT[w % 2]])
                qprojb(0)
                for w in range(4):
                    if w < 3:
                        qprojb(w + 1)
                    started = [False, False]
                    attn_window(w, (2 * c, 2 * c + 1), segs_g1(w, K1c, b_K1c, V1c, b_V1c, QT[w % 2], b_QT[w % 2], qoff=512 * w), started)
                    finish_window(w, c, bT, b_bT, sink_col=c)
        if STOP == 3:
            return finish()
        for c in range(4):
            silu_chunk(CH_ZA(c))
            S.op(POOL, lambda e: e.tensor_copy(out=K1c[:, 0:128], in_=K1t[c][:, :]), reads=[b_K1t[c]], writes=[b_K1c])
            S.op(POOL, lambda e: e.tensor_copy(out=V1c[:, 0:1, :], in_=V1t[c][:, :, :]), reads=[b_V1t[c]], writes=[b_V1c])
            S.op(POOL, lambda e: e.tensor_copy(out=K2c[:, 0:512], in_=K2t[c][:, :]), reads=[b_K2t[c]], writes=[b_K2c])
            S.op(POOL, lambda e: e.tensor_copy(out=V2c[:, 0:4, :], in_=V2t[c][:, :, :]), reads=[b_V2t[c]], writes=[b_V2c])
            slc = spare[0]; slp = slot_prev[c]
            plan = [(1, K1c, b_K1c, 128, 1), (2, K2c, b_K2c, 512, 3), (3, K3[slc], b_K3[slc], 0, 5)]
            for (g, Kd, bKd, koff, gcol) in plan:
                wk, bwk = load_w(CH_K(g, c))
                for w in range(4):
                    bank = rPJ6.next()
                    proj_fm(wk, bwk, bank, 512, xT_win(w))
                    need_out = last and (g == 3 or w == 3)
                    f32o = (KF[:, :], [b_KF]) if need_out else None
                    norm_evac(bank, 512, gcol, Kd[:, koff + 512 * w:koff + 512 * w + 512], [bKd], f32_out=f32o)
                    if need_out:
                        if g == 3:
                            k_out_rows(pa_d[2], 512 * w, 128, 4, 128 * c)
                        elif g == 2:
                            k_out_rows(pa_d[1], 0, 128, 4, 128 * c)
                        else:
                            k_tail_out(pa_d[0][:, 128 * c:128 * c + 128], 128)
            wv, bwv = load_w(CH_V(1, c))
            vo1 = (lambda bi, c=c: (pa_d[0][:, 512 + 128 * c:512 + 128 * c + 128] if bi == 15 else None)) if last else None
            proj_v_blocks(wv, bwv, [(128 * b, 1, b + 1) for b in range(16)], V1c, b_V1c, vout=vo1)
            wv, bwv = load_w(CH_V(2, c))

            def vo2f(bi, c=c):
                j, r = bi // 4, bi % 4
                if j != 3:
                    return None
                return pa_d[1][:, 512 + 128 * c:512 + 128 * c + 128].rearrange("(i r) c -> r i c", r=4)[r]
            proj_v_blocks(wv, bwv, [(512 * j + r, 4, 4 + 4 * j + r) for j in range(4) for r in range(4)], V2c, b_V2c,
                          vout=(vo2f if last else None))
            wv, bwv = load_w(CH_V(3, c))

            def vo3f(bi, c=c):
                return pa_d[2][:, 512 + 128 * c:512 + 128 * c + 128].rearrange("(i r) c -> r i c", r=16)[bi]
            proj_v_blocks(wv, bwv, [(r, 16, r) for r in range(16)], V3[slc], b_V3[slc], vout=(vo3f if last else None))
            wq = [load_w(CH_Q(g, c)) for g in (1, 2, 3)]

            def qproj(w):
                for gi in range(3):
                    bank = rPJ.next()
                    qi = 3 * (w % 2) + gi
                    proj_fm(wq[gi][0], wq[gi][1], bank, 512, xT_win(w))
                    norm_evac(bank, 512, 2 * gi, QT[qi][:, :], [b_QT[qi]])
            qproj(0)
            for w in range(4):
                started = [False, False]
                hg = (2 * c, 2 * c + 1)
                if w < 3:
                    qproj(w + 1)
                q0 = 3 * (w % 2)
                attn_window(w, hg, segs_g1(w, K1c, b_K1c, V1c, b_V1c, QT[q0], b_QT[q0], qoff=512 * w)
                            + segs_g2(w, K2c, b_K2c, V2c, b_V2c, QT[q0 + 1], b_QT[q0 + 1], qoff=512 * w)
                            + segs_g3(w, K3[slc], b_K3[slc], K3[slp], b_K3[slp], V3[slc], b_V3[slc], V3[slp], b_V3[slp],
                                      QT[q0 + 2], b_QT[q0 + 2], qoff=512 * w), started)
                finish_window(w, c, aT, b_aT)
            S.op(POOL, lambda e: e.tensor_copy(out=K1t[c][:, :], in_=K1c[:, SBT:SBT + 128]), reads=[b_K1c], writes=[b_K1t[c]])
            S.op(POOL, lambda e: e.tensor_copy(out=V1t[c][:, :, :], in_=V1c[:, 16:17, :]), reads=[b_V1c], writes=[b_V1t[c]])
            S.op(POOL, lambda e: e.tensor_copy(out=K2t[c][:, :], in_=K2c[:, SBT:SBT + 512]), reads=[b_K2c], writes=[b_K2t[c]])
            S.op(POOL, lambda e: e.tensor_copy(out=V2t[c][:, :, :], in_=V2c[:, 16:20, :]), reads=[b_V2c], writes=[b_V2t[c]])
            spare[0] = slp
            slot_prev[c] = slc
        fence()
        if STOP == 4:
            return finish()
        final_stage(s * SBT, SBT, x_d, y_d, xT, 4, 512)
        fence()
        if STOP == 5:
            return finish()


    if with_sample:
        conv_eng[0] = DVE
        a_off[0] = 0
        CT = [carve([128, 1024], F32) for _ in range(3)]; b_CT = [S.buf("ct%d" % i) for i in range(3)]
        TMP = [carve([128, 512], F32) for _ in range(4)]; b_TMP = [S.buf("tmp%d" % i) for i in range(4)]
        PBC = carve([128, 512], F32); b_PBC = S.buf("pbc")
        PBCs = [PBC, carve([128, 512], F32)]; b_PBCs = [S.buf("pbcs%d" % i) for i in range(2)]
        SC = carve([128, 16], F32); b_SC = S.buf("sc")
        SCs = [SC, carve([128, 16], F32)]; b_SCs = [S.buf("scs%d" % i) for i in range(2)]
        QSB = carve([128, 16, NS], BF16); b_QSB = S.buf("qsb")
        TMPB = [carve([128, 512], BF16) for _ in range(3)]; b_TMPB = [S.buf("tmpb%d" % i) for i in range(3)]
        PBB = [carve([128, 512], BF16) for _ in range(2)]; b_PBB = [S.buf("pbb%d" % i) for i in range(2)]
        QS = carve([128, 16, NS], F32); b_QS = S.buf("qs")
        KS = carve([128, 14, NS], F32); b_KS = S.buf("ks")
        VS = carve([128, 14, NS], F32); b_VS = S.buf("vs")
        ZS = carve([128, 8, NS], F32); b_ZS = S.buf("zs")
        P0 = carve([128, 16, NS], F32); b_P0 = S.buf("p0")
        UT = carve([128, 8, NS], F32); b_UT = S.buf("ut")
        ZT = carve([128, 8, NS], F32); b_ZT = S.buf("zt")
        NR = carve([NS, 1024], F32); b_NR = S.buf("nr")
        NRB = carve([NS, 256], F32); b_NRB = S.buf("nrb")
        XSS = carve([NS, 1024], F32); b_XSS = S.buf("xss")
        XSB = carve([NS, 1024], BF16); b_XSB = S.buf("xsb")
        onesF = carve([128, 1], F32)
        bonesF = carve([128, 128], F32)
        assert a_off[0] <= ARENA_BYTES
        S.op(DVE, lambda e: e.memset(onesF, 1.0), writes=[b_SC])
        S.dma(SP, lambda e: e.dma_start(out=bonesF, in_=bones_d), writes=[b_PBC], semowner=b_PBC)
        S.dma(SP, lambda e: e.dma_start(out=XSS, in_=xs_d), writes=[b_XSS], semowner=b_XSS)
        S.op(ACT, lambda e: e.activation(out=XSB, in_=XSS, func=AF.Square, accum_out=stat[0:NS, 0:1]), reads=[b_XSS], writes=[b_XSB, b_stat])
        S.op(ACT, lambda e: e.activation(out=stat[0:NS, 1:2], in_=stat[0:NS, 0:1], func=AF.Ln, bias=epsS[0:NS, 0:1], scale=1.0 / D), reads=[b_stat] + C, writes=[b_stat])
        S.op(ACT, lambda e: e.activation(out=stat[0:NS, 2:3], in_=stat[0:NS, 1:2], func=AF.Exp, scale=-0.5), reads=[b_stat], writes=[b_stat])
        S.op(DVE, lambda e: e.tensor_scalar(out=XSB, in0=XSS, scalar1=stat[0:NS, 2:3], scalar2=None, op0=ALU.mult), reads=[b_XSS, b_stat], writes=[b_XSB])
        bank = rNB.next()
        pbf = PS[bank][:, :].bitcast(BF16)
        for k in range(8):
            S.op(PE, lambda e, k=k: e.transpose(pbf[:, NS * k:NS * k + NS], XSB[:, 128 * k:128 * k + 128], identB[0:NS, 0:NS]),
                 reads=[b_XSB] + C, writes=[b_PS[bank]])
        S.op(DVE, lambda e: e.tensor_copy(out=xT[:, :, 0:NS], in_=pbf[:, 0:8 * NS].rearrange("p (k t) -> p k t", t=NS)),
             reads=[b_PS[bank]], writes=[b_xT])
        xs_rhs = lambda k: xT[:, k, 0:NS]

        def proj_s(ch, dup=None):
            wq_, bw_ = load_w(ch, dup=dup)
            bank = rPJ.next()
            proj_fm(wq_, bw_, bank, NS, xs_rhs)
            return bank

        def norm_s(bank, gcol, dst, bdst):
            ti = 0
            S.op(ACT, lambda e: e.activation(out=TMP[ti][:, 0:NS], in_=PS[bank][:, 0:NS], func=AF.Square), reads=[b_PS[bank]], writes=[b_TMP[ti]])
            nb = rNB.next()
            S.op(PE, lambda e: e.matmul(PS[nb][:, 0:NS], lhsT=bonesF, rhs=TMP[ti][:, 0:NS], start=True, stop=True), reads=[b_TMP[ti], b_PBC], writes=[b_PS[nb]])
            S.op(ACT, lambda e: e.activation(out=TMP[ti][:, 0:NS], in_=PS[nb][:, 0:NS], func=AF.Ln, bias=epsS[:, 0:1], scale=1.0), reads=[b_PS[nb]] + C, writes=[b_TMP[ti]])
            S.op(ACT, lambda e: e.activation(out=TMP[ti][:, 0:NS], in_=TMP[ti][:, 0:NS], func=AF.Exp, scale=-0.5), reads=[b_TMP[ti]], writes=[b_TMP[ti]])
            S.op(DVE, lambda e: e.scalar_tensor_tensor(out=dst, in0=PS[bank][:, 0:NS], scalar=qkgS[:, gcol:gcol + 1], in1=TMP[ti][:, 0:NS], op0=ALU.mult, op1=ALU.mult),
                 reads=[b_PS[bank], b_TMP[ti]] + C, writes=[bdst])

        for g in (1, 2, 3):
            for c in range(4):
                norm_s(proj_s(CH_Q(g, c)), 2 * (g - 1), QS[:, 4 * (g - 1) + c, :], b_QS)
                norm_s(proj_s(CH_K(g, c)), 2 * (g - 1) + 1, KS[:, 4 * (g - 1) + c, :], b_KS)
                bank = proj_s(CH_V(g, c))
                S.op(DVE, lambda e, g=g, c=c, bank=bank: e.tensor_copy(out=VS[:, 4 * (g - 1) + c, :], in_=PS[bank][:, 0:NS]), reads=[b_PS[bank]], writes=[b_VS])
        for c in range(4):
            norm_s(proj_s(CH_BQ(c)), 6, QS[:, 12 + c, :], b_QS)
        for kvh in range(2):
            norm_s(proj_s(CH_BK, dup=kvh), 7, KS[:, 12 + kvh, :], b_KS)
            bank = proj_s(CH_BV, dup=kvh)
            S.op(DVE, lambda e, kvh=kvh, bank=bank: e.tensor_copy(out=VS[:, 12 + kvh, :], in_=PS[bank][:, 0:NS]), reads=[b_PS[bank]], writes=[b_VS])
        for i, ch in enumerate([CH_ZA(c) for c in range(4)] + [CH_ZB(c) for c in range(4)]):
            bank = proj_s(ch)
            ti = 1
            S.op(ACT, lambda e, bank=bank: e.activation(out=TMP[ti][:, 0:NS], in_=PS[bank][:, 0:NS], func=AF.Exp, scale=-1.0), reads=[b_PS[bank]], writes=[b_TMP[ti]])
            S.op(ACT, lambda e: e.activation(out=TMP[ti][:, 0:NS], in_=TMP[ti][:, 0:NS], func=AF.Ln, bias=oneS[:, 0:1], scale=1.0), reads=[b_TMP[ti]] + C, writes=[b_TMP[ti]])
            S.op(ACT, lambda e: e.activation(out=TMP[ti][:, 0:NS], in_=TMP[ti][:, 0:NS], func=AF.Exp, scale=-1.0), reads=[b_TMP[ti]], writes=[b_TMP[ti]])
            S.op(DVE, lambda e, i=i, bank=bank: e.tensor_tensor(out=ZS[:, i, :], in0=PS[bank][:, 0:NS], in1=TMP[ti][:, 0:NS], op=ALU.mult),
                 reads=[b_PS[bank], b_TMP[ti]], writes=[b_ZS])

        for g in range(3):
            nb = rNB.next()
            for c in range(4):
                S.op(PE, lambda e, g=g, c=c: e.transpose(PS[nb][0:NS, 128 * c:128 * c + 128], KS[:, 4 * g + c, :], identF[:, :]), reads=[b_KS] + C, writes=[b_PS[nb]])
            S.op(DVE, lambda e: e.tensor_copy(out=NR[:, 0:512], in_=PS[nb][0:NS, :]), reads=[b_PS[nb]], writes=[b_NR])
            nb2 = rNB.next()
            for c in range(4):
                S.op(PE, lambda e, g=g, c=c: e.transpose(PS[nb2][0:NS, 128 * c:128 * c + 128], VS[:, 4 * g + c, :], identF[:, :]), reads=[b_VS] + C, writes=[b_PS[nb2]])
            S.op(DVE, lambda e: e.tensor_copy(out=NR[:, 512:1024], in_=PS[nb2][0:NS, :]), reads=[b_PS[nb2]], writes=[b_NR])
            L = (128, 512, 2048)[g]
            ev = S.dma(SP, lambda e, g=g, L=L: e.dma_start(out=sa_d[g][:, L - 1, :], in_=NR[:, :]), reads=[b_NR], semowner=b_NR)
            out_evs.append(ev)
        nb = rNB.next()
        for kvh in range(2):
            S.op(PE, lambda e, kvh=kvh: e.transpose(PS[nb][0:NS, 128 * kvh:128 * kvh + 128], KS[:, 12 + kvh, :], identF[:, :]), reads=[b_KS] + C, writes=[b_PS[nb]])
            S.op(PE, lambda e, kvh=kvh: e.transpose(PS[nb][0:NS, 256 + 128 * kvh:256 + 128 * kvh + 128], VS[:, 12 + kvh, :], identF[:, :]), reads=[b_VS] + C, writes=[b_PS[nb]])
        S.op(DVE, lambda e: e.tensor_copy(out=NRB[:, :].rearrange("p (a b) -> p a b", b=64),
                                          in_=PS[nb][0:NS, :].rearrange("p (a b) -> p a b", b=128)[:, :, 0:64]), reads=[b_PS[nb]], writes=[b_NRB])
        ev = S.dma(SP, lambda e: e.dma_start(out=sb_d[:, 127, :], in_=NRB[:, :]), reads=[b_NRB], semowner=b_NRB)
        out_evs.append(ev)

        for qc in range(16):
            kc = qc if qc < 12 else 12 + (qc - 12) // 2
            S.op(DVE, lambda e, qc=qc, kc=kc: e.tensor_tensor(out=TMP[0][:, 0:NS], in0=QS[:, qc, :], in1=KS[:, kc, :], op=ALU.mult), reads=[b_QS, b_KS], writes=[b_TMP[0]])
            nb = rNB.next()
            S.op(PE, lambda e: e.matmul(PS[nb][:, 0:NS], lhsT=bonesF, rhs=TMP[0][:, 0:NS], start=True, stop=True), reads=[b_TMP[0], b_PBC], writes=[b_PS[nb]])
            S.op(ACT, lambda e, qc=qc: e.activation(out=P0[:, qc, :], in_=PS[nb][:, 0:NS], func=AF.Exp, scale=8.0), reads=[b_PS[nb]], writes=[b_P0])

        PU, PZ = UA, UB
        first_u = [True]
        rCT = Rot([0, 1, 2]); rTMP = Rot([0, 1, 2, 3]); rPBC = Rot([0, 1]); rSC = Rot([0, 1]); rTMPB = Rot([0, 1, 2])
        S.op(DVE, lambda e: e.tensor_copy(out=QSB[:, :, :], in_=QS[:, :, :]), reads=[b_QS], writes=[b_QSB])

        def stage1(n, g):
            ci = rCT.next()
            if g < 3:
                dil = (1, 4, 16)[g]; L = (128, 512, 2048)[g]
                S.dma(SP, lambda e: e.dma_start(out=CT[ci][:, :], in_=ca_d[g][n, 0:L:dil, :]), writes=[b_CT[ci]], semowner=b_CT[ci])
                kview = CT[ci][:, 0:512]
                vview = CT[ci][:, 512:1024]
            else:
                S.dma(SP, lambda e: e.dma_start(out=CT[ci][:, 0:256], in_=cb_d[n, :, :]), writes=[b_CT[ci]], semowner=b_CT[ci])
                kview = CT[ci][:, 0:128].rearrange("p (a d) -> p a d", d=64).unsqueeze(2).to_broadcast([128, 2, 4, 64])
                vview = CT[ci][:, 128:256].rearrange("p (a d) -> p a d", d=64).unsqueeze(2).to_broadcast([128, 2, 4, 64])
            qb_bank = rST.next()
            for c in range(4):
                S.op(PE, lambda e, c=c: e.matmul(PS[qb_bank][:, 128 * c:128 * c + 128], lhsT=QSB[:, 4 * g + c, n:n + 1].to_broadcast([128, 128]),
                                                 rhs=identB[:, :], start=True, stop=True), reads=[b_QSB] + C, writes=[b_PS[qb_bank]])
            ti = rTMP.next(); sci = rSC.next(); pbi = rPBC.next()
            SCv = SCs[sci]; PBv = PBCs[pbi]
            if g < 3:
                S.op(DVE, lambda e: e.tensor_tensor(out=TMP[ti][:, :], in0=kview, in1=PS[qb_bank][:, :], op=ALU.mult),
                     reads=[b_CT[ci], b_PS[qb_bank]], writes=[b_TMP[ti]])
            else:
                S.op(DVE, lambda e: e.tensor_tensor(out=TMP[ti][:, :].rearrange("p (a r d) -> p a r d", r=4, d=64), in0=kview,
                                                    in1=PS[qb_bank][:, :].rearrange("p (a r d) -> p a r d", r=4, d=64), op=ALU.mult),
                     reads=[b_CT[ci], b_PS[qb_bank]], writes=[b_TMP[ti]])
            S.op(DVE, lambda e: e.tensor_reduce(out=SCv[:, 0:8], in_=TMP[ti][:, :].rearrange("p (h d) -> p h d", d=64), axis=AX.X, op=ALU.add),
                 reads=[b_TMP[ti]], writes=[b_SCs[sci]])
            S.op(DVE, lambda e: e.scalar_tensor_tensor(out=SCv[:, 0:8], in0=SCv[:, 0:8], scalar=0.125, in1=dbiasS[:, :], op0=ALU.mult, op1=ALU.add),
                 reads=[b_SCs[sci]] + C, writes=[b_SCs[sci]])
            S.op(ACT, lambda e: e.activation(out=SCv[:, 8:16], in_=SCv[:, 0:8], func=AF.Exp), reads=[b_SCs[sci]], writes=[b_SCs[sci]])
            PBv = PBB[pbi]
            S.op(DVE, lambda e: e.tensor_copy(out=PBv[:, :].rearrange("p (h d) -> p h d", d=64), in_=SCv[:, 8:16].unsqueeze(2).to_broadcast([128, 8, 64])),
                 reads=[b_SCs[sci]], writes=[b_PBB[pbi]])
            ti2 = rTMPB.next()
            if g < 3:
                S.op(DVE, lambda e: e.tensor_tensor(out=TMPB[ti2][:, :], in0=vview, in1=PBv[:, :], op=ALU.mult),
                     reads=[b_CT[ci], b_PBB[pbi]], writes=[b_TMPB[ti2]])
            else:
                S.op(DVE, lambda e: e.tensor_tensor(out=TMPB[ti2][:, :].rearrange("p (a r d) -> p a r d", r=4, d=64), in0=vview,
                                                    in1=PBv[:, :].rearrange("p (a r d) -> p a r d", r=4, d=64), op=ALU.mult),
                     reads=[b_CT[ci], b_PBB[pbi]], writes=[b_TMPB[ti2]])
            return (n, g, ti2, pbi)

        def stage2(n, g, ti2, pbi):
            colbase = 0 if g < 3 else 4
            for c in range(4):
                col = (colbase + c) * NS + n
                fu = first_u[0]
                S.op(PE, lambda e, c=c, col=col, fu=fu: e.matmul(PS[PU][:, col:col + 1], lhsT=TMPB[ti2][:, 128 * c:128 * c + 128], rhs=onesB[:, 0:1],
                                                                 start=fu, stop=False, skip_group_check=True), reads=[b_TMPB[ti2]] + C, writes=[b_PS[PU]])
                S.op(PE, lambda e, c=c, col=col, fu=fu: e.matmul(PS[PZ][:, col:col + 1], lhsT=PBB[pbi][:, 128 * c:128 * c + 128], rhs=onesB[:, 0:1],
                                                                 start=fu, stop=False, skip_group_check=True), reads=[b_PBB[pbi]] + C, writes=[b_PS[PZ]])
                first_u[0] = False

        prev_u = None
        for n in range(NS):
            for g in range(4):
                cur = stage1(n, g)
                if prev_u is not None:
                    stage2(*prev_u)
                prev_u = cur
        stage2(*prev_u)
        S.op(DVE, lambda e: e.tensor_copy(out=UT[:, :, :].rearrange("p a b -> p (a b)"), in_=PS[PU][:, 0:8 * NS]), reads=[b_PS[PU]], writes=[b_UT])
        S.op(DVE, lambda e: e.tensor_copy(out=ZT[:, :, :].rearrange("p a b -> p (a b)"), in_=PS[PZ][:, 0:8 * NS]), reads=[b_PS[PZ]], writes=[b_ZT])
        for qc in range(16):
            uc = (qc % 4) if qc < 12 else 4 + (qc - 12)
            vc = qc if qc < 12 else 12 + (qc - 12) // 2
            S.op(DVE, lambda e, qc=qc, vc=vc: e.tensor_tensor(out=TMP[0][:, 0:NS], in0=P0[:, qc, :], in1=VS[:, vc, :], op=ALU.mult), reads=[b_P0, b_VS], writes=[b_TMP[0]])
            S.op(DVE, lambda e, uc=uc: e.tensor_tensor(out=UT[:, uc, :], in0=UT[:, uc, :], in1=TMP[0][:, 0:NS], op=ALU.add), reads=[b_UT, b_TMP[0]], writes=[b_UT])
            S.op(DVE, lambda e, uc=uc, qc=qc: e.tensor_tensor(out=ZT[:, uc, :], in0=ZT[:, uc, :], in1=P0[:, qc, :], op=ALU.add), reads=[b_ZT, b_P0], writes=[b_ZT])
        for c in range(4):
            S.op(DVE, lambda e, c=c: e.tensor_scalar(out=ZT[:, 4 + c, :], in0=ZT[:, 4 + c, :], scalar1=skS[:, c:c + 1], scalar2=None, op0=ALU.add), reads=[b_ZT] + C, writes=[b_ZT])
        S.op(DVE, lambda e: e.reciprocal(out=ZT[:, :, :], in_=ZT[:, :, :]), reads=[b_ZT], writes=[b_ZT])
        S.op(DVE, lambda e: e.tensor_tensor(out=UT[:, :, :], in0=UT[:, :, :], in1=ZT[:, :, :], op=ALU.mult), reads=[b_UT, b_ZT], writes=[b_UT])
        S.op(DVE, lambda e: e.tensor_tensor(out=aT[:, :, 0:NS], in0=UT[:, 0:4, :], in1=ZS[:, 0:4, :], op=ALU.mult), reads=[b_UT, b_ZS], writes=b_aT)
        S.op(DVE, lambda e: e.tensor_tensor(out=bT[:, :, 0:NS], in0=UT[:, 4:8, :], in1=ZS[:, 4:8, :], op=ALU.mult), reads=[b_UT, b_ZS], writes=b_bT)
        fence()
        final_stage(0, NS, xs_d, ys_d, xT, 1, NS)
        fence()

    return finish()

N_CORES = 8
NSB_FULL = 2
NS_FULL = 16
_prog_cache = {}


def _consts():
    ident = np.eye(128, dtype=np.float32)
    bones = np.zeros((128, 128), np.float32)
    bones[:64, :64] = 1.0 / 64
    bones[64:, 64:] = 1.0 / 64
    swapm = np.zeros((128, 128), np.float32)
    for m in range(128):
        swapm[(m + 64) % 128, m] = 1.0
    kk = np.arange(128)[:, None].astype(np.float64)
    qq = np.arange(128)[None, :].astype(np.float64)
    masks = np.zeros((128, 8, 256), np.float32)
    for h in range(8):
        m = 2.0 ** -(h + 1)
        md = np.where(qq >= kk, np.exp(-m * (qq - kk)), 0.0)
        mp = np.where(kk >= qq, np.exp(-m * (qq - kk + 128)), 0.0)
        masks[:, h, :128] = md
        masks[:, h, 128:] = mp
    dbias = np.zeros((128, 8), np.float32)
    for h in range(8):
        dbias[:, h] = -(2.0 ** -(h + 1)) * (128 - np.arange(128))
    return ident, bones, swapm, masks.reshape(128, 2048), dbias


def make_in_maps(inp, n_cores, NSB, NS, seq_len):
    f = lambda a: np.ascontiguousarray(a, dtype=np.float32)
    w_in = inp["w_in"][0]
    win = f(w_in.reshape(8, 128, NCH, 128).transpose(2, 1, 0, 3).reshape(NCH, 128, 1024))
    gain = f(inp["norm_gain"][0].reshape(8, 128).T)
    wba = f(inp["w_branch_a"][0].reshape(4, 128, 1024).transpose(1, 0, 2).reshape(128, 4096))
    wbb = f(inp["w_branch_b"][0].reshape(4, 128, 1024).transpose(1, 0, 2).reshape(128, 4096))
    wout = f(inp["w_out"][0].reshape(8, 128, 1024).transpose(1, 0, 2).reshape(128, 8192))
    qa = inp["qk_norm_a"][0]
    qb = inp["qk_norm_b"][0]
    cols = [qa[0, 0], qa[0, 1], qa[1, 0], qa[1, 1], qa[2, 0], qa[2, 1], qb[0], qb[1]]
    qkg = f(np.stack([np.tile(c, 2) for c in cols], axis=1))
    sinks = inp["b_sinks"][0]
    sk = f(np.stack([np.repeat(sinks[2 * c:2 * c + 2], 64) for c in range(4)], axis=1))
    ident, bones, swapm, masks, dbias = _consts()
    T = NSB * SBT
    halves = seq_len // T
    maps = []
    for core in range(n_cores):
        b, hf = core // halves, core % halves
        x = inp["x_prompt"][b, hf * T:(hf + 1) * T]
        if hf == 0:
            xh = np.zeros((SBT, D), np.float32)
            flag = np.zeros((128, 1), np.float32)
        else:
            xh = inp["x_prompt"][b, hf * T - SBT:hf * T]
            flag = np.ones((128, 1), np.float32)
        sl = slice(core * NS, (core + 1) * NS)
        m = dict(x=f(x), xh=f(xh), flag=flag, xs=f(inp["x_sample"][sl, 0]), win=win, gain=gain, wba=wba, wbb=wbb, wout=wout,
                 qkg=qkg, sk=sk, ident=ident, bones=bones, swapm=swapm, masks=masks, dbias=dbias,
                 ca1=f(inp["cache_a1_kv"][0, sl].reshape(NS, 128, 1024)),
                 ca2=f(inp["cache_a2_kv"][0, sl].reshape(NS, 512, 1024)),
                 ca3=f(inp["cache_a3_kv"][0, sl].reshape(NS, 2048, 1024)),
                 cb=f(inp["cache_b_kv"][0, sl].reshape(NS, 128, 256)))
        maps.append(m)
    return maps


def kernel(**inputs):
    key = (NSB_FULL, NS_FULL)
    if key not in _prog_cache:
        _prog_cache[key] = build_program(NSB_FULL, NS_FULL)
    nc = _prog_cache[key]
    maps = make_in_maps(inputs, N_CORES, NSB_FULL, NS_FULL, 8192)
    res = run_bass_kernel_spmd(nc, maps, core_ids=list(range(N_CORES)))
    R = res.results
    T = NSB_FULL * SBT
    y = np.stack([np.concatenate([R[2 * b]["y"], R[2 * b + 1]["y"]], axis=0) for b in range(4)], axis=0)
    ys = np.concatenate([R[c]["ys"] for c in range(N_CORES)], axis=0)[:, None, :]
    pa = [np.stack([R[2 * b + 1][n].reshape(-1, 2, 8, 64) for b in range(4)], axis=0)[None] for n in ("pa1", "pa2", "pa3")]
    pb = np.stack([R[2 * b + 1]["pb"].reshape(-1, 2, 2, 64) for b in range(4)], axis=0)[None]
    sa = [np.concatenate([R[c][n] for c in range(N_CORES)], axis=0).reshape(128, -1, 2, 8, 64)[None] for n in ("sa1", "sa2", "sa3")]
    sbo = np.concatenate([R[c]["sb"] for c in range(N_CORES)], axis=0).reshape(128, -1, 2, 2, 64)[None]
    return (y.astype(np.float32), ys.astype(np.float32), pa[0], pa[1], pa[2], pb, sa[0], sa[1], sa[2], sbo)
```
